# Optimizing a Trainium2 kernel written in Bass

```python
import math
import jax
import jax.numpy as jnp
from jax import lax
import numpy as np

D_MODEL = 1024
BATCH = 32
SEQ = 256
DEPTH = 2
DEC_BATCH = 4
DEC_SEQ = 2048
PAST_LEN = 256

GRID_W = 64
MIX_W = D_MODEL // 2
N_BRANCH = 3
A_HEADS = 4
A_HEAD_DIM = MIX_W // (2 * A_HEADS)
ROPE_THETA = 10000.0
Q_BLOCK = 128
B_HEADS = 4
B_VDIM = MIX_W // B_HEADS
B_KDIM = B_VDIM // 2
GLA_RANK = 16
GLA_TAU = 16.0
GLA_CHUNK = 32
S5_W = MIX_W
S5_GROUP = 16
S5_G = S5_W // S5_GROUP
S5_P = 64
FFN_DIM = -(-8 * D_MODEL // (3 * 256)) * 256
EPS = 1e-6
IN_WIDTHS = (MIX_W, MIX_W, MIX_W, B_HEADS * B_KDIM, B_HEADS * B_KDIM, MIX_W, MIX_W, 2 * GLA_RANK, S5_W, N_BRANCH * D_MODEL)
IN_DIM = sum(IN_WIDTHS)

kernel_name = "hybrid_diffattn_gla_s5_dit_step"


def rms_norm(x, g):
    xf = x.astype(jnp.float32)
    return xf * lax.rsqrt(jnp.mean(xf * xf, axis=-1, keepdims=True) + EPS) * g


def split_cols(z):
    idx = [int(i) for i in np.cumsum(IN_WIDTHS)[:-1]]
    return jnp.split(z, idx, axis=-1)


def axial_rope_tables(n_tok):
    rows = n_tok // GRID_W
    t = jnp.arange(rows * GRID_W)
    row = (t // GRID_W).astype(jnp.float32)
    col = (t % GRID_W).astype(jnp.float32)
    half = A_HEAD_DIM // 2
    inv = ROPE_THETA ** (-jnp.arange(0, half, 2, dtype=jnp.float32) / half)
    ang = jnp.concatenate([row[:, None] * inv, col[:, None] * inv], axis=-1)
    return jnp.cos(ang), jnp.sin(ang)


def apply_axial_rope(x, cos, sin):
    half = A_HEAD_DIM // 2
    q4 = half // 2
    c = cos[None, :, None, None, :]
    s = sin[None, :, None, None, :]

    def rot(xp, cp, sp):
        x1, x2 = xp[..., :q4], xp[..., q4:]
        return jnp.concatenate([x1 * cp - x2 * sp, x2 * cp + x1 * sp], axis=-1)

    return jnp.concatenate([rot(x[..., :half], c[..., :q4], s[..., :q4]),
                            rot(x[..., half:], c[..., q4:], s[..., q4:])], axis=-1)


def diff_attention(q, k, v, lam):
    bsz, lq, nh, _, d = q.shape
    nb = lq // Q_BLOCK
    qb = q.astype(jnp.float32).reshape(bsz, nb, Q_BLOCK, nh, 2, d).swapaxes(0, 1)
    kf = k.astype(jnp.float32)
    vf = v.astype(jnp.float32)
    scale = d ** -0.5

    def block(qblk):
        s = jnp.einsum('bqhmd,bkhmd->bhmqk', qblk, kf) * scale
        p = jax.nn.softmax(s, axis=-1)
        w = p[:, :, 0] - lam * p[:, :, 1]
        return jnp.einsum('bhqk,bkhe->bqhe', w, vf)

    o = lax.map(block, qb)
    return o.swapaxes(0, 1).reshape(bsz, lq, nh, 2 * d)


def gla_chunked(q, k, v, log_a, s0):
    bsz, n_tok, nh, dk = q.shape
    dv = v.shape[-1]
    n = n_tok // GLA_CHUNK
    f32 = jnp.float32
    q = q.astype(f32).reshape(bsz, n, GLA_CHUNK, nh, dk)
    k = k.astype(f32).reshape(bsz, n, GLA_CHUNK, nh, dk)
    v = v.astype(f32).reshape(bsz, n, GLA_CHUNK, nh, dv)
    bc = jnp.cumsum(log_a.astype(f32).reshape(bsz, n, GLA_CHUNK, nh, dk), axis=2)
    rel = bc[:, :, :, None] - bc[:, :, None, :]
    causal = jnp.tril(jnp.ones((GLA_CHUNK, GLA_CHUNK), dtype=bool))[None, None, :, :, None, None]
    decay = jnp.where(causal, jnp.exp(jnp.minimum(rel, 0.0)), 0.0)
    attn = jnp.einsum('bnthk,bnshk,bntshk->bnhts', q, k, decay)
    intra = jnp.einsum('bnhts,bnshv->bnthv', attn, v)
    b_last = bc[:, :, -1]
    kdec = k * jnp.exp(b_last[:, :, None] - bc)
    ds = jnp.einsum('bnchk,bnchv->nbhkv', kdec, v)

    def step(state, inp):
        dec, d_state = inp
        return dec[..., None] * state + d_state, state

    s_final, s_start = lax.scan(step, s0.astype(f32), (jnp.exp(b_last).swapaxes(0, 1), ds))
    inter = jnp.einsum('bnchk,nbhkv->bnchv', q * jnp.exp(bc), s_start)
    return (intra + inter).reshape(bsz, n_tok, nh, dv), s_final


def complex_affine_combine(e1, e2):
    a1r, a1i, b1r, b1i = e1
    a2r, a2i, b2r, b2i = e2
    return (a2r * a1r - a2i * a1i, a2r * a1i + a2i * a1r,
            a2r * b1r - a2i * b1i + b2r, a2r * b1i + a2i * b1r + b2i)


def s5_scan(u, lam_re, lam_im, log_dt, b_re, b_im, c_re, c_im, h0, reverse):
    f32 = jnp.float32
    dt = jnp.exp(log_dt.astype(f32))[:, None]
    lr = lam_re.astype(f32)
    li = lam_im.astype(f32)
    mag = jnp.exp(lr * dt)
    ar = mag * jnp.cos(li * dt)
    ai = mag * jnp.sin(li * dt)
    den = lr * lr + li * li
    fr = ((ar - 1.0) * lr + ai * li) / den
    fi = (ai * lr - (ar - 1.0) * li) / den
    bbr = fr[..., None] * b_re - fi[..., None] * b_im
    bbi = fr[..., None] * b_im + fi[..., None] * b_re
    bu_r = jnp.einsum('gpi,blgi->blgp', bbr, u)
    bu_i = jnp.einsum('gpi,blgi->blgp', bbi, u)
    a_r = jnp.broadcast_to(ar, bu_r.shape)
    a_i = jnp.broadcast_to(ai, bu_r.shape)
    pr, pim, hr, hi = lax.associative_scan(complex_affine_combine, (a_r, a_i, bu_r, bu_i), axis=1, reverse=reverse)
    h0 = h0.astype(f32)
    h0r = h0[:, 0][:, None]
    h0i = h0[:, 1][:, None]
    hr = pr * h0r - pim * h0i + hr
    hi = pr * h0i + pim * h0r + hi
    y = jnp.einsum('gip,blgp->blgi', c_re, hr) - jnp.einsum('gip,blgp->blgi', c_im, hi)
    idx = 0 if reverse else -1
    return y, jnp.stack([hr[:, idx], hi[:, idx]], axis=1)


def trunk_layer(x, cond, p, lam_init, ctx, rope):
    bsz, n_tok, _ = x.shape
    dtype = x.dtype
    f32 = jnp.float32
    mod = jax.nn.silu(cond.astype(f32)) @ p['w_mod'] + p['b_mod']
    sh1, sc1, g1, sh2, sc2, g2 = jnp.split(mod[:, None, :], 6, axis=-1)
    h = rms_norm(x, p['norm1_g']) * (1.0 + sc1) + sh1
    aq, ak, av, bq, bk, bv, bg, br, cu, gz = split_cols(h @ p['w_in'])

    aq = rms_norm(aq.reshape(bsz, n_tok, A_HEADS, 2, A_HEAD_DIM), p['diff_qn_g'])
    ak = rms_norm(ak.reshape(bsz, n_tok, A_HEADS, 2, A_HEAD_DIM), p['diff_kn_g'])
    av = av.reshape(bsz, n_tok, A_HEADS, 2 * A_HEAD_DIM)
    lv = p['diff_lam']
    lam = jnp.exp(jnp.sum(lv[0] * lv[1])) - jnp.exp(jnp.sum(lv[2] * lv[3])) + lam_init
    if ctx is None:
        keys, vals = ak, av
    else:
        cos, sin = rope
        aq = apply_axial_rope(aq, cos, sin)
        keys = jnp.concatenate([ctx['k'], apply_axial_rope(ak, cos, sin)], axis=1)
        vals = jnp.concatenate([ctx['v'], av], axis=1)
    oa = diff_attention(aq, keys, vals, lam)
    oa = (rms_norm(oa, p['diff_subln_g']) * (1.0 - lam_init)).reshape(bsz, n_tok, MIX_W)

    bq = bq.reshape(bsz, n_tok, B_HEADS, B_KDIM) * (B_KDIM ** -0.5)
    bk = bk.reshape(bsz, n_tok, B_HEADS, B_KDIM)
    bv = bv.reshape(bsz, n_tok, B_HEADS, B_VDIM)
    r_f, r_b = jnp.split(br, 2, axis=-1)
    la_f = (jax.nn.log_sigmoid(r_f @ p['gla_wa2'][0] + p['gla_ba'][0]) / GLA_TAU).reshape(bsz, n_tok, B_HEADS, B_KDIM)
    la_b = (jax.nn.log_sigmoid(r_b @ p['gla_wa2'][1] + p['gla_ba'][1]) / GLA_TAU).reshape(bsz, n_tok, B_HEADS, B_KDIM)
    s0 = jnp.zeros((bsz, 2, B_HEADS, B_KDIM, B_VDIM), f32) if ctx is None else ctx['gla']
    flip = lambda t: jnp.flip(t, axis=1)
    o_f, s_f = gla_chunked(bq, bk, bv, la_f, s0[:, 0])
    o_b, s_b = gla_chunked(flip(bq), flip(bk), flip(bv), flip(la_b), s0[:, 1])
    ob = rms_norm(o_f + flip(o_b), p['gla_on_g']) * jax.nn.silu(bg.reshape(bsz, n_tok, B_HEADS, B_VDIM))
    ob = ob.reshape(bsz, n_tok, MIX_W)

    u = cu.astype(f32).reshape(bsz, n_tok, S5_G, S5_GROUP)
    h0 = jnp.zeros((bsz, 2, 2, S5_G, S5_P), f32) if ctx is None else ctx['s5']
    y_f, hf = s5_scan(u, p['s5_lam_re'][0], p['s5_lam_im'][0], p['s5_log_dt'][0], p['s5_b_re'][0], p['s5_b_im'][0],
                      p['s5_c_re'][0], p['s5_c_im'][0], h0[:, 0], False)
    y_b, hb = s5_scan(u, p['s5_lam_re'][1], p['s5_lam_im'][1], p['s5_log_dt'][1], p['s5_b_re'][1], p['s5_b_im'][1],
                      p['s5_c_re'][1], p['s5_c_im'][1], h0[:, 1], True)
    yc = jax.nn.gelu((y_f + y_b).reshape(bsz, n_tok, S5_W) + p['s5_d'] * cu)
    glu_a, glu_b = jnp.split(yc @ p['s5_w_glu'] + p['s5_b_glu'], 2, axis=-1)
    oc = glu_a * jax.nn.sigmoid(glu_b)

    branches = jnp.einsum('blrm,rmd->blrd', jnp.stack([oa, ob, oc], axis=2), p['w_branch'])
    gates = jax.nn.sigmoid(gz.reshape(bsz, n_tok, N_BRANCH, D_MODEL))
    merged = jnp.sum(gates * branches, axis=2)
    x = x + g1 * (merged @ p['w_out'])

    h = rms_norm(x, p['norm2_g']) * (1.0 + sc2) + sh2
    x = x + g2 * ((jax.nn.silu(h @ p['w_ffn_gate']) * (h @ p['w_ffn_up'])) @ p['w_ffn_down'])
    x = x.astype(dtype)
    if ctx is None:
        return x, (ak, av, jnp.stack([s_f, s_b], axis=1), jnp.stack([hf, hb], axis=1))
    return x, None


def setup_inputs(seed: int = 0) -> dict:
    key = jax.random.key(seed)
    keys = jax.random.split(key, 48)
    counter = iter(range(48))
    f32 = jnp.float32

    def nrm(shape, scale):
        return jax.random.normal(keys[next(counter)], shape, f32) * scale

    def gain(shape):
        return 1.0 + nrm(shape, 0.02)

    return {
        'x_prompt': nrm((BATCH, SEQ, D_MODEL), 1.0),
        'x_sample': nrm((DEC_BATCH, DEC_SEQ, D_MODEL), 1.0),
        'cache_diff_k': nrm((DEC_BATCH, DEPTH, PAST_LEN, A_HEADS, 2, A_HEAD_DIM), 1.0),
        'cache_diff_v': nrm((DEC_BATCH, DEPTH, PAST_LEN, A_HEADS, 2 * A_HEAD_DIM), 1.0),
        'state_gla': nrm((DEC_BATCH, DEPTH, 2, B_HEADS, B_KDIM, B_VDIM), 1.0),
        'state_s5': nrm((DEC_BATCH, DEPTH, 2, 2, S5_G, S5_P), 0.1),
        'c': nrm((DEC_BATCH, D_MODEL), 1.0),
        'c_ctx': nrm((D_MODEL,), 1.0),
        'w_mod': nrm((DEPTH, D_MODEL, 6 * D_MODEL), D_MODEL ** -0.5),
        'b_mod': nrm((DEPTH, 6 * D_MODEL), 0.02),
        'norm1_g': gain((DEPTH, D_MODEL)),
        'norm2_g': gain((DEPTH, D_MODEL)),
        'w_in': nrm((DEPTH, D_MODEL, IN_DIM), D_MODEL ** -0.5),
        'diff_qn_g': gain((DEPTH, A_HEAD_DIM)),
        'diff_kn_g': gain((DEPTH, A_HEAD_DIM)),
        'diff_lam': nrm((DEPTH, 4, A_HEAD_DIM), 0.1),
        'diff_subln_g': gain((DEPTH, 2 * A_HEAD_DIM)),
        'gla_wa2': nrm((DEPTH, 2, GLA_RANK, B_HEADS * B_KDIM), GLA_RANK ** -0.5),
        'gla_ba': nrm((DEPTH, 2, B_HEADS * B_KDIM), 0.1),
        'gla_on_g': gain((DEPTH, B_VDIM)),
        's5_lam_re': -0.5 + nrm((DEPTH, 2, S5_G, S5_P), 0.01),
        's5_lam_im': math.pi * jnp.arange(S5_P, dtype=f32) + nrm((DEPTH, 2, S5_G, S5_P), 0.01),
        's5_log_dt': jax.random.uniform(keys[next(counter)], (DEPTH, 2, S5_G), f32, minval=math.log(1e-3), maxval=math.log(1e-1)),
        's5_b_re': nrm((DEPTH, 2, S5_G, S5_P, S5_GROUP), (2 * S5_GROUP) ** -0.5),
        's5_b_im': nrm((DEPTH, 2, S5_G, S5_P, S5_GROUP), (2 * S5_GROUP) ** -0.5),
        's5_c_re': nrm((DEPTH, 2, S5_G, S5_GROUP, S5_P), S5_P ** -0.5),
        's5_c_im': nrm((DEPTH, 2, S5_G, S5_GROUP, S5_P), S5_P ** -0.5),
        's5_d': nrm((DEPTH, S5_W), 1.0),
        's5_w_glu': nrm((DEPTH, S5_W, 2 * S5_W), S5_W ** -0.5),
        's5_b_glu': nrm((DEPTH, 2 * S5_W), 0.02),
        'w_branch': nrm((DEPTH, N_BRANCH, MIX_W, D_MODEL), MIX_W ** -0.5),
        'w_out': nrm((DEPTH, D_MODEL, D_MODEL), D_MODEL ** -0.5),
        'w_ffn_gate': nrm((DEPTH, D_MODEL, FFN_DIM), D_MODEL ** -0.5),
        'w_ffn_up': nrm((DEPTH, D_MODEL, FFN_DIM), D_MODEL ** -0.5),
        'w_ffn_down': nrm((DEPTH, FFN_DIM, D_MODEL), FFN_DIM ** -0.5),
    }


def reference(x_prompt, x_sample, cache_diff_k, cache_diff_v, state_gla, state_s5, c, c_ctx,
              w_mod, b_mod, norm1_g, norm2_g, w_in, diff_qn_g, diff_kn_g, diff_lam, diff_subln_g,
              gla_wa2, gla_ba, gla_on_g, s5_lam_re, s5_lam_im, s5_log_dt, s5_b_re, s5_b_im,
              s5_c_re, s5_c_im, s5_d, s5_w_glu, s5_b_glu, w_branch, w_out, w_ffn_gate, w_ffn_up, w_ffn_down):
    weights = dict(w_mod=w_mod, b_mod=b_mod, norm1_g=norm1_g, norm2_g=norm2_g, w_in=w_in,
                   diff_qn_g=diff_qn_g, diff_kn_g=diff_kn_g, diff_lam=diff_lam, diff_subln_g=diff_subln_g,
                   gla_wa2=gla_wa2, gla_ba=gla_ba, gla_on_g=gla_on_g,
                   s5_lam_re=s5_lam_re, s5_lam_im=s5_lam_im, s5_log_dt=s5_log_dt,
                   s5_b_re=s5_b_re, s5_b_im=s5_b_im, s5_c_re=s5_c_re, s5_c_im=s5_c_im,
                   s5_d=s5_d, s5_w_glu=s5_w_glu, s5_b_glu=s5_b_glu,
                   w_branch=w_branch, w_out=w_out,
                   w_ffn_gate=w_ffn_gate, w_ffn_up=w_ffn_up, w_ffn_down=w_ffn_down)
    cond_ctx = jnp.broadcast_to(c_ctx, (x_prompt.shape[0], c_ctx.shape[-1]))
    rope = axial_rope_tables(x_sample.shape[1])
    y_prompt, y_sample = x_prompt, x_sample
    k_list, v_list, gla_list, s5_list = [], [], [], []
    for l in range(DEPTH):
        p = {name: w[l] for name, w in weights.items()}
        lam_init = 0.8 - 0.6 * math.exp(-0.3 * l)
        y_prompt, (k_l, v_l, g_l, s_l) = trunk_layer(y_prompt, cond_ctx, p, lam_init, None, None)
        k_list.append(k_l)
        v_list.append(v_l)
        gla_list.append(g_l)
        s5_list.append(s_l)
        ctx = dict(k=cache_diff_k[:, l], v=cache_diff_v[:, l], gla=state_gla[:, l], s5=state_s5[:, l])
        y_sample, _ = trunk_layer(y_sample, c, p, lam_init, ctx, rope)
    new_diff_k = jnp.stack(k_list, axis=1)
    new_diff_v = jnp.stack(v_list, axis=1)
    new_gla_state = jnp.stack(gla_list, axis=1)
    new_s5_state = jnp.stack(s5_list, axis=1)
    return (y_prompt, y_sample, new_diff_k, new_diff_v, new_gla_state, new_s5_state)
```

```python
import math
from contextlib import ExitStack

import numpy as np
import concourse.bass as bass
import concourse.mybir as mybir
from concourse.bass_utils import run_bass_kernel_spmd

F32 = mybir.dt.float32
BF16 = mybir.dt.bfloat16
I32 = mybir.dt.int32
ALU = mybir.AluOpType
AF = mybir.ActivationFunctionType

D = 1024
T = 2048
NT = 4
NB = 16
DEPTH = 2
IN_DIM = 6688
FFN = 2816
NJ = FFN // 128
C_AQ, C_AK, C_AV, C_BQ, C_BK, C_BV, C_BG, C_BR, C_CU, C_GZ = 0, 512, 1024, 1536, 1792, 2048, 2560, 3072, 3104, 3616
EPS = 1e-6
TWO_PI_LO = 6.28318
NEG = -30000.0

W_SPECS = [
    ("w_mod", (2, 1024, 6144)), ("b_mod", (2, 6144)), ("norm1_g", (2, 1024)), ("norm2_g", (2, 1024)),
    ("w_in", (2, 1024, 6688)), ("diff_qn_g", (2, 64)), ("diff_kn_g", (2, 64)), ("diff_lam", (2, 4, 64)),
    ("diff_subln_g", (2, 128)), ("gla_wa2", (2, 2, 16, 256)), ("gla_ba", (2, 2, 256)), ("gla_on_g", (2, 128)),
    ("s5_lam_re", (2, 2, 32, 64)), ("s5_lam_im", (2, 2, 32, 64)), ("s5_log_dt", (2, 2, 32)),
    ("s5_b_re", (2, 2, 32, 64, 16)), ("s5_b_im", (2, 2, 32, 64, 16)), ("s5_c_re", (2, 2, 32, 16, 64)),
    ("s5_c_im", (2, 2, 32, 16, 64)), ("s5_d", (2, 512)), ("s5_w_glu", (2, 512, 1024)), ("s5_b_glu", (2, 1024)),
    ("w_branch", (2, 3, 512, 1024)), ("w_out", (2, 1024, 1024)), ("w_ffn_gate", (2, 1024, 2816)),
    ("w_ffn_up", (2, 1024, 2816)), ("w_ffn_down", (2, 2816, 1024)),
]
IN_SPECS = [
    ("xin", (2048, 1024)), ("cond", (1024,)), ("ctxk", (2, 256, 512)), ("ctxv", (2, 256, 512)),
    ("gla0", (2, 2, 4, 64, 128)), ("s5h0", (2, 2, 2, 32, 64)), ("flags", (128, 2)), ("mq", (8, 2048)), ("mk", (8, 2304)),
]
OUT_SPECS = [
    ("y", (2048, 1024)), ("newk", (2, 2048, 512)), ("newv", (2, 2048, 512)),
    ("newgla", (2, 8, 2, 4, 64, 128)), ("news5", (2, 8, 2, 2, 32, 64)),
]


S5STOP = [None]


class _Stop(Exception):
    pass


def chk(tag):
    if S5STOP[0] == tag:
        raise _Stop()


class Trk:
    __slots__ = ("w", "r", "dsem", "x")

    def __init__(self):
        self.w = None
        self.r = {}
        self.dsem = None
        self.x = False


class KB:
    def _emit_wait(self, e, k, v):
        if k in self.dtot:
            self.engs[e].wait_ge(self.sem[k], v)
            return
        self.waited.add((k, v))
        if self.needed is None:
            self.engs[e].wait_ge(self.sem[k], v)
        else:
            self.engs[e].wait_ge(self.sem[k], self.vmap[k][v])

    def __init__(self, nc, es, needed=None):
        self.nc = nc
        self.es = es
        self.needed = needed
        self.waited = set()
        self.incs = {}
        self.vmap = {}
        self.engs = {"pe": nc.tensor, "act": nc.scalar, "dve": nc.vector, "pool": nc.gpsimd, "sp": nc.sync}
        self.sem = {}
        self.cnt = {}
        self.seen = {}
        for e in self.engs:
            self.sem[e] = es.enter_context(nc.semaphore("s_" + e))
            self.cnt[e] = 0
            self.seen[e] = {}
            self.incs[e] = 0
            self.vmap[e] = {}
        self.dtot = {}
        self.dfree = []
        self.dfree_sw = []
        self.swkeys = set()
        self.dassigned = []
        self.nds = 0
        self.uid = 0

    def name(self, p):
        self.uid += 1
        return "%s%d" % (p, self.uid)

    def _wait(self, e, r, w):
        need = {}
        for t in r:
            if t.w is not None:
                k, v = t.w
                if need.get(k, 0) < v:
                    need[k] = v
            if t.x:
                for k, v in t.r.items():
                    if k != e and need.get(k, 0) < v:
                        need[k] = v
        for t in w:
            if t.w is not None:
                k, v = t.w
                if need.get(k, 0) < v:
                    need[k] = v
            for k, v in t.r.items():
                if need.get(k, 0) < v:
                    need[k] = v
        seen = self.seen[e]
        for k, v in need.items():
            if k in self.dtot:
                v = self.dtot[k]
            if seen.get(k, 0) < v:
                self._emit_wait(e, k, v)
                seen[k] = v

    def _commit(self, ev, r, w):
        k, v = ev
        for t in r:
            if t.r.get(k, 0) < v:
                t.r[k] = v
        for t in w:
            t.w = ev
            t.r = {}

    def op(self, e, fn, r=(), w=()):
        self._wait(e, r, w)
        ins = fn(self.engs[e])
        self.cnt[e] += 1
        if self.needed is None or (e, self.cnt[e]) in self.needed:
            self.incs[e] += 1
            self.vmap[e][self.cnt[e]] = self.incs[e]
            ins.then_inc(self.sem[e], 1)
        self._commit((e, self.cnt[e]), r, w)

    def dma(self, q, out, in_, trk, r=(), w=(), **kw):
        self._wait(q, r, w)
        if trk.dsem is None:
            pool_ = self.dfree_sw if q == "pool" else self.dfree
            if pool_:
                key = pool_.pop()
            else:
                self.nds += 1
                key = "d%d" % self.nds
                self.sem[key] = self.es.enter_context(self.nc.semaphore("s_" + key))
                self.dtot[key] = 0
                if q == "pool":
                    self.swkeys.add(key)
            trk.dsem = key
            self.dassigned.append(trk)
        key = trk.dsem
        ins = self.engs[q].dma_start(out=out, in_=in_, **kw)
        ins.then_inc(self.sem[key], 16)
        self.dtot[key] += 16
        self._commit((key, self.dtot[key]), r, w)

    def barrier(self):
        for e in self.engs:
            for o in self.engs:
                if o != e and self.seen[e].get(o, 0) < self.cnt[o]:
                    self._emit_wait(e, o, self.cnt[o])
                    self.seen[e][o] = self.cnt[o]
            for k, v in self.dtot.items():
                if self.seen[e].get(k, 0) < v:
                    self.engs[e].wait_ge(self.sem[k], v)
                    self.seen[e][k] = v
        for t in self.dassigned:
            if t.dsem is not None:
                (self.dfree_sw if t.dsem in self.swkeys else self.dfree).append(t.dsem)
                t.dsem = None
        self.dassigned = []

    def final_wait(self):
        for k, v in self.dtot.items():
            if self.seen["sp"].get(k, 0) < v:
                self.nc.sync.wait_ge(self.sem[k], v)
        for o in self.engs:
            if o != "sp" and self.cnt[o] > 0:
                self._emit_wait("sp", o, self.cnt[o])


class Tile:
    def __init__(self, h):
        self.h = h
        self.t = {}

    def __getitem__(self, k):
        return self.h[k]

    def trk(self, key=0):
        t = self.t.get(key)
        if t is None:
            t = self.t[key] = Trk()
        return t


class SubView:
    def __init__(self, tile, c0):
        self.tile = tile
        self.c0 = c0

    def __getitem__(self, k):
        r, c = k
        return self.tile[r, slice(self.c0 + c.start, self.c0 + c.stop)]

    def trk(self):
        return self.tile.trk()


def build_program(stop_after=None, dbg_names=(), needed=None, want_waited=False):
    nc = bass.Bass("TRN2", target_bir_lowering=False)
    es = ExitStack()
    kb = KB(nc, es, needed)
    dr = {}
    for n, s in IN_SPECS + W_SPECS:
        dr[n] = nc.dram_tensor(n, list(s), F32, kind="ExternalInput").ap()
    for n, s in OUT_SPECS:
        dr[n] = nc.dram_tensor(n, list(s), F32, kind="ExternalOutput").ap()
    dbg_out = {}

    def sb(stack, name, shape, dt=F32):
        return Tile(stack.enter_context(nc.sbuf_tensor(kb.name(name), list(shape), dt)))

    PS = [Tile(es.enter_context(nc.psum_tensor("ps%d" % i, [128, 512], F32))) for i in range(8)]
    for t_ in PS:
        t_.trk().x = True

    def dump(name, tile_ap, shape, trk, dt=F32):
        if name not in dbg_names:
            return
        o = nc.dram_tensor("dbg_" + name, list(shape), dt, kind="ExternalOutput").ap()
        dbg_out[name] = shape
        kb.dma("sp", o, tile_ap, trk, r=[trk])
        kb.barrier()

    dump.names = dbg_names

    xT = sb(es, "xT", [128, 8, T])
    hT = sb(es, "hT", [128, 8, T], BF16)
    ident_f = sb(es, "identf", [128, 128])
    ones_f = sb(es, "onesf", [128, 128])
    ones_b = sb(es, "onesb", [128, 128], BF16)
    blk64 = sb(es, "blk64", [128, 128])
    m_le = sb(es, "mle", [128, 128])
    m_ge = sb(es, "mge", [128, 128])
    m_gt = sb(es, "mgt", [128, 128])
    m_lt = sb(es, "mlt", [128, 128])
    rrot = sb(es, "rrot", [128, 128])
    m_le_b = sb(es, "mleb", [128, 128], BF16)
    m_ge_b = sb(es, "mgeb", [128, 128], BF16)
    m_gt_b = sb(es, "mgtb", [128, 128], BF16)
    m_lt_b = sb(es, "mltb", [128, 128], BF16)
    blk64_b = sb(es, "blk64b", [128, 128], BF16)
    rrot_b = sb(es, "rrotb", [128, 128], BF16)
    mtz_f = sb(es, "mtzf", [128, 128])
    mtz_b = sb(es, "mtzb", [128, 128])
    oT = sb(es, "oT", [128, 4, T], BF16)
    OT = [[oT.trk((c, n)) for n in range(NT)] for c in range(4)]
    cst = sb(es, "cst", [128, 8])
    flg = sb(es, "flg", [128, 2])
    par = sb(es, "par", [128, DEPTH, 160])
    CONST = Trk()

    V, P_, A_ = "dve", "pool", "act"

    kb.op(P_, lambda e: e.memset(ident_f[:], 0.0), w=[CONST])
    kb.op(P_, lambda e: e.affine_select(out=ident_f[:], in_=ident_f[:], pattern=[[-1, 128]], compare_op=ALU.not_equal,
                                        fill=1.0, base=0, channel_multiplier=1), w=[CONST])
    kb.op(P_, lambda e: e.memset(ones_f[:], 1.0), w=[CONST])
    kb.op(V, lambda e: e.tensor_copy(out=ones_b[:], in_=ones_f[:]), r=[CONST], w=[CONST])
    for (mt, cmp, sgn) in ((m_le, ALU.is_ge, -1), (m_ge, ALU.is_ge, 1), (m_gt, ALU.is_gt, 1), (m_lt, ALU.is_gt, -1)):
        kb.op(P_, lambda e, mt=mt, cmp=cmp, sgn=sgn: e.affine_select(out=mt[:], in_=ones_f[:], pattern=[[-sgn, 128]], compare_op=cmp,
                                                                     fill=0.0, base=0, channel_multiplier=sgn), r=[CONST], w=[CONST])
    kb.op(P_, lambda e: e.memset(blk64[:], 0.0), w=[CONST])
    kb.op(P_, lambda e: e.memset(blk64[0:64, 0:64], 1.0), w=[CONST])
    kb.op(P_, lambda e: e.memset(blk64[64:128, 64:128], 1.0), w=[CONST])
    kb.op(P_, lambda e: e.memset(mtz_f[:], 0.0), w=[CONST])
    kb.op(P_, lambda e: e.memset(mtz_b[:], 0.0), w=[CONST])
    for s in range(4):
        kb.op(P_, lambda e, s=s: e.memset(mtz_f[32 * s:32 * s + 32, 32 * s:128], 1.0), w=[CONST])
        kb.op(P_, lambda e, s=s: e.memset(mtz_b[32 * s:32 * s + 32, 0:32 * s + 32], 1.0), w=[CONST])
    rv = rrot[:].rearrange("p (b h i) -> p b h i", h=2, i=16)
    iv = ident_f[:].rearrange("p (b h i) -> p b h i", h=2, i=16)
    kb.op(V, lambda e: e.tensor_scalar(out=rv[:, :, 0, :], in0=iv[:, :, 1, :], scalar1=-1.0, scalar2=None, op0=ALU.mult),
          r=[CONST], w=[CONST])
    kb.op(V, lambda e: e.tensor_copy(out=rv[:, :, 1, :], in_=iv[:, :, 0, :]), r=[CONST], w=[CONST])
    for (src_, dst_) in ((m_le, m_le_b), (m_ge, m_ge_b), (m_gt, m_gt_b), (m_lt, m_lt_b), (blk64, blk64_b), (rrot, rrot_b)):
        kb.op(V, lambda e, src_=src_, dst_=dst_: e.tensor_copy(out=dst_[:], in_=src_[:]), r=[CONST], w=[CONST])
    kb.op(P_, lambda e: e.memset(cst[:, 0:1], EPS), w=[CONST])
    kb.op(P_, lambda e: e.memset(cst[:, 1:2], 1.0), w=[CONST])
    kb.op(P_, lambda e: e.memset(cst[:, 2:3], 0.25), w=[CONST])
    kb.op(P_, lambda e: e.memset(cst[:, 3:4], 0.0), w=[CONST])
    kb.dma("sp", flg[:], dr["flags"], CONST, w=[CONST])

    def sincos_tmps(stack, shape):
        return (sb(stack, "sc_i", shape, I32), sb(stack, "sc_f", shape), sb(stack, "sc_q", shape), Trk())

    def sincos(tmps, out_c, out_s, turns, r, w):
        ti, tf, tq, tl = tmps
        kb.op(V, lambda e: e.tensor_copy(out=ti, in_=turns), r=r, w=tl)
        kb.op(V, lambda e: e.tensor_tensor(out=tf, in0=turns, in1=ti, op=ALU.subtract), r=list(r), w=tl)
        kb.op(A_, lambda e: e.activation(out=out_s, in_=tf, func=AF.Sin, scale=TWO_PI_LO), r=tl, w=w)
        kb.op(V, lambda e: e.tensor_scalar(out=tq, in0=turns, scalar1=0.25, scalar2=None, op0=ALU.add), r=r, w=tl)
        kb.op(V, lambda e: e.tensor_copy(out=ti, in_=tq), r=[], w=tl)
        kb.op(V, lambda e: e.tensor_tensor(out=tf, in0=tq, in1=ti, op=ALU.subtract), r=[], w=tl)
        kb.op(A_, lambda e: e.activation(out=out_c, in_=tf, func=AF.Sin, scale=TWO_PI_LO), r=tl, w=w)

    def make_rope(stack):
        ropec = sb(stack, "ropec", [128, T], BF16)
        ropes = sb(stack, "ropes", [128, T], BF16)
        RT = Trk()
        HF = T // 2
        with ExitStack() as st:
            pos_i = sb(st, "posi", [128, HF], I32)
            pos_f = sb(st, "posf", [128, HF])
            pid = sb(st, "pid", [128, 1], I32)
            pidf = sb(st, "pidf", [128, 1])
            invc = sb(st, "invc", [128, 1])
            tc_ = sb(st, "rc", [128, HF])
            ts_ = sb(st, "rs", [128, HF])
            tm = sincos_tmps(st, [128, HF])
            t1 = Trk()
            kb.op(P_, lambda e: e.iota(pid[:], pattern=[[0, 1]], base=0, channel_multiplier=1), w=[t1])
            kb.op(V, lambda e: e.tensor_single_scalar(out=pid[:], in_=pid[:], scalar=15, op=ALU.bitwise_and), r=[t1], w=[t1])
            kb.op(V, lambda e: e.tensor_copy(out=pidf[:], in_=pid[:]), r=[t1], w=[t1])
            kb.op(A_, lambda e: e.activation(out=invc[:], in_=pidf[:], func=AF.Exp, scale=-math.log(10000.0) / 16.0), r=[t1], w=[t1])
            kb.op(V, lambda e: e.tensor_scalar(out=invc[:], in0=invc[:], scalar1=flg[:, 1:2], scalar2=None, op0=ALU.mult),
                  r=[t1, CONST], w=[t1])
            kb.op(V, lambda e: e.tensor_scalar(out=invc[:], in0=invc[:], scalar1=1.0 / (2 * math.pi), scalar2=None, op0=ALU.mult), r=[t1], w=[t1])
            for hf in range(2):
                for b in range(4):
                    pat, base = ([[1, 16], [0, 64]], hf * 16) if b % 2 == 0 else ([[0, 16], [1, 64]], 0)
                    kb.op(P_, lambda e, b=b, pat=pat, base=base: e.iota(pos_i[32 * b:32 * b + 32, :].rearrange("p (r c) -> p r c", c=64), pattern=pat,
                                                                        base=base, channel_multiplier=0), w=[t1])
                kb.op(V, lambda e: e.tensor_copy(out=pos_f[:], in_=pos_i[:]), r=[t1], w=[t1])
                kb.op(V, lambda e: e.tensor_scalar(out=pos_f[:], in0=pos_f[:], scalar1=invc[:, 0:1], scalar2=None, op0=ALU.mult), r=[t1], w=[t1])
                t2 = Trk()
                sincos((tm[0][:], tm[1][:], tm[2][:], [tm[3]]), tc_[:], ts_[:], pos_f[:], [t1], [t2])
                kb.op(V, lambda e, hf=hf: e.tensor_copy(out=ropec[:, hf * HF:(hf + 1) * HF], in_=tc_[:]), r=[t2], w=[RT])
                kb.op(V, lambda e, hf=hf: e.tensor_copy(out=ropes[:, hf * HF:(hf + 1) * HF], in_=ts_[:]), r=[t2], w=[RT])
                kb.op(V, lambda e: e.tensor_copy(out=pidf[:], in_=pidf[:]), r=[t2, RT], w=[t1])
            kb.barrier()
        return ropec, ropes, RT

    XT = [[xT.trk((c, n)) for n in range(NT)] for c in range(8)]
    HT = [[hT.trk((c, n)) for n in range(NT)] for c in range(8)]
    with ExitStack() as st:
        xs = [sb(st, "xs", [128, 1024]) for _ in range(4)]
        for b in range(NB):
            s_ = xs[b % 4]
            kb.dma("sp", s_[:], dr["xin"][b * 128:(b + 1) * 128, :], s_.trk(), w=[s_.trk()])
            for half in range(2):
                ps = PS[(2 * b + half) % 4]
                kb.op("pe", lambda e, ps=ps, s_=s_, half=half: [e.transpose(out=ps[:, j * 128:(j + 1) * 128], in_=s_[:, (half * 4 + j) * 128:(half * 4 + j + 1) * 128],
                                                                             identity=ident_f[:]) for j in range(4)][-1],
                      r=[s_.trk(), CONST], w=[ps.trk()])
                eng = A_ if half == 0 else V
                outv = xT[:, half * 4:half * 4 + 4, b * 128:(b + 1) * 128]
                inv_ = ps[:].rearrange("p (j t) -> p j t", t=128)
                wl = [XT[half * 4 + j][b // 4] for j in range(4)]
                if eng == A_:
                    kb.op(A_, lambda e, outv=outv, inv_=inv_: e.activation(out=outv, in_=inv_, func=AF.Copy), r=[ps.trk()], w=wl)
                else:
                    kb.op(V, lambda e, outv=outv, inv_=inv_: e.tensor_copy(out=outv, in_=inv_), r=[ps.trk()], w=wl)
        kb.barrier()

    PAR = Trk()
    with ExitStack() as st:
        condT = sb(st, "condT", [128, 8])
        scond = sb(st, "scond", [128, 8], BF16)
        bmT = sb(st, "bmT", [128, 48])
        wm = [sb(st, "wm", [128, 8, 512], BF16) for _ in range(4)]
        tcnd = Trk()
        kb.dma("sp", condT[:], dr["cond"].rearrange("(c p) -> p c", p=128), tcnd, w=[tcnd], allow_slow_non_contiguous=True)
        kb.op(A_, lambda e: e.activation(out=scond[:], in_=condT[:], func=AF.Silu), r=[tcnd], w=[tcnd])
        for l in range(DEPTH):
            tb = Trk()
            kb.dma("sp", bmT[:], dr["b_mod"][l].rearrange("(c p) -> p c", p=128), tb, w=[tb], allow_slow_non_contiguous=True)
            psm = PS[4 + l]
            wv = dr["w_mod"][l].rearrange("(kc p) n -> p kc n", p=128)
            for cb in range(12):
                wt = wm[cb % 4]
                kb.dma("pool", wt[:], wv[:, :, cb * 512:(cb + 1) * 512], wt.trk(), w=[wt.trk()])

                def mm(e, wt=wt, cb=cb, psm=psm):
                    ins = None
                    for j in range(4):
                        col = cb * 4 + j
                        for kc in range(8):
                            ins = e.matmul(psm[:, col:col + 1], lhsT=wt[:, kc, j * 128:(j + 1) * 128], rhs=scond[:, kc:kc + 1],
                                           start=(kc == 0), stop=(kc == 7))
                    return ins
                kb.op("pe", mm, r=[wt.trk(), tcnd], w=[psm.trk()])
            kb.op(V, lambda e, l=l, psm=psm: e.tensor_tensor(out=par[:, l, 0:48], in0=psm[:, 0:48], in1=bmT[:], op=ALU.add),
                  r=[psm.trk(), tb], w=[PAR])
            tv = Trk()
            kb.dma("sp", par[:, l, 64:72], dr["norm1_g"][l].rearrange("(c p) -> p c", p=128), tv, w=[PAR], allow_slow_non_contiguous=True)
            kb.dma("sp", par[:, l, 72:80], dr["norm2_g"][l].rearrange("(c p) -> p c", p=128), tv, w=[PAR], allow_slow_non_contiguous=True)
            for mth in range(2):
                kb.dma("sp", par[64 * mth:64 * mth + 64, l, 80:81], dr["diff_qn_g"][l].rearrange("(p o) -> p o", o=1), tv, w=[PAR], allow_slow_non_contiguous=True)
                kb.dma("sp", par[64 * mth:64 * mth + 64, l, 81:82], dr["diff_kn_g"][l].rearrange("(p o) -> p o", o=1), tv, w=[PAR], allow_slow_non_contiguous=True)
            kb.dma("sp", par[:, l, 82:83], dr["diff_subln_g"][l].rearrange("(p o) -> p o", o=1), tv, w=[PAR], allow_slow_non_contiguous=True)
            kb.dma("sp", par[:, l, 84:85], dr["gla_on_g"][l].rearrange("(p o) -> p o", o=1), tv, w=[PAR], allow_slow_non_contiguous=True)
            kb.dma("sp", par[:, l, 88:96], dr["s5_b_glu"][l].rearrange("(c p) -> p c", p=128), tv, w=[PAR], allow_slow_non_contiguous=True)
            for s in range(4):
                kb.dma("sp", par[32 * s:32 * s + 32, l, 96:112], dr["s5_d"][l].rearrange("(gp jj) -> jj gp", jj=32), tv, w=[PAR], allow_slow_non_contiguous=True)
            lamt = sb(st, "lamt", [128, 256])
            lamp = sb(st, "lamp", [128, 128])
            lams = sb(st, "lams", [128, 2])
            kb.dma("sp", lamt[:], dr["diff_lam"][l].rearrange("a b -> (a b)").rearrange("(o n) -> o n", o=1).partition_broadcast(128), tv, w=[tv])
            lam_init = 0.8 - 0.6 * math.exp(-0.3 * l)
            kb.op(V, lambda e: e.tensor_tensor(out=lamp[:].rearrange("p (a b) -> p a b", b=64), in0=lamt[:].rearrange("p (a t b) -> p a t b", t=2, b=64)[:, :, 0, :],
                                               in1=lamt[:].rearrange("p (a t b) -> p a t b", t=2, b=64)[:, :, 1, :], op=ALU.mult), r=[tv], w=[tv])
            kb.op(V, lambda e: e.reduce_sum(out=lams[:], in_=lamp[:].rearrange("p (a b) -> p a b", b=64), axis=mybir.AxisListType.X), r=[tv], w=[tv])
            kb.op(A_, lambda e: e.activation(out=lams[:], in_=lams[:], func=AF.Exp), r=[tv], w=[tv])
            kb.op(V, lambda e, l=l, lam_init=lam_init: e.scalar_tensor_tensor(out=par[:, l, 83:84], in0=lams[:, 1:2], scalar=-lam_init, in1=lams[:, 0:1],
                                                                             op0=ALU.add, op1=ALU.subtract), r=[tv], w=[PAR])
            kb.op(V, lambda e, l=l, lam_init=lam_init: e.tensor_scalar(out=par[:, l, 82:83], in0=par[:, l, 82:83], scalar1=1.0 - lam_init, scalar2=None, op0=ALU.mult),
                  r=[tv, PAR], w=[PAR])
            kb.op(V, lambda e, l=l: e.scalar_tensor_tensor(out=par[:, l, 48:56], in0=par[:, l, 8:16], scalar=1.0, in1=par[:, l, 64:72], op0=ALU.add, op1=ALU.mult),
                  r=[PAR, tv], w=[PAR])
            kb.op(V, lambda e, l=l: e.scalar_tensor_tensor(out=par[:, l, 56:64], in0=par[:, l, 32:40], scalar=1.0, in1=par[:, l, 72:80], op0=ALU.add, op1=ALU.mult),
                  r=[PAR, tv], w=[PAR])
        kb.barrier()
    dump("par", par[:].rearrange("p l c -> p (l c)"), [128, DEPTH * 160], PAR)

    def rstd_from_ps(ps, tmp, out, scale):
        kb.op(A_, lambda e: e.activation(out=tmp[0], in_=ps[0], func=AF.Ln, scale=scale, bias=cst[0:tmp[2], 0:1]), r=[ps[1], CONST], w=[tmp[1]])
        kb.op(A_, lambda e: e.activation(out=out[0], in_=tmp[0], func=AF.Exp, scale=-0.5), r=[tmp[1]], w=[out[1]])

    def norm(l, scol, shcol):
        with ExitStack() as st:
            sq = [sb(st, "nsq", [128, 512]) for _ in range(2)]
            lnv = sb(st, "nln", [128, 512])
            rstd = [sb(st, "nrs", [128, 512]) for _ in range(2)]
            tmp = [sb(st, "ntm", [128, 512]) for _ in range(2)]
            for n in range(NT):
                tsl = slice(n * 512, (n + 1) * 512)
                ps = PS[n % 2]
                for c in range(8):
                    q = sq[c % 2]
                    kb.op(A_, lambda e, q=q, c=c: e.activation(out=q[:], in_=xT[:, c, tsl], func=AF.Square), r=[XT[c][n]], w=[q.trk()])
                    kb.op("pe", lambda e, q=q, c=c, ps=ps: e.matmul(ps[:], lhsT=ones_f[:], rhs=q[:], start=(c == 0), stop=(c == 7)),
                          r=[q.trk(), CONST], w=[ps.trk()])
                rs = rstd[n % 2]
                rstd_from_ps((ps[:], ps.trk()), (lnv[:], lnv.trk(), 128), (rs[:], rs.trk()), 1.0 / D)
                for c in range(8):
                    tm = tmp[c % 2]
                    kb.op(V, lambda e, tm=tm, c=c, rs=rs: e.scalar_tensor_tensor(out=tm[:], in0=xT[:, c, tsl], scalar=par[:, l, scol + c:scol + c + 1], in1=rs[:],
                                                                                 op0=ALU.mult, op1=ALU.mult), r=[XT[c][n], rs.trk(), PAR], w=[tm.trk()])
                    kb.op(A_, lambda e, tm=tm, c=c: e.activation(out=hT[:, c, tsl], in_=tm[:], func=AF.Identity, bias=par[:, l, shcol + c:shcol + c + 1]),
                          r=[tm.trk(), PAR], w=[HT[c][n]])
            kb.barrier()

    WIN = [dr["w_in"][l].rearrange("(kc p) n -> p kc n", p=128) for l in range(DEPTH)]

    def load_cols(wt, src3, c0, nc_):
        kb.dma("pool", wt[:, :, 0:nc_], src3[:, :, c0:c0 + nc_], wt.trk(), w=[wt.trk()])

    def proj_fm(ps, wt, m0, m, n, rows=None, extra_r=()):
        tsl = slice(n * 512, (n + 1) * 512)

        def f(e):
            ins = None
            for kc in range(8):
                ins = e.matmul(ps[0:m, :], lhsT=wt[:, kc, m0:m0 + m], rhs=hT[:, kc, tsl], start=(kc == 0), stop=(kc == 7))
            return ins
        kb.op("pe", f, r=[wt.trk()] + [HT[c][n] for c in range(8)] + list(extra_r), w=[ps.trk()])

    for l in range(DEPTH):
        lam_init = 0.8 - 0.6 * math.exp(-0.3 * l)
        norm(l, 48, 0)
        dump("h%d" % l, hT[:].rearrange("p c t -> p (c t)"), [128, 8 * T], HT[7][3], BF16)
        if stop_after == "norm1" and l == 0:
            break
        with ExitStack() as lst:
            s5_phase(nc, kb, sb, dr, PS, l, hT, HT, oT, OT, par, PAR, cst, CONST, flg, ident_f, mtz_f, mtz_b, sincos, sincos_tmps, WIN, dump)
            if S5STOP[0] is not None:
                break
            merged = sb(lst, "merged", [128, 8, T], BF16)
            MG = [[merged.trk((c, n)) for n in range(NT)] for c in range(8)]
            def merge_branch(r, first):
                with ExitStack() as st:
                    wg = [sb(st, "wg", [128, 8, 128], BF16) for _ in range(2)]
                    wb = [sb(st, "wb", [128, 4, 128], BF16) for _ in range(2)]
                    sg = [sb(st, "sg", [128, 512]) for _ in range(2)]
                    tm = [sb(st, "mtm", [128, 512]) for _ in range(2)]
                    wbv = dr["w_branch"][l, r].rearrange("(kc p) n -> p kc n", p=128)
                    it = 0
                    for dc in range(8):
                        g_ = wg[dc % 2]
                        b_ = wb[dc % 2]
                        load_cols(g_, WIN[l], C_GZ + r * 1024 + dc * 128, 128)
                        kb.dma("pool", b_[:], wbv[:, :, dc * 128:(dc + 1) * 128], b_.trk(), w=[b_.trk()])
                        for n in range(NT):
                            tsl = slice(n * 512, (n + 1) * 512)
                            pg = PS[(it * 2) % 8]
                            pb = PS[(it * 2 + 1) % 8]
                            it += 1
                            proj_fm(pg, g_, 0, 128, n)

                            def f(e, pb=pb, b_=b_, tsl=tsl):
                                ins = None
                                for kc in range(4):
                                    ins = e.matmul(pb[:], lhsT=b_[:, kc, :], rhs=oT[:, kc, tsl], start=(kc == 0), stop=(kc == 3))
                                return ins
                            kb.op("pe", f, r=[b_.trk()] + [OT[kc][n] for kc in range(4)], w=[pb.trk()])
                            s_ = sg[it % 2]
                            kb.op(A_, lambda e, s_=s_, pg=pg: e.activation(out=s_[:], in_=pg[:], func=AF.Sigmoid), r=[pg.trk()], w=[s_.trk()])
                            if first:
                                kb.op(V, lambda e, s_=s_, pb=pb, dc=dc, tsl=tsl: e.tensor_tensor(out=merged[:, dc, tsl], in0=pb[:], in1=s_[:], op=ALU.mult),
                                      r=[pb.trk(), s_.trk()], w=[MG[dc][n]])
                            else:
                                t_ = tm[it % 2]
                                kb.op(V, lambda e, s_=s_, pb=pb, t_=t_: e.tensor_tensor(out=t_[:], in0=pb[:], in1=s_[:], op=ALU.mult),
                                      r=[pb.trk(), s_.trk()], w=[t_.trk()])
                                kb.op(V, lambda e, t_=t_, dc=dc, tsl=tsl: e.tensor_tensor(out=merged[:, dc, tsl], in0=merged[:, dc, tsl], in1=t_[:], op=ALU.add),
                                      r=[t_.trk(), MG[dc][n]], w=[MG[dc][n]])
                    kb.barrier()

            merge_branch(2, True)
            dump("mergedc%d" % l, merged[:].rearrange("p c t -> p (c t)"), [128, 8 * T], MG[7][3], BF16)
            if stop_after == "s5":
                break
            gla_phase(nc, kb, sb, dr, PS, l, hT, HT, oT, OT, par, PAR, cst, CONST, flg, ones_f, m_le, m_ge, (m_le_b, m_ge_b, m_gt_b, m_lt_b), None, WIN, rstd_from_ps, proj_fm, load_cols, dump)
            merge_branch(1, False)
            if stop_after == "gla":
                break
            attn_phase(nc, kb, sb, dr, PS, l, hT, HT, oT, OT, par, PAR, cst, CONST, None, ident_f, ones_f, ones_b, blk64_b, rrot, make_rope,
                       WIN, rstd_from_ps, proj_fm, load_cols, dump)
            merge_branch(0, False)
            dump("merged%d" % l, merged[:].rearrange("p c t -> p (c t)"), [128, 8 * T], MG[7][3], BF16)
            with ExitStack() as st:
                wo = [sb(st, "wo", [128, 8, 128], BF16) for _ in range(2)]
                wov = dr["w_out"][l].rearrange("(kc p) n -> p kc n", p=128)
                it = 0
                for dc in range(8):
                    w_ = wo[dc % 2]
                    kb.dma("pool", w_[:], wov[:, :, dc * 128:(dc + 1) * 128], w_.trk(), w=[w_.trk()])
                    for n in range(NT):
                        tsl = slice(n * 512, (n + 1) * 512)
                        ps = PS[it % 8]
                        it += 1

                        def f(e, ps=ps, w_=w_, tsl=tsl):
                            ins = None
                            for kc in range(8):
                                ins = e.matmul(ps[:], lhsT=w_[:, kc, :], rhs=merged[:, kc, tsl], start=(kc == 0), stop=(kc == 7))
                            return ins
                        kb.op("pe", f, r=[w_.trk()] + [MG[kc][n] for kc in range(8)], w=[ps.trk()])
                        kb.op(V, lambda e, ps=ps, dc=dc, tsl=tsl: e.scalar_tensor_tensor(out=xT[:, dc, tsl], in0=ps[:], scalar=par[:, l, 16 + dc:17 + dc], in1=xT[:, dc, tsl],
                                                                                       op0=ALU.mult, op1=ALU.add), r=[ps.trk(), PAR, XT[dc][n]], w=[XT[dc][n]])
                kb.barrier()
        dump("xmid%d" % l, xT[:].rearrange("p c t -> p (c t)"), [128, 8 * T], XT[7][3])
        norm(l, 56, 24)
        with ExitStack() as st:
            aT = sb(st, "aT", [128, NJ, 1024], BF16)
            AT = [[aT.trk((j, n)) for n in range(2)] for j in range(NJ)]
            wgt = [sb(st, "fwg", [128, 8, 128], BF16) for _ in range(2)]
            wut = [sb(st, "fwu", [128, 8, 128], BF16) for _ in range(2)]
            wdt = [sb(st, "fwd", [128, NJ, 128], BF16) for _ in range(2)]
            sl = [sb(st, "fsl", [128, 512]) for _ in range(2)]
            wgv = dr["w_ffn_gate"][l].rearrange("(kc p) n -> p kc n", p=128)
            wuv = dr["w_ffn_up"][l].rearrange("(kc p) n -> p kc n", p=128)
            wdv = dr["w_ffn_down"][l].rearrange("(j p) n -> p j n", p=128)
            it = 0
            for half in range(2):
                for j in range(NJ):
                    g_ = wgt[j % 2]
                    u_ = wut[j % 2]
                    kb.dma("pool", g_[:], wgv[:, :, j * 128:(j + 1) * 128], g_.trk(), w=[g_.trk()])
                    kb.dma("pool", u_[:], wuv[:, :, j * 128:(j + 1) * 128], u_.trk(), w=[u_.trk()])
                    for nl in range(2):
                        n = half * 2 + nl
                        pg = PS[(it * 2) % 8]
                        pu = PS[(it * 2 + 1) % 8]
                        it += 1
                        proj_fm(pg, g_, 0, 128, n)
                        proj_fm(pu, u_, 0, 128, n)
                        s_ = sl[it % 2]
                        kb.op(A_, lambda e, s_=s_, pg=pg: e.activation(out=s_[:], in_=pg[:], func=AF.Silu), r=[pg.trk()], w=[s_.trk()])
                        kb.op(V, lambda e, s_=s_, pu=pu, j=j, nl=nl: e.tensor_tensor(out=aT[:, j, nl * 512:(nl + 1) * 512], in0=pu[:], in1=s_[:], op=ALU.mult),
                              r=[pu.trk(), s_.trk()], w=[AT[j][nl]])
                for dc in range(8):
                    w_ = wdt[dc % 2]
                    kb.dma("pool", w_[:], wdv[:, :, dc * 128:(dc + 1) * 128], w_.trk(), w=[w_.trk()])
                    for nl in range(2):
                        n = half * 2 + nl
                        tsl = slice(n * 512, (n + 1) * 512)
                        ps = PS[it % 8]
                        it += 1

                        def f(e, ps=ps, w_=w_, nl=nl):
                            ins = None
                            for j in range(NJ):
                                ins = e.matmul(ps[:], lhsT=w_[:, j, :], rhs=aT[:, j, nl * 512:(nl + 1) * 512], start=(j == 0), stop=(j == NJ - 1))
                            return ins
                        kb.op("pe", f, r=[w_.trk()] + [AT[j][nl] for j in range(NJ)], w=[ps.trk()])
                        kb.op(V, lambda e, ps=ps, dc=dc, tsl=tsl: e.scalar_tensor_tensor(out=xT[:, dc, tsl], in0=ps[:], scalar=par[:, l, 40 + dc:41 + dc], in1=xT[:, dc, tsl],
                                                                                       op0=ALU.mult, op1=ALU.add), r=[ps.trk(), PAR, XT[dc][n]], w=[XT[dc][n]])
            kb.barrier()
        dump("xout%d" % l, xT[:].rearrange("p c t -> p (c t)"), [128, 8 * T], XT[7][3])

    with ExitStack() as st:
        ys = [sb(st, "ys", [128, 1024]) for _ in range(2)]
        for b in range(NB):
            s_ = ys[b % 2]
            for half in range(2):
                ps = PS[(2 * b + half) % 4]
                kb.op("pe", lambda e, ps=ps, b=b, half=half: [e.transpose(out=ps[:, j * 128:(j + 1) * 128], in_=xT[:, half * 4 + j, b * 128:(b + 1) * 128],
                                                                         identity=ident_f[:]) for j in range(4)][-1],
                      r=[XT[half * 4 + j][b // 4] for j in range(4)] + [CONST], w=[ps.trk()])
                if half == 0:
                    kb.op(A_, lambda e, ps=ps, s_=s_: e.activation(out=s_[:, 0:512], in_=ps[:], func=AF.Copy), r=[ps.trk()], w=[s_.trk()])
                else:
                    kb.op(V, lambda e, ps=ps, s_=s_: e.tensor_copy(out=s_[:, 512:1024], in_=ps[:]), r=[ps.trk()], w=[s_.trk()])
            kb.dma("sp", dr["y"][b * 128:(b + 1) * 128, :], s_[:], s_.trk(), r=[s_.trk()])
        kb.barrier()
    kb.final_wait()
    es.close()
    if want_waited:
        return kb.waited
    return nc, dbg_out


def s5_phase(nc, kb, sb, dr, PS, l, hT, HT, ocT, OC, par, PAR, cst, CONST, flg, ident_f, mtz_f, mtz_b, sincos, sincos_tmps, WIN, dump):
    V, P_, A_ = "dve", "pool", "act"
    with ExitStack() as st:
        ycT = sb(st, "ycT", [128, 16, 512], BF16)
        YC = [ycT.trk(g) for g in range(16)]
        hfin = sb(st, "hfin", [128, 512])
        HF = hfin.trk()
        ciota = sb(st, "ciota", [128, 512])
        ones512 = sb(st, "ones512", [128, 512])
        ci_i = sb(st, "cii", [128, 512], I32)
        C5 = Trk()
        kb.op(P_, lambda e: e.iota(ci_i[:], pattern=[[1, 512]], base=0, channel_multiplier=0), w=[C5])
        kb.op(V, lambda e: e.tensor_copy(out=ciota[:], in_=ci_i[:]), r=[C5], w=[C5])
        kb.op(P_, lambda e: e.memset(ones512[:], 1.0), w=[C5])
        PR = Trk()
        NPR = 40
        pr = [sb(st, "s5pr", [128, NPR, 16]) for _ in range(2)]
        bc = [[sb(st, "bcm", [128, 16, 16]) for _ in range(2)] for _ in range(2)]
        cc = [[sb(st, "ccm", [128, 16, 16]) for _ in range(2)] for _ in range(2)]
        pwL = [[sb(st, "pwL", [128, 16, 4]) for _ in range(2)] for _ in range(2)]
        pwR = [[sb(st, "pwR", [128, 16, 4]) for _ in range(2)] for _ in range(2)]
        (LR, LI, LDT, DTt, LRDT, TH, MAG, TURN, CS, SN, AR, AI, A2R, A2I, A3R, A3I, A4R, A4I, IM2, Q1R, Q1I, Q2R, Q2I, Q3R, Q3I,
         DEN, AM1, FR, FI, R4, PHT, T0, T1, T2, H0R, H0I, ONE, ZERO, PXR, PXI) = range(40)
        with ExitStack() as st2:
            braw = [sb(st2, "braw", [128, 16, 16]) for _ in range(2)]
            btmp = sb(st2, "btmp", [128, 16, 16])
            craw = sb(st2, "craw", [16, 2048])
            tms = sincos_tmps(st2, [128, 16])
            for d in range(2):
                p = pr[d]

                def S(i, p=p):
                    return p[:, i, :]

                def tt(o, a, b, op, S=S):
                    kb.op(V, lambda e: e.tensor_tensor(out=S(o), in0=S(a), in1=S(b), op=op), r=[PR], w=[PR])

                def ts(o, a, s1, op0, S=S):
                    kb.op(V, lambda e: e.tensor_scalar(out=S(o), in0=S(a), scalar1=s1, scalar2=None, op0=op0), r=[PR], w=[PR])

                def act(o, a, func, scale=1.0, S=S):
                    kb.op(A_, lambda e: e.activation(out=S(o), in_=S(a), func=func, scale=scale), r=[PR], w=[PR])

                def cmul(o_r, o_i, a_r, a_i, b_r, b_i):
                    tt(T0, a_i, b_i, ALU.mult)
                    tt(T1, a_r, b_r, ALU.mult)
                    tt(T2, a_r, b_i, ALU.mult)
                    tt(o_i, a_i, b_r, ALU.mult)
                    tt(o_i, o_i, T2, ALU.add)
                    tt(o_r, T1, T0, ALU.subtract)

                kb.dma("sp", S(LR), dr["s5_lam_re"][l, d].rearrange("(gp g2) p -> (g2 p) gp", g2=2), PR, w=[PR], allow_slow_non_contiguous=True)
                kb.dma("sp", S(LI), dr["s5_lam_im"][l, d].rearrange("(gp g2) p -> (g2 p) gp", g2=2), PR, w=[PR], allow_slow_non_contiguous=True)
                ldv = dr["s5_log_dt"][l, d].rearrange("(gp g2) -> g2 gp", g2=2)
                for g2 in range(2):
                    kb.dma("sp", p[64 * g2:64 * g2 + 64, LDT, :], ldv[g2:g2 + 1, :].partition_broadcast(64), PR, w=[PR], allow_slow_non_contiguous=True)
                for ri in range(2):
                    kb.dma("sp", S(H0R + ri), dr["s5h0"][l, d, ri].rearrange("(gp g2) p -> (g2 p) gp", g2=2), PR, w=[PR], allow_slow_non_contiguous=True)
                kb.op(P_, lambda e, S=S: e.memset(S(ONE), 1.0), w=[PR])
                kb.op(P_, lambda e, S=S: e.memset(S(ZERO), 0.0), w=[PR])
                act(DTt, LDT, AF.Exp)
                tt(LRDT, LR, DTt, ALU.mult)
                tt(TH, LI, DTt, ALU.mult)
                act(MAG, LRDT, AF.Exp)
                ts(TURN, TH, 1.0 / (2 * math.pi), ALU.mult)
                sincos((tms[0][:], tms[1][:], tms[2][:], [tms[3]]), S(CS), S(SN), S(TURN), [PR], [PR])
                tt(AR, MAG, CS, ALU.mult)
                tt(AI, MAG, SN, ALU.mult)
                cmul(A2R, A2I, AR, AI, AR, AI)
                cmul(A3R, A3I, A2R, A2I, AR, AI)
                cmul(A4R, A4I, A2R, A2I, A2R, A2I)
                act(IM2, LRDT, AF.Exp, -2.0)
                tt(Q1R, AR, IM2, ALU.mult)
                tt(Q1I, AI, IM2, ALU.mult)
                ts(Q1I, Q1I, -1.0, ALU.mult)
                cmul(Q2R, Q2I, Q1R, Q1I, Q1R, Q1I)
                cmul(Q3R, Q3I, Q2R, Q2I, Q1R, Q1I)
                tt(T0, LR, LR, ALU.mult)
                tt(T1, LI, LI, ALU.mult)
                tt(DEN, T0, T1, ALU.add)
                kb.op(V, lambda e, S=S: e.reciprocal(out=S(DEN), in_=S(DEN)), r=[PR], w=[PR])
                ts(AM1, AR, -1.0, ALU.add)
                tt(T0, AM1, LR, ALU.mult)
                tt(T1, AI, LI, ALU.mult)
                tt(T0, T0, T1, ALU.add)
                tt(FR, T0, DEN, ALU.mult)
                tt(T0, AI, LR, ALU.mult)
                tt(T1, AM1, LI, ALU.mult)
                tt(T0, T0, T1, ALU.subtract)
                tt(FI, T0, DEN, ALU.mult)
                act(R4, LRDT, AF.Exp, 4.0)
                ts(PHT, TURN, 4.0, ALU.mult)
                pos = [(ONE, ZERO), (AR, AI), (A2R, A2I), (A3R, A3I)]
                neg = [(ONE, ZERO), (Q1R, Q1I), (Q2R, Q2I), (Q3R, Q3I)]
                Lp, Rp = (neg, pos) if d == 0 else (pos, neg)
                for s in range(4):
                    for ri in range(2):
                        kb.op(V, lambda e, s=s, ri=ri, S=S, Lp=Lp: e.tensor_copy(out=pwL[d][ri][:, :, s], in_=S(Lp[s][ri])), r=[PR], w=[PR])
                        kb.op(V, lambda e, s=s, ri=ri, S=S, Rp=Rp: e.tensor_copy(out=pwR[d][ri][:, :, s], in_=S(Rp[s][ri])), r=[PR], w=[PR])
                for ri, nm in enumerate(("s5_b_re", "s5_b_im")):
                    kb.dma("sp", braw[ri][:], dr[nm][l, d].rearrange("(gp g2) p j -> (g2 p) gp j", g2=2), PR, w=[PR])
                frb = S(FR).unsqueeze(2).to_broadcast([128, 16, 16])
                fib = S(FI).unsqueeze(2).to_broadcast([128, 16, 16])
                bb = bc[d]
                kb.op(V, lambda e, bb=bb, frb=frb: e.tensor_tensor(out=bb[0][:], in0=braw[0][:], in1=frb, op=ALU.mult), r=[PR], w=[PR])
                kb.op(V, lambda e, fib=fib: e.tensor_tensor(out=btmp[:], in0=braw[1][:], in1=fib, op=ALU.mult), r=[PR], w=[PR])
                kb.op(V, lambda e, bb=bb: e.tensor_tensor(out=bb[0][:], in0=bb[0][:], in1=btmp[:], op=ALU.subtract), r=[PR], w=[PR])
                kb.op(V, lambda e, bb=bb, frb=frb: e.tensor_tensor(out=bb[1][:], in0=braw[1][:], in1=frb, op=ALU.mult), r=[PR], w=[PR])
                kb.op(V, lambda e, fib=fib: e.tensor_tensor(out=btmp[:], in0=braw[0][:], in1=fib, op=ALU.mult), r=[PR], w=[PR])
                kb.op(V, lambda e, bb=bb: e.tensor_tensor(out=bb[1][:], in0=bb[1][:], in1=btmp[:], op=ALU.add), r=[PR], w=[PR])
                for ri, nm in enumerate(("s5_c_re", "s5_c_im")):
                    kb.dma("sp", craw[:].rearrange("i (g p) -> i g p", p=64), dr[nm][l, d].rearrange("g i p -> i g p"), PR, w=[PR])
                    ps = PS[6 + ri]
                    kb.op("pe", lambda e, ps=ps: [e.transpose(out=ps[:, gp * 16:(gp + 1) * 16], in_=craw[:, gp * 128:(gp + 1) * 128], identity=ident_f[0:16, 0:16])
                                                  for gp in range(16)][-1], r=[PR, CONST], w=[ps.trk()])
                    kb.op(V, lambda e, ri=ri, ps=ps, d=d: e.tensor_copy(out=cc[d][ri][:], in_=ps[:, 0:256].rearrange("p (g i) -> p g i", i=16)),
                          r=[ps.trk(), PR], w=[PR])
                if d == 0:
                    kb.op(V, lambda e, S=S: e.tensor_copy(out=S(PXR), in_=S(A3R)), r=[PR], w=[PR])
                    kb.op(V, lambda e, S=S: e.tensor_copy(out=S(PXI), in_=S(A3I)), r=[PR], w=[PR])
            kb.barrier()
        dump("s5pr%d" % l, pr[0][:].rearrange("p a b -> p (a b)"), [128, NPR * 16], PR)
        if S5STOP[0] == "prep":
            kb.barrier()
            return
        QX = [(AR, AI), (A4R, A4I)]

        with ExitStack() as st3:
            wcu = [sb(st3, "wcu", [128, 8, 32], BF16) for _ in range(2)]
            u4b = [sb(st3, "u4b", [128, 512], BF16) for _ in range(2)]
            u4f = sb(st3, "u4f", [128, 512])
            zsrc = [[sb(st3, "zs", [128, 32]) for _ in range(2)] for _ in range(2)]
            Lt = [sb(st3, "Lt", [128, 4, 32]) for _ in range(2)]
            Rt = [sb(st3, "Rt", [128, 4, 32]) for _ in range(2)]
            L3 = [sb(st3, "L3", [128, 128]) for _ in range(2)]
            Qm = [[sb(st3, "Qm", [128, 128]) for _ in range(2)] for _ in range(2)]
            ctmp = sb(st3, "ctmp", [128, 4, 32])
            c128 = sb(st3, "c128", [128, 128])
            tzb = [sb(st3, "tzb", [128, 128], BF16) for _ in range(2)]
            pmb = [[sb(st3, "pmb", [128, 128], BF16) for _ in range(2)] for _ in range(2)]
            cosT = sb(st3, "cosT", [128, 512])
            sinT = sb(st3, "sinT", [128, 512])
            xr = [sb(st3, "xr", [128, 512]) for _ in range(2)]
            gg = [sb(st3, "gg", [128, 512]) for _ in range(2)]
            tA = sb(st3, "tA", [128, 512])
            tB = sb(st3, "tB", [128, 512])
            dec = sb(st3, "dec", [128, 512])
            hs = [sb(st3, "hs", [128, 513]) for _ in range(2)]
            ini = sb(st3, "ini", [128, 4])
            TT_ = Trk()
            ZS = Trk()
            for a_ in range(2):
                for b_ in range(2):
                    kb.op(P_, lambda e, a_=a_, b_=b_: e.memset(zsrc[a_][b_][:], 0.0), w=[ZS])
            hfv = hfin[:].rearrange("p (q d r g) -> p q d r g", d=2, r=2, g=16)
            wcv = WIN[l]
            try:
                def prep_gen(gp, d):
                    ub = u4b[gp % 2]
                    if d == 0:
                        wc_ = wcu[gp % 2]
                        yield
                        kb.dma("pool", wc_[:], wcv[:, :, C_CU + gp * 32:C_CU + gp * 32 + 32], wc_.trk(), w=[wc_.trk()])
                        pu = PS[0]
                        def fu(e, wc_=wc_):
                            ins = None
                            for kc in range(8):
                                for s in range(4):
                                    ins = e.matmul(pu[32 * s:32 * s + 32, :], lhsT=wc_[:, kc, :], rhs=hT[:, kc, s::4], start=(kc == 0), stop=(kc == 7),
                                                   tile_position=(0, 32 * s))
                            return ins
                        yield
                        kb.op("pe", fu, r=[wc_.trk()] + [HT[c][n] for c in range(8) for n in range(NT)], w=[pu.trk()])
                        yield
                        kb.op(A_, lambda e, ub=ub: e.activation(out=ub[:], in_=pu[:], func=AF.Copy), r=[pu.trk()], w=[ub.trk()])
                        yield
                    p = pr[d]
                    for a_, srcs in ((0, bc[d]), (1, cc[d])):
                        for ri in range(2):
                            for g2 in range(2):
                                kb.op(V, lambda e, a_=a_, ri=ri, g2=g2, srcs=srcs: e.tensor_copy(out=zsrc[a_][ri][64 * g2:64 * g2 + 64, 16 * g2:16 * g2 + 16],
                                                                                                 in_=srcs[ri][64 * g2:64 * g2 + 64, gp, :]), r=[PR], w=[ZS])
                    for (dst, src, pw, neg_im) in ((Lt, zsrc[0], pwL[d], False), (Rt, zsrc[1], pwR[d], True)):
                        s_re = src[0][:, :].unsqueeze(1).to_broadcast([128, 4, 32])
                        s_im = src[1][:, :].unsqueeze(1).to_broadcast([128, 4, 32])
                        w_re = pw[0][:, gp, :].unsqueeze(2).to_broadcast([128, 4, 32])
                        w_im = pw[1][:, gp, :].unsqueeze(2).to_broadcast([128, 4, 32])
                        yield
                        kb.op(V, lambda e, dst=dst, s_re=s_re, w_re=w_re: e.tensor_tensor(out=dst[0][:], in0=s_re, in1=w_re, op=ALU.mult), r=[PR, ZS], w=[dst[0].trk()])
                        yield
                        kb.op(V, lambda e, s_im=s_im, w_im=w_im: e.tensor_tensor(out=ctmp[:], in0=s_im, in1=w_im, op=ALU.mult), r=[PR, ZS], w=[ctmp.trk()])
                        yield
                        kb.op(V, lambda e, dst=dst: e.tensor_tensor(out=dst[0][:], in0=dst[0][:], in1=ctmp[:], op=ALU.subtract), r=[ctmp.trk()], w=[dst[0].trk()])
                        yield
                        kb.op(V, lambda e, dst=dst, s_re=s_re, w_im=w_im: e.tensor_tensor(out=dst[1][:], in0=s_re, in1=w_im, op=ALU.mult), r=[PR, ZS], w=[dst[1].trk()])
                        yield
                        kb.op(V, lambda e, s_im=s_im, w_re=w_re: e.tensor_tensor(out=ctmp[:], in0=s_im, in1=w_re, op=ALU.mult), r=[PR, ZS], w=[ctmp.trk()])
                        if neg_im:
                            yield
                            kb.op(V, lambda e, dst=dst: e.scalar_tensor_tensor(out=dst[1][:], in0=dst[1][:], scalar=-1.0, in1=ctmp[:], op0=ALU.mult, op1=ALU.subtract),
                                  r=[ctmp.trk()], w=[dst[1].trk()])
                        else:
                            kb.op(V, lambda e, dst=dst: e.tensor_tensor(out=dst[1][:], in0=dst[1][:], in1=ctmp[:], op=ALU.add), r=[ctmp.trk()], w=[dst[1].trk()])
                    L2 = [Lt[i][:].rearrange("p s j -> p (s j)") for i in range(2)]
                    R2 = [Rt[i][:].rearrange("p s j -> p (s j)") for i in range(2)]
                    LT_ = [Lt[0].trk(), Lt[1].trk()]
                    RT_ = [Rt[0].trk(), Rt[1].trk()]
                    pt = PS[2]
                    yield
                    kb.op("pe", lambda e, L2=L2, R2=R2: [e.matmul(pt[:, 0:128], lhsT=L2[0], rhs=R2[0], start=True, stop=False),
                                                        e.matmul(pt[:, 0:128], lhsT=L2[1], rhs=R2[1], start=False, stop=True)][-1],
                          r=LT_ + RT_, w=[pt.trk()])
                    mk = mtz_f if d == 0 else mtz_b
                    yield
                    kb.op(V, lambda e, d=d, mk=mk: e.tensor_tensor(out=tzb[d][:], in0=pt[:, 0:128], in1=mk[:], op=ALU.mult), r=[pt.trk(), CONST], w=[tzb[d].trk()])
                    if d == 0:
                        c_r = p[:, PXR, gp:gp + 1]
                        c_i = p[:, PXI, gp:gp + 1]
                        yield
                        kb.op(V, lambda e, L2=L2, c_i=c_i: e.tensor_scalar(out=c128[:], in0=L2[1], scalar1=c_i, scalar2=None, op0=ALU.mult), r=[LT_[1], PR], w=[c128.trk()])
                        yield
                        kb.op(V, lambda e, L2=L2, c_r=c_r: e.scalar_tensor_tensor(out=L3[0][:], in0=L2[0], scalar=c_r, in1=c128[:], op0=ALU.mult, op1=ALU.subtract),
                              r=[LT_[0], c128.trk(), PR], w=[L3[0].trk()])
                        yield
                        kb.op(V, lambda e, L2=L2, c_i=c_i: e.tensor_scalar(out=c128[:], in0=L2[0], scalar1=c_i, scalar2=None, op0=ALU.mult), r=[LT_[0], PR], w=[c128.trk()])
                        yield
                        kb.op(V, lambda e, L2=L2, c_r=c_r: e.scalar_tensor_tensor(out=L3[1][:], in0=L2[1], scalar=c_r, in1=c128[:], op0=ALU.mult, op1=ALU.add),
                              r=[LT_[1], c128.trk(), PR], w=[L3[1].trk()])
                        Lsrc = [L3[0][:], L3[1][:]]
                        ltr = [L3[0].trk(), L3[1].trk()]
                    else:
                        Lsrc = L2
                        ltr = LT_
                    pp = PS[3]
                    yield
                    kb.op("pe", lambda e, Lsrc=Lsrc: [e.transpose(out=pp[:, 0:128], in_=Lsrc[0], identity=ident_f[:]),
                                                      e.transpose(out=pp[:, 128:256], in_=Lsrc[1], identity=ident_f[:])][-1], r=ltr + [CONST], w=[pp.trk()])
                    yield
                    kb.op(A_, lambda e, d=d: e.activation(out=pmb[d][0][:], in_=pp[:, 0:128], func=AF.Copy), r=[pp.trk()], w=[pmb[d][0].trk()])
                    yield
                    kb.op(A_, lambda e, d=d: e.activation(out=pmb[d][1][:], in_=pp[:, 128:256], func=AF.Copy), r=[pp.trk()], w=[pmb[d][1].trk()])
                    c_r = p[:, QX[d][0], gp:gp + 1]
                    c_i = p[:, QX[d][1], gp:gp + 1]
                    yield
                    kb.op(V, lambda e, R2=R2, c_i=c_i: e.tensor_scalar(out=c128[:], in0=R2[1], scalar1=c_i, scalar2=None, op0=ALU.mult), r=[RT_[1], PR], w=[c128.trk()])
                    yield
                    kb.op(V, lambda e, d=d, R2=R2, c_r=c_r: e.scalar_tensor_tensor(out=Qm[d][0][:], in0=R2[0], scalar=c_r, in1=c128[:], op0=ALU.mult, op1=ALU.add),
                          r=[RT_[0], c128.trk(), PR], w=[Qm[d][0].trk()])
                    yield
                    kb.op(V, lambda e, R2=R2, c_i=c_i: e.tensor_scalar(out=c128[:], in0=R2[0], scalar1=c_i, scalar2=None, op0=ALU.mult), r=[RT_[0], PR], w=[c128.trk()])
                    yield
                    kb.op(V, lambda e, d=d, R2=R2, c_r=c_r: e.scalar_tensor_tensor(out=Qm[d][1][:], in0=R2[1], scalar=c_r, in1=c128[:], op0=ALU.mult, op1=ALU.subtract),
                          r=[RT_[1], c128.trk(), PR], w=[Qm[d][1].trk()])
                    yield

                def recur_gen(gp, d):
                    ub = u4b[gp % 2]
                    py = PS[1]
                    p = pr[d]
                    px = [PS[4], PS[5]]
                    for ri in range(2):
                        kb.op("pe", lambda e, ri=ri, d=d, ub=ub: e.matmul(px[ri][:], lhsT=pmb[d][ri][:], rhs=ub[:], start=True, stop=True),
                              r=[pmb[d][ri].trk(), ub.trk()], w=[px[ri].trk()])
                    tur = gg[1]
                    yield
                    kb.op(V, lambda e, p=p: e.tensor_scalar(out=tur[:], in0=ciota[:], scalar1=p[:, PHT, gp:gp + 1], scalar2=None, op0=ALU.mult),
                          r=[C5, PR], w=[tur.trk()])
                    yield
                    sincos((xr[0][:].bitcast(I32), xr[1][:], gg[0][:], [xr[0].trk(), xr[1].trk(), gg[0].trk()]), cosT[:], sinT[:], tur[:], [tur.trk()], [TT_])
                    def pv(ap, d=d):
                        return ap if d == 0 else ap[:, ::-1]
                    X_r, X_i = pv(px[0][:]), pv(px[1][:])
                    yield
                    kb.op(V, lambda e, X_i=X_i: e.tensor_tensor(out=tA[:], in0=X_i, in1=sinT[:], op=ALU.mult), r=[px[1].trk(), TT_], w=[tA.trk()])
                    yield
                    kb.op(V, lambda e, X_r=X_r: e.tensor_tensor(out=xr[0][:], in0=X_r, in1=cosT[:], op=ALU.mult), r=[px[0].trk(), TT_], w=[xr[0].trk()])
                    yield
                    kb.op(V, lambda e: e.tensor_tensor(out=xr[0][:], in0=xr[0][:], in1=tA[:], op=ALU.add), r=[tA.trk()], w=[xr[0].trk()])
                    yield
                    kb.op(V, lambda e, X_r=X_r: e.tensor_tensor(out=tB[:], in0=X_r, in1=sinT[:], op=ALU.mult), r=[px[0].trk(), TT_], w=[tB.trk()])
                    yield
                    kb.op(V, lambda e, X_i=X_i: e.tensor_tensor(out=xr[1][:], in0=X_i, in1=cosT[:], op=ALU.mult), r=[px[1].trk(), TT_], w=[xr[1].trk()])
                    yield
                    kb.op(V, lambda e: e.tensor_tensor(out=xr[1][:], in0=xr[1][:], in1=tB[:], op=ALU.subtract), r=[tB.trk()], w=[xr[1].trk()])
                    yield
                    kb.op(A_, lambda e, p=p: e.activation(out=dec[:], in_=ciota[:], func=AF.Identity, scale=0.0, bias=p[:, R4, gp:gp + 1]), r=[C5, PR], w=[dec.trk()])
                    yield
                    kb.op(V, lambda e: e.tensor_scalar(out=dec[:, ::64], in0=dec[:, ::64], scalar1=flg[:, 0:1], scalar2=None, op0=ALU.mult), r=[CONST], w=[dec.trk()])
                    h0r = p[:, H0R, gp:gp + 1]
                    h0i = p[:, H0I, gp:gp + 1]
                    c1 = cosT[:, 1:2]
                    s1 = sinT[:, 1:2]
                    IN_ = ini.trk()
                    yield
                    kb.op(V, lambda e, h0i=h0i, s1=s1: e.tensor_tensor(out=ini[:, 2:3], in0=h0i, in1=s1, op=ALU.mult), r=[PR, TT_], w=[IN_])
                    yield
                    kb.op(V, lambda e, h0r=h0r, c1=c1: e.scalar_tensor_tensor(out=ini[:, 0:1], in0=h0r, scalar=c1, in1=ini[:, 2:3], op0=ALU.mult, op1=ALU.subtract), r=[PR, TT_], w=[IN_])
                    yield
                    kb.op(V, lambda e, h0r=h0r, s1=s1: e.tensor_tensor(out=ini[:, 3:4], in0=h0r, in1=s1, op=ALU.mult), r=[PR, TT_], w=[IN_])
                    yield
                    kb.op(V, lambda e, h0i=h0i, c1=c1: e.scalar_tensor_tensor(out=ini[:, 1:2], in0=h0i, scalar=c1, in1=ini[:, 3:4], op0=ALU.mult, op1=ALU.add), r=[PR, TT_], w=[IN_])
                    for ri in range(2):
                        kb.op(V, lambda e, ri=ri: e.tensor_tensor_scan(out=gg[ri][:], data0=dec[:], data1=xr[ri][:], initial=ini[:, ri:ri + 1], op0=ALU.mult, op1=ALU.add),
                              r=[dec.trk(), xr[ri].trk(), IN_], w=[gg[ri].trk()])
                    if d == 0:
                        Hre, Him = hs[0][:, 1:513], hs[1][:, 1:513]
                    else:
                        Hre, Him = hs[0][:, 0:512][:, ::-1], hs[1][:, 0:512][:, ::-1]
                    HS = [hs[0].trk(), hs[1].trk()]
                    yield
                    kb.op(V, lambda e: e.tensor_tensor(out=tA[:], in0=gg[1][:], in1=sinT[:], op=ALU.mult), r=[gg[1].trk(), TT_], w=[tA.trk()])
                    yield
                    kb.op(V, lambda e: e.tensor_tensor(out=tB[:], in0=gg[0][:], in1=cosT[:], op=ALU.mult), r=[gg[0].trk(), TT_], w=[tB.trk()])
                    yield
                    kb.op(V, lambda e, Hre=Hre: e.tensor_tensor(out=Hre, in0=tB[:], in1=tA[:], op=ALU.subtract), r=[tA.trk(), tB.trk()], w=[HS[0]])
                    yield
                    kb.op(V, lambda e: e.tensor_tensor(out=tA[:], in0=gg[0][:], in1=sinT[:], op=ALU.mult), r=[gg[0].trk(), TT_], w=[tA.trk()])
                    yield
                    kb.op(V, lambda e: e.tensor_tensor(out=tB[:], in0=gg[1][:], in1=cosT[:], op=ALU.mult), r=[gg[1].trk(), TT_], w=[tB.trk()])
                    yield
                    kb.op(V, lambda e, Him=Him: e.tensor_tensor(out=Him, in0=tB[:], in1=tA[:], op=ALU.add), r=[tA.trk(), tB.trk()], w=[HS[1]])
                    for ri in range(2):
                        h_ = hs[ri]
                        if d == 0:
                            fin = h_[:, 64:513:64]
                            icol = h_[:, 0:1]
                        else:
                            fin = h_[:, 0:512:64]
                            icol = h_[:, 512:513]
                        yield
                        kb.op(V, lambda e, ri=ri, d=d, fin=fin: e.tensor_copy(out=hfv[:, :, d, ri, gp], in_=fin), r=[HS[ri]], w=[HF])
                        yield
                        kb.op(V, lambda e, h_=h_: e.tensor_scalar(out=h_[:, 64:512:64], in0=h_[:, 64:512:64], scalar1=flg[:, 0:1], scalar2=None, op0=ALU.mult),
                              r=[CONST, HF], w=[HS[ri]])
                        yield
                        kb.op(V, lambda e, icol=icol, ri=ri, p=p: e.tensor_copy(out=icol, in_=p[:, H0R + ri, gp:gp + 1]), r=[PR], w=[HS[ri]])
                    hv = [hs[0][:, 0:512], hs[1][:, 0:512]] if d == 0 else [hs[0][:, 1:513], hs[1][:, 1:513]]
                    def fy(e, d=d, hv=hv, ub=ub):
                        e.matmul(py[:], lhsT=tzb[d][:], rhs=ub[:], start=(d == 0), stop=False)
                        e.matmul(py[:], lhsT=Qm[d][0][:], rhs=hv[0], start=False, stop=False)
                        return e.matmul(py[:], lhsT=Qm[d][1][:], rhs=hv[1], start=False, stop=(d == 1))
                    yield
                    kb.op("pe", fy, r=[tzb[d].trk(), ub.trk(), Qm[d][0].trk(), Qm[d][1].trk()] + HS, w=[py.trk()])
                    if d == 1:
                        ytm = xr[0]
                        yield
                        kb.op(V, lambda e: e.scalar_tensor_tensor(out=ytm[:], in0=ub[:], scalar=par[:, l, 96 + gp:97 + gp], in1=py[:], op0=ALU.mult, op1=ALU.add),
                              r=[ub.trk(), PAR, py.trk()], w=[ytm.trk()])
                        if gp == 0:
                            dump("s5y%d" % l, ytm[:], [128, 512], ytm.trk())
                        yield
                        kb.op(A_, lambda e: e.activation(out=tA[:], in_=ytm[:], func=AF.Square), r=[ytm.trk()], w=[tA.trk()])
                        yield
                        kb.op(V, lambda e: e.tensor_scalar(out=tA[:], in0=tA[:], scalar1=0.044715, scalar2=1.0, op0=ALU.mult, op1=ALU.add), r=[], w=[tA.trk()])
                        yield
                        kb.op(V, lambda e: e.tensor_tensor(out=tA[:], in0=tA[:], in1=ytm[:], op=ALU.mult), r=[ytm.trk()], w=[tA.trk()])
                        yield
                        kb.op(A_, lambda e: e.activation(out=tA[:], in_=tA[:], func=AF.Tanh, scale=0.7978845608), r=[], w=[tA.trk()])
                        yield
                        kb.op(V, lambda e: e.tensor_scalar(out=tA[:], in0=tA[:], scalar1=1.0, scalar2=0.5, op0=ALU.add, op1=ALU.mult), r=[], w=[tA.trk()])
                        yield
                        kb.op(V, lambda e, gp=gp: e.tensor_tensor(out=ycT[:, gp, :], in0=tA[:], in1=ytm[:], op=ALU.mult), r=[ytm.trk(), tA.trk()], w=[YC[gp]])
                    yield

                NK = 32
                g0 = prep_gen(0, 0)
                for _ in g0:
                    pass
                for k in range(NK):
                    gens = [recur_gen(k // 2, k % 2)]
                    if k + 1 < NK:
                        gens.append(prep_gen((k + 1) // 2, (k + 1) % 2))
                    while gens:
                        for g_ in list(gens):
                            try:
                                next(g_)
                            except StopIteration:
                                gens.remove(g_)
            except _Stop:
                pass
            kb.barrier()
        if S5STOP[0] not in (None, "fin", "glu"):
            return
        with ExitStack() as st5:
            hfo = sb(st5, "hfo", [128, 512])
            ps = PS[0]
            kb.op("pe", lambda e: [e.transpose(out=ps[:, j * 128:(j + 1) * 128], in_=hfin[:, j * 128:(j + 1) * 128], identity=ident_f[:]) for j in range(4)][-1],
                  r=[HF, CONST], w=[ps.trk()])
            kb.op(V, lambda e: e.tensor_copy(out=hfo[:], in_=ps[:]), r=[ps.trk()], w=[hfo.trk()])
            ov = dr["news5"][l].rearrange("q d r g p -> (q d r g p)").rearrange("(j x y) -> x j y", x=128, y=128)
            kb.dma("sp", ov, hfo[:].rearrange("p (j y) -> p j y", y=128), hfo.trk(), r=[hfo.trk()])
            kb.barrier()
        if S5STOP[0] == "fin":
            return
        with ExitStack() as st6:
            wrep = [[sb(st6, "wrep", [128, 16, 128], BF16) for _ in range(2)] for _ in range(2)]
            sgt = [sb(st6, "sgt", [128, 512]) for _ in range(2)]
            wgv = dr["s5_w_glu"][l].rearrange("(gp ii) n -> ii gp n", ii=32)
            it = 0
            for j in range(4):
                for ab in range(2):
                    w_ = wrep[j % 2][ab]
                    c0 = (ab * 4 + j) * 128
                    for t in range(4):
                        kb.dma("pool", w_[32 * t:32 * t + 32, :, :], wgv[:, :, c0:c0 + 128], w_.trk(), w=[w_.trk()])
                bk = [[PS[ab * 4 + t] for t in range(4)] for ab in range(2)]
                for ab in range(2):
                    w_ = wrep[j % 2][ab]

                    def fg(e, w_=w_, ab=ab):
                        ins = None
                        for gp in range(16):
                            for t in range(4):
                                ins = e.matmul(bk[ab][t][:], lhsT=w_[32 * t:32 * t + 32, gp, :], rhs=ycT[32 * t:32 * t + 32, gp, :], start=(gp == 0), stop=(gp == 15),
                                               tile_position=(32 * t, 0))
                        return ins
                    kb.op("pe", fg, r=[w_.trk()] + YC, w=[bk[ab][t].trk() for t in range(4)])
                for t in range(4):
                    pa, pb = bk[0][t], bk[1][t]
                    it += 1
                    s_ = sgt[it % 2]
                    kb.op(A_, lambda e, s_=s_, pb=pb, j=j: e.activation(out=s_[:], in_=pb[:], func=AF.Sigmoid, bias=par[:, l, 92 + j:93 + j]), r=[pb.trk(), PAR], w=[s_.trk()])
                    kb.op(V, lambda e, s_=s_, pa=pa, j=j, t=t: e.scalar_tensor_tensor(out=ocT[:, j, t::4], in0=pa[:], scalar=par[:, l, 88 + j:89 + j], in1=s_[:], op0=ALU.add, op1=ALU.mult),
                          r=[pa.trk(), s_.trk(), PAR], w=[OC[j][n] for n in range(NT)])
            kb.barrier()
    dump("oc%d" % l, ocT[:].rearrange("p c t -> p (c t)"), [128, 4 * T], OC[3][3], BF16)


def gla_phase(nc, kb, sb, dr, PS, l, hT, HT, oT, OT, par, PAR, cst, CONST, flg, ones_f, m_le, m_ge, m_gt, m_lt, WIN, rstd_from_ps, proj_fm, load_cols, dump):
    V, P_, A_ = "dve", "pool", "act"
    with ExitStack() as st:
        rTa = [sb(st, "rTa", [17, T], BF16) for _ in range(2)]
        wa2a = [sb(st, "wa2a", [17, 256], BF16) for _ in range(2)]
        wbr = sb(st, "wbr", [128, 8, 32], BF16)
        load_cols(wbr, WIN[l], C_BR, 32)
        RT = [Trk(), Trk()]
        for d in range(2):
            kb.op(P_, lambda e, d=d: e.memset(rTa[d][:], 1.0), w=[RT[d]])
            kb.dma("pool", wa2a[d][0:16, :], dr["gla_wa2"][l, d], wa2a[d].trk(), w=[wa2a[d].trk()])
            kb.dma("pool", wa2a[d][16:17, :], dr["gla_ba"][l, d].rearrange("(o n) -> o n", o=1), wa2a[d].trk(), w=[wa2a[d].trk()])
            for n in range(NT):
                ps = PS[n % 2]
                proj_fm(ps, wbr, d * 16, 16, n)
                kb.op(A_, lambda e, d=d, n=n, ps=ps: e.activation(out=rTa[d][0:16, n * 512:(n + 1) * 512], in_=ps[0:16, :], func=AF.Copy), r=[ps.trk()], w=[RT[d]])
        wq = sb(st, "gwq", [128, 8, 64], BF16)
        wk = sb(st, "gwk", [128, 8, 64], BF16)
        wv = sb(st, "gwv", [128, 8, 128], BF16)
        wg = sb(st, "gwg", [128, 8, 128], BF16)
        bqT = sb(st, "bqT", [64, T], BF16)
        bkT = sb(st, "bkT", [64, T], BF16)
        bkt = sb(st, "bkt", [128, NB, 64], BF16)
        bvt = sb(st, "bvt", [128, NB, 128], BF16)
        sgT = sb(st, "sgT", [128, T], BF16)
        obuf = sb(st, "obuf", [128, T])
        OBF = [obuf.trk(b) for b in range(NB)]
        Sd = [sb(st, "gS", [64, 128]) for _ in range(2)]
        Sbd = [sb(st, "gSb", [64, 128], BF16) for _ in range(2)]
        stg = [sb(st, "gstg", [64, 128]) for _ in range(2)]
        step = 0
        nst_ = [0]
        for h in range(4):
            load_cols(wq, WIN[l], C_BQ + h * 64, 64)
            load_cols(wk, WIN[l], C_BK + h * 64, 64)
            load_cols(wv, WIN[l], C_BV + h * 128, 128)
            load_cols(wg, WIN[l], C_BG + h * 128, 128)
            for n in range(NT):
                tsl = slice(n * 512, (n + 1) * 512)
                pa, pb = PS[0], PS[1]
                proj_fm(pa, wq, 0, 64, n)
                kb.op(A_, lambda e, tsl=tsl, pa=pa: e.activation(out=bqT[:, tsl], in_=pa[0:64, :], func=AF.Copy, scale=0.125), r=[pa.trk()], w=[bqT.trk()])
                proj_fm(pb, wk, 0, 64, n)
                kb.op(V, lambda e, tsl=tsl, pb=pb: e.tensor_copy(out=bkT[:, tsl], in_=pb[0:64, :]), r=[pb.trk()], w=[bkT.trk()])
                proj_fm(pa, wg, 0, 128, n)
                kb.op(A_, lambda e, tsl=tsl, pa=pa: e.activation(out=sgT[:, tsl], in_=pa[:], func=AF.Silu), r=[pa.trk()], w=[sgT.trk()])
            for g4 in range(4):
                pk, pv = PS[0], PS[1]

                def fk(e, g4=g4, pk=pk):
                    ins = None
                    for j in range(4):
                        blk = g4 * 4 + j
                        for kc in range(8):
                            ins = e.matmul(pk[:, j * 64:(j + 1) * 64], lhsT=hT[:, kc, blk * 128:(blk + 1) * 128], rhs=wk[:, kc, :], start=(kc == 0), stop=(kc == 7))
                    return ins
                kb.op("pe", fk, r=[wk.trk()] + [HT[c][g4] for c in range(8)], w=[pk.trk()])
                kb.op(V, lambda e, g4=g4, pk=pk: e.tensor_copy(out=bkt[:, g4 * 4:(g4 + 1) * 4, :], in_=pk[:, 0:256].rearrange("p (j f) -> p j f", f=64)), r=[pk.trk()], w=[bkt.trk()])

                def fv(e, g4=g4, pv=pv):
                    ins = None
                    for j in range(4):
                        blk = g4 * 4 + j
                        for kc in range(8):
                            ins = e.matmul(pv[:, j * 128:(j + 1) * 128], lhsT=hT[:, kc, blk * 128:(blk + 1) * 128], rhs=wv[:, kc, :], start=(kc == 0), stop=(kc == 7))
                    return ins
                kb.op("pe", fv, r=[wv.trk()] + [HT[c][g4] for c in range(8)], w=[pv.trk()])
                kb.op(A_, lambda e, g4=g4, pv=pv: e.activation(out=bvt[:, g4 * 4:(g4 + 1) * 4, :], in_=pv[:].rearrange("p (j f) -> p j f", f=128), func=AF.Copy), r=[pv.trk()], w=[bvt.trk()])
            for d in range(2):
                kb.dma("sp", Sd[d][:], dr["gla0"][l, d, h], Sd[d].trk(), w=[Sd[d].trk()])
                kb.op(A_, lambda e, d=d: e.activation(out=Sbd[d][:], in_=Sd[d][:], func=AF.Copy), r=[Sd[d].trk()], w=[Sbd[d].trk()])
            with ExitStack() as lt:
                def tn(nm, shape, dt=F32, n=4):
                    return [sb(lt, nm, shape, dt) for _ in range(n)]
                e1, spt, eD = tn("ge1", [128, 64], F32, 2), tn("gsp", [128, 64], BF16, 6), tn("ged", [128, 64], BF16, 6)
                eGT, enGT = tn("geg", [64, 128], F32, 6), tn("gen", [64, 128], F32, 6)
                qt, kt = tn("gqt", [64, 128], BF16), tn("gkt", [64, 128], BF16)
                kp, am = tn("gkp", [128, 64], BF16), tn("gam", [128, 128], BF16)

                def la_gen(d, i_):
                    trib = m_gt[0] if d == 0 else m_gt[1]
                    strict = m_gt[2] if d == 0 else m_gt[3]
                    b = i_ if d == 0 else NB - 1 - i_
                    i3 = d * 3 + i_ % 3
                    bsl = slice(b * 128, (b + 1) * 128)
                    bA = PS[2 * d + i_ % 2]
                    pla, pd, pgt = SubView(bA, 0), SubView(bA, 64), SubView(bA, 128)
                    kb.op("pe", lambda e: e.matmul(pla[:, 0:64], lhsT=rTa[d][:, bsl], rhs=wa2a[d][:, h * 64:(h + 1) * 64], start=True, stop=True),
                          r=[RT[d], wa2a[d].trk()], w=[pla.trk()])
                    yield
                    kb.op(A_, lambda e: e.activation(out=e1[d][:], in_=pla[:, 0:64], func=AF.Exp, scale=-1.0), r=[pla.trk()], w=[e1[d].trk()])
                    yield
                    kb.op(A_, lambda e: e.activation(out=spt[i3][:], in_=e1[d][:], func=AF.Ln, bias=cst[:, 1:2]), r=[e1[d].trk(), CONST], w=[spt[i3].trk()])
                    yield
                    kb.op("pe", lambda e: [e.matmul(pgt[0:64, 0:128], lhsT=spt[i3][:], rhs=trib[:], start=True, stop=True),
                                          e.matmul(pd[:, 0:64], lhsT=strict[:], rhs=spt[i3][:], start=True, stop=True)][-1], r=[spt[i3].trk(), CONST], w=[bA.trk()])
                    yield
                    kb.op(A_, lambda e: e.activation(out=eGT[i3][:], in_=pgt[0:64, 0:128], func=AF.Exp, scale=-1.0 / 16.0), r=[bA.trk()], w=[eGT[i3].trk()])
                    yield
                    kb.op(A_, lambda e: e.activation(out=enGT[i3][:], in_=pgt[0:64, 0:128], func=AF.Exp, scale=1.0 / 16.0), r=[bA.trk()], w=[enGT[i3].trk()])
                    yield
                    kb.op(A_, lambda e: e.activation(out=eD[i3][:], in_=pd[:, 0:64], func=AF.Exp, scale=-1.0 / 16.0), r=[bA.trk()], w=[eD[i3].trk()])
                    yield

                def prod_gen(d, i_):
                    tri = m_le if d == 0 else m_ge
                    b = i_ if d == 0 else NB - 1 - i_
                    i3 = d * 3 + i_ % 3
                    i2 = d * 2 + i_ % 2
                    bsl = slice(b * 128, (b + 1) * 128)
                    pat = SubView(PS[4 + d], 0)
                    kb.op(V, lambda e: e.tensor_tensor(out=qt[i2][:], in0=bqT[:, bsl], in1=eGT[i3][:], op=ALU.mult), r=[bqT.trk(), eGT[i3].trk()], w=[qt[i2].trk()])
                    yield
                    kb.op(P_, lambda e: e.tensor_tensor(out=kt[i2][:], in0=bkT[:, bsl], in1=enGT[i3][:], op=ALU.mult), r=[bkT.trk(), enGT[i3].trk()], w=[kt[i2].trk()])
                    yield
                    kb.op(P_, lambda e: e.tensor_tensor(out=kp[i2][:], in0=bkt[:, b, :], in1=eD[i3][:], op=ALU.mult), r=[bkt.trk(), eD[i3].trk()], w=[kp[i2].trk()])
                    yield
                    kb.op("pe", lambda e: e.matmul(pat[:, 0:128], lhsT=kt[i2][:], rhs=qt[i2][:], start=True, stop=True), r=[kt[i2].trk(), qt[i2].trk()], w=[pat.trk()])
                    yield
                    kb.op(V, lambda e: e.tensor_tensor(out=am[i2][:], in0=pat[:, 0:128], in1=tri[:], op=ALU.mult), r=[pat.trk(), CONST], w=[am[i2].trk()])
                    yield

                def fin_gen(d, i_):
                    edge = 127 if d == 0 else 0
                    S, Sb = Sd[d], Sbd[d]
                    b = i_ if d == 0 else NB - 1 - i_
                    first_touch = i_ < NB // 2
                    i2 = d * 2 + i_ % 2
                    i3 = d * 3 + i_ % 3
                    bsl = slice(b * 128, (b + 1) * 128)
                    bC = PS[6 + d]
                    pS, po = SubView(bC, 0), SubView(bC, 128)
                    kb.op("pe", lambda e: [e.matmul(po[:, 0:128], lhsT=bvt[:, b, :], rhs=am[i2][:], start=True, stop=False),
                                          e.matmul(po[:, 0:128], lhsT=Sb[:], rhs=qt[i2][:], start=False, stop=True),
                                          e.matmul(pS[0:64, 0:128], lhsT=kp[i2][:], rhs=bvt[:, b, :], start=True, stop=True)][-1],
                          r=[bvt.trk(), am[i2].trk(), Sb.trk(), qt[i2].trk(), kp[i2].trk()], w=[bC.trk()])
                    yield
                    kb.op(V, lambda e: e.scalar_tensor_tensor(out=S[:], in0=S[:], scalar=eGT[i3][:, edge:edge + 1], in1=pS[0:64, 0:128], op0=ALU.mult, op1=ALU.add),
                          r=[S.trk(), eGT[i3].trk(), bC.trk(), Sb.trk()], w=[S.trk()])
                    yield
                    seq_end = (b % 2 == 1) if d == 0 else (b % 2 == 0)
                    if seq_end:
                        sg_ = stg[d]
                        nst_[0] += 1
                        kb.op(P_, lambda e: e.tensor_copy(out=sg_[:], in_=S[:]), r=[S.trk()], w=[sg_.trk()])
                        yield
                        kb.dma("sp", dr["newgla"][l, b // 2, d, h], sg_[:], sg_.trk(), r=[sg_.trk()])
                        yield
                        kb.op(V, lambda e: e.tensor_scalar(out=S[:], in0=S[:], scalar1=flg[0:64, 0:1], scalar2=None, op0=ALU.mult), r=[S.trk(), CONST], w=[S.trk()])
                        yield
                    kb.op(A_, lambda e: e.activation(out=Sb[:], in_=S[:], func=AF.Copy), r=[S.trk()], w=[Sb.trk()])
                    yield
                    if first_touch:
                        kb.op(V, lambda e: e.tensor_copy(out=obuf[:, bsl], in_=po[:, 0:128]), r=[bC.trk()], w=[OBF[b]])
                    else:
                        kb.op(V, lambda e: e.tensor_tensor(out=obuf[:, bsl], in0=obuf[:, bsl], in1=po[:, 0:128], op=ALU.add), r=[bC.trk(), OBF[b]], w=[OBF[b]])
                    yield

                for i_ in range(NB + 2):
                    gens = []
                    if i_ >= 2:
                        gens += [fin_gen(0, i_ - 2), fin_gen(1, i_ - 2)]
                    if 1 <= i_ <= NB:
                        gens += [prod_gen(0, i_ - 1), prod_gen(1, i_ - 1)]
                    if i_ < NB:
                        gens += [la_gen(0, i_), la_gen(1, i_)]
                    while gens:
                        for g_ in list(gens):
                            try:
                                next(g_)
                            except StopIteration:
                                gens.remove(g_)
                kb.barrier()
            with ExitStack() as nt_:
                nsq = sb(nt_, "gnsq", [128, 512])
                nln = sb(nt_, "gnln", [128, 512])
                nrs = sb(nt_, "gnrs", [128, 512])
                ntm = sb(nt_, "gntm", [128, 512])
                for n in range(NT):
                    tsl = slice(n * 512, (n + 1) * 512)
                    ob_tr = [OBF[n * 4 + j] for j in range(4)]
                    ps = PS[n % 2]
                    kb.op(A_, lambda e, tsl=tsl: e.activation(out=nsq[:], in_=obuf[:, tsl], func=AF.Square), r=ob_tr, w=[nsq.trk()])
                    kb.op("pe", lambda e, ps=ps: e.matmul(ps[:], lhsT=ones_f[:], rhs=nsq[:], start=True, stop=True), r=[nsq.trk(), CONST], w=[ps.trk()])
                    rstd_from_ps((ps[:], ps.trk()), (nln[:], nln.trk(), 128), (nrs[:], nrs.trk()), 1.0 / 128.0)
                    kb.op(V, lambda e, tsl=tsl: e.scalar_tensor_tensor(out=ntm[:], in0=obuf[:, tsl], scalar=par[:, l, 84:85], in1=nrs[:], op0=ALU.mult, op1=ALU.mult),
                          r=ob_tr + [nrs.trk(), PAR], w=[ntm.trk()])
                    kb.op(P_, lambda e, tsl=tsl, h=h: e.tensor_tensor(out=oT[:, h, tsl], in0=ntm[:], in1=sgT[:, tsl], op=ALU.mult), r=[ntm.trk(), sgT.trk()], w=[OT[h][n]])
                kb.barrier()
        kb.barrier()
    dump("ob%d" % l, oT[:].rearrange("p c t -> p (c t)"), [128, 4 * T], OT[3][3], BF16)


def attn_phase(nc, kb, sb, dr, PS, l, hT, HT, oT, OT, par, PAR, cst, CONST, maskb, ident_f, ones_f, ones_b, blk64, rrot, make_rope,
               WIN, rstd_from_ps, proj_fm, load_cols, dump):
    V, P_, A_ = "dve", "pool", "act"
    with ExitStack() as st:
        ropec, ropes, RT = make_rope(st)
        wq = sb(st, "awq", [128, 8, 128], BF16)
        wk = sb(st, "awk", [128, 8, 128], BF16)
        wv = sb(st, "awv", [128, 8, 128], BF16)
        qP = [sb(st, "aqP", [128, T], BF16) for _ in range(2)]
        kP = [sb(st, "akP", [128, 256 + T], BF16) for _ in range(2)]
        ZQ = Trk()
        kb.op(P_, lambda e: e.memset(qP[0][64:128, :], 0.0), w=[ZQ])
        kb.op(P_, lambda e: e.memset(qP[1][0:64, :], 0.0), w=[ZQ])
        kb.op(P_, lambda e: e.memset(kP[0][64:128, :], 0.0), w=[ZQ])
        kb.op(P_, lambda e: e.memset(kP[1][0:64, :], 0.0), w=[ZQ])
        kb.dma("pool", qP[0][64:72, :], dr["mq"], ZQ, w=[ZQ])
        kb.dma("pool", qP[1][0:8, :], dr["mq"], ZQ, w=[ZQ])
        kb.dma("pool", kP[0][64:72, :], dr["mk"], ZQ, w=[ZQ])
        kb.dma("pool", kP[1][0:8, :], dr["mk"], ZQ, w=[ZQ])
        vt = sb(st, "avt", [128, 18, 128], BF16)
        ckst = sb(st, "ackst", [128, 2, 128])
        sq = sb(st, "asq", [128, 512])
        sqb = sb(st, "asqb", [128, 512], BF16)
        lnv = sb(st, "aln", [128, 512])
        lnv2 = sb(st, "aln2", [128, 512])
        rstd = sb(st, "ars", [128, 512])
        qn = sb(st, "aqn", [128, 512])
        t1 = lnv
        pT = [sb(st, "apT", [128, 512], BF16) for _ in range(4)]

        acc = sb(st, "aacc", [128, 512])
        rs = sq
        tmpo = qn
        kst = [sb(st, "akst", [128, 512]) for _ in range(1)]
        vst = kst
        QT = [Trk() for n in range(NT)]
        KT = [Trk() for n in range(NT + 1)]
        VT = [vt.trk(g) for g in range(5)]
        nst = 0
        npt = 0
        for h in range(4):
            load_cols(wq, WIN[l], C_AQ + h * 128, 128)
            load_cols(wk, WIN[l], C_AK + h * 128, 128)
            load_cols(wv, WIN[l], C_AV + h * 128, 128)
            kb.dma("sp", ckst[:], dr["ctxk"][l, :, h * 128:(h + 1) * 128].rearrange("(b p) f -> p b f", p=128), ckst.trk(), w=[ckst.trk()])
            pc = PS[2]
            kb.op("pe", lambda e: [e.transpose(out=pc[:, b * 128:(b + 1) * 128], in_=ckst[:, b, :], identity=ident_f[:]) for b in range(2)][-1],
                  r=[ckst.trk(), CONST], w=[pc.trk()])
            kb.op(V, lambda e: e.tensor_copy(out=kP[0][0:64, 0:256], in_=pc[0:64, 0:256]), r=[pc.trk(), ZQ], w=[KT[0]])
            kb.op(V, lambda e: e.tensor_copy(out=kP[1][64:128, 0:256], in_=pc[64:128, 0:256]), r=[pc.trk(), ZQ], w=[KT[0]])
            kb.dma("pool", vt[:, 0:2, :], dr["ctxv"][l, :, h * 128:(h + 1) * 128].rearrange("(b p) f -> p b f", p=128), VT[0], w=[VT[0]])
            items = [(is_k, n) for is_k in (False, True) for n in range(NT)]
            psb = [PS[0], PS[4]]
            sqbb = [sqb, pT[0]]
            lnvb = [lnv, lnv2]
            rstb = [rstd, acc]

            def st1(i):
                is_k, n = items[i]
                w_ = wk if is_k else wq
                ps, pss = psb[i % 2], PS[1]
                sb_, ln_, rs_ = sqbb[i % 2], lnvb[i % 2], rstb[i % 2]
                proj_fm(ps, w_, 0, 128, n)
                yield
                kb.op(A_, lambda e: e.activation(out=sb_[:], in_=ps[:], func=AF.Square), r=[ps.trk()], w=[sb_.trk()])
                yield
                kb.op("pe", lambda e: e.matmul(pss[:], lhsT=blk64[:], rhs=sb_[:], start=True, stop=True), r=[sb_.trk(), CONST], w=[pss.trk()])
                yield
                kb.op(A_, lambda e: e.activation(out=ln_[:], in_=pss[:], func=AF.Ln, scale=1.0 / 64.0, bias=cst[:, 0:1]), r=[pss.trk(), CONST], w=[ln_.trk()])
                yield
                kb.op(A_, lambda e: e.activation(out=rs_[:], in_=ln_[:], func=AF.Exp, scale=-0.5), r=[ln_.trk()], w=[rs_.trk()])
                yield

            def st2(i):
                is_k, n = items[i]
                gcol = 81 if is_k else 80
                tsl = slice(n * 512, (n + 1) * 512)
                ps, prq = psb[i % 2], PS[3]
                rs_, t1_ = rstb[i % 2], lnvb[i % 2]
                kb.op(V, lambda e: e.scalar_tensor_tensor(out=qn[:], in0=ps[:], scalar=par[:, l, gcol:gcol + 1], in1=rs_[:], op0=ALU.mult, op1=ALU.mult),
                      r=[ps.trk(), rs_.trk(), PAR], w=[qn.trk()])
                yield
                if is_k:
                    ko = kst[0]
                    pk = PS[2]
                    kb.op("pe", lambda e: [e.transpose(out=pk[:, j * 128:(j + 1) * 128], in_=qn[:, j * 128:(j + 1) * 128], identity=ident_f[:]) for j in range(4)][-1],
                          r=[qn.trk(), CONST], w=[pk.trk()])
                    yield
                    kb.op(A_, lambda e: e.activation(out=ko[:], in_=pk[:], func=AF.Copy), r=[pk.trk()], w=[ko.trk()])
                    yield
                    kb.dma("sp", dr["newk"][l, n * 512:(n + 1) * 512, h * 128:(h + 1) * 128].rearrange("(j p) f -> p j f", p=128),
                           ko[:].rearrange("p (j f) -> p j f", f=128), ko.trk(), r=[ko.trk()])
                    yield
                kb.op("pe", lambda e: e.matmul(prq[:], lhsT=rrot[:], rhs=qn[:], start=True, stop=True), r=[qn.trk(), CONST], w=[prq.trk()])
                yield
                kb.op(P_, lambda e: e.tensor_tensor(out=t1_[:], in0=qn[:], in1=ropec[:, tsl], op=ALU.mult), r=[qn.trk(), RT], w=[t1_.trk()])
                yield
                kb.op(V, lambda e: e.tensor_tensor(out=sq[:], in0=prq[:], in1=ropes[:, tsl], op=ALU.mult), r=[prq.trk(), RT], w=[sq.trk()])
                yield
                if is_k:
                    kb.op(V, lambda e: e.tensor_tensor(out=kP[0][0:64, 256 + n * 512:256 + (n + 1) * 512], in0=t1_[0:64, :], in1=sq[0:64, :], op=ALU.add), r=[t1_.trk(), sq.trk(), ZQ], w=[KT[n + 1]])
                    yield
                    kb.op(V, lambda e: e.tensor_tensor(out=kP[1][64:128, 256 + n * 512:256 + (n + 1) * 512], in0=t1_[64:128, :], in1=sq[64:128, :], op=ALU.add), r=[t1_.trk(), sq.trk(), ZQ], w=[KT[n + 1]])
                else:
                    kb.op(V, lambda e: e.tensor_tensor(out=qP[0][0:64, tsl], in0=t1_[0:64, :], in1=sq[0:64, :], op=ALU.add), r=[t1_.trk(), sq.trk(), ZQ], w=[QT[n]])
                    yield
                    kb.op(V, lambda e: e.tensor_tensor(out=qP[1][64:128, tsl], in0=t1_[64:128, :], in1=sq[64:128, :], op=ALU.add), r=[t1_.trk(), sq.trk(), ZQ], w=[QT[n]])
                yield

            for i in range(len(items) + 1):
                gens = []
                if i >= 1:
                    gens.append(st2(i - 1))
                if i < len(items):
                    gens.append(st1(i))
                while gens:
                    for g_ in list(gens):
                        try:
                            next(g_)
                        except StopIteration:
                            gens.remove(g_)
            for g4 in range(4):
                pv = PS[4]

                def fv(e, g4=g4, pv=pv):
                    ins = None
                    for j in range(4):
                        blk = g4 * 4 + j
                        for kc in range(8):
                            ins = e.matmul(pv[:, j * 128:(j + 1) * 128], lhsT=hT[:, kc, blk * 128:(blk + 1) * 128], rhs=wv[:, kc, :], start=(kc == 0), stop=(kc == 7))
                    return ins
                kb.op("pe", fv, r=[wv.trk()] + [HT[c][g4] for c in range(8)], w=[pv.trk()])
                kb.op(A_, lambda e, g4=g4, pv=pv: e.activation(out=vt[:, 2 + g4 * 4:2 + (g4 + 1) * 4, :], in_=pv[:].rearrange("p (j f) -> p j f", f=128), func=AF.Copy),
                      r=[pv.trk()], w=[VT[g4 + 1]])
                vo = vst[0]
                kb.op(V, lambda e, vo=vo, pv=pv: e.tensor_copy(out=vo[:], in_=pv[:]), r=[pv.trk()], w=[vo.trk()])
                kb.dma("sp", dr["newv"][l, g4 * 512:(g4 + 1) * 512, h * 128:(h + 1) * 128].rearrange("(j p) f -> p j f", p=128),
                       vo[:].rearrange("p (j f) -> p j f", f=128), vo.trk(), r=[vo.trk()])
            units = [(qt, m, u) for qt in range(NT) for m in range(2) for u in range(9)]

            def emit_S(i):
                qt, m, u = units[i]
                qsl = slice(qt * 512, (qt + 1) * 512)
                banks = [PS[(2 * i) % 4], PS[(2 * i + 1) % 4]]
                ktrs = list({KT[0] if kc < 2 else KT[1 + (kc - 2) // 4] for kc in (2 * u, 2 * u + 1)})

                def f(e):
                    ins = None
                    for j in range(2):
                        kc = 2 * u + j
                        ins = e.matmul(banks[j][:], lhsT=kP[m][:, kc * 128:(kc + 1) * 128], rhs=qP[m][:, qsl], start=True, stop=True)
                    return ins
                kb.op("pe", f, r=ktrs + [QT[qt], ZQ], w=[banks[0].trk(), banks[1].trk()])
                for j in range(2):
                    p_ = pT[(2 * i + j) % 4]
                    kb.op(A_, lambda e, p_=p_, j=j: e.activation(out=p_[:], in_=banks[j][:], func=AF.Exp, scale=0.125), r=[banks[j].trk()], w=[p_.trk()])

            def emit_PV(i):
                qt, m, u = units[i]
                qsl = slice(qt * 512, (qt + 1) * 512)
                g = (qt * 2 + m) % 2
                po, psm = PS[4 + 2 * g], PS[5 + 2 * g]
                ps_ = [pT[(2 * i) % 4], pT[(2 * i + 1) % 4]]
                vtrs = list({VT[0] if kc < 2 else VT[1 + (kc - 2) // 4] for kc in (2 * u, 2 * u + 1)})

                def f(e):
                    ins = None
                    for j in range(2):
                        kc = 2 * u + j
                        e.matmul(po[:], lhsT=vt[:, kc, :], rhs=ps_[j][:], start=(kc == 0), stop=(kc == 17))
                        ins = e.matmul(psm[:], lhsT=ones_b[:], rhs=ps_[j][:], start=(kc == 0), stop=(kc == 17))
                    return ins
                kb.op("pe", f, r=[ps_[0].trk(), ps_[1].trk(), CONST] + vtrs, w=[po.trk(), psm.trk()])
                if u != 8:
                    return
                kb.op(V, lambda e: e.reciprocal(out=rs[:], in_=psm[:]), r=[psm.trk()], w=[rs.trk()])
                if m == 0:
                    kb.op(V, lambda e: e.tensor_tensor(out=acc[:], in0=po[:], in1=rs[:], op=ALU.mult), r=[po.trk(), rs.trk()], w=[acc.trk()])
                    return
                kb.op(V, lambda e: e.tensor_tensor(out=tmpo[:], in0=po[:], in1=rs[:], op=ALU.mult), r=[po.trk(), rs.trk()], w=[tmpo.trk()])
                kb.op(V, lambda e: e.scalar_tensor_tensor(out=acc[:], in0=tmpo[:], scalar=par[:, l, 83:84], in1=acc[:], op0=ALU.mult, op1=ALU.add),
                      r=[tmpo.trk(), PAR], w=[acc.trk()])
                pn = PS[4 + 2 * g]
                kb.op(A_, lambda e: e.activation(out=sq[:], in_=acc[:], func=AF.Square), r=[acc.trk()], w=[sq.trk()])
                kb.op("pe", lambda e: e.matmul(pn[:], lhsT=ones_f[:], rhs=sq[:], start=True, stop=True), r=[sq.trk(), CONST], w=[pn.trk()])
                rstd_from_ps((pn[:], pn.trk()), (lnv[:], lnv.trk(), 128), (rstd[:], rstd.trk()), 1.0 / 128.0)
                kb.op(V, lambda e: e.scalar_tensor_tensor(out=oT[:, h, qsl], in0=acc[:], scalar=par[:, l, 82:83], in1=rstd[:], op0=ALU.mult, op1=ALU.mult),
                      r=[acc.trk(), rstd.trk(), PAR], w=[OT[h][qt]])

            LA = 1
            for i in range(len(units) + LA):
                if i < len(units):
                    emit_S(i)
                if i - LA >= 0:
                    emit_PV(i - LA)
        kb.barrier()
    dump("oa%d" % l, oT[:].rearrange("p c t -> p (c t)"), [128, 4 * T], OT[3][3], BF16)


_CACHE = {}


def make_in_maps(inputs):
    f32 = np.float32
    xs = np.ascontiguousarray(inputs["x_sample"], dtype=f32)
    xp = np.ascontiguousarray(inputs["x_prompt"], dtype=f32)
    maps = []
    wts = {n: np.ascontiguousarray(inputs[n], dtype=f32) for n, _ in W_SPECS}
    for core in range(8):
        m = dict(wts)
        if core < 4:
            b = core
            m["xin"] = xs[b]
            m["cond"] = np.ascontiguousarray(inputs["c"][b], dtype=f32)
            m["ctxk"] = np.ascontiguousarray(inputs["cache_diff_k"][b], dtype=f32).reshape(2, 256, 512)
            m["ctxv"] = np.ascontiguousarray(inputs["cache_diff_v"][b], dtype=f32).reshape(2, 256, 512)
            m["gla0"] = np.ascontiguousarray(inputs["state_gla"][b], dtype=f32)
            m["s5h0"] = np.ascontiguousarray(inputs["state_s5"][b], dtype=f32)
            m["flags"] = np.ones((128, 2), f32)
            m["mq"] = np.zeros((8, 2048), f32)
            m["mk"] = np.zeros((8, 2304), f32)
        else:
            q0 = (core - 4) * 8
            m["xin"] = xp[q0:q0 + 8].reshape(2048, 1024)
            m["cond"] = np.ascontiguousarray(inputs["c_ctx"], dtype=f32)
            m["ctxk"] = np.zeros((2, 256, 512), f32)
            m["ctxv"] = np.zeros((2, 256, 512), f32)
            m["gla0"] = np.zeros((2, 2, 4, 64, 128), f32)
            m["s5h0"] = np.zeros((2, 2, 2, 32, 64), f32)
            m["flags"] = np.zeros((128, 2), f32)
            mq = np.zeros((8, 2048), f32)
            mk = np.full((8, 2304), NEG * 8.0, f32)
            for j in range(8):
                mq[j, j * 256:(j + 1) * 256] = 1.0
                mk[j, 256 + j * 256:256 + (j + 1) * 256] = 0.0
            m["mq"] = mq
            m["mk"] = mk
        maps.append(m)
    return maps


def kernel(**inputs):
    if "nc" not in _CACHE:
        waited = build_program(want_waited=True)
        _CACHE["nc"] = build_program(needed=waited)[0]
    nc = _CACHE["nc"]
    maps = make_in_maps(inputs)
    res = run_bass_kernel_spmd(nc, maps, core_ids=list(range(8))).results
    f32 = np.float32
    y_sample = np.stack([res[b]["y"] for b in range(4)]).astype(f32)
    y_prompt = np.concatenate([res[c]["y"].reshape(8, 256, 1024) for c in range(4, 8)]).astype(f32)
    nk = np.concatenate([res[c]["newk"].reshape(2, 8, 256, 4, 2, 64).transpose(1, 0, 2, 3, 4, 5) for c in range(4, 8)]).astype(f32)
    nv = np.concatenate([res[c]["newv"].reshape(2, 8, 256, 4, 128).transpose(1, 0, 2, 3, 4) for c in range(4, 8)]).astype(f32)
    ng = np.concatenate([res[c]["newgla"].transpose(1, 0, 2, 3, 4, 5) for c in range(4, 8)]).astype(f32)
    n5 = np.concatenate([res[c]["news5"].transpose(1, 0, 2, 3, 4, 5) for c in range(4, 8)]).astype(f32)
    return (y_prompt, y_sample, nk, nv, ng, n5)
```

```python
import math
from contextlib import ExitStack

import numpy as np
import concourse.bass as bass
import concourse.mybir as mybir
from concourse.bass_utils import run_bass_kernel_spmd

F32 = mybir.dt.float32
BF16 = mybir.dt.bfloat16
I32 = mybir.dt.int32
ALU = mybir.AluOpType
AF = mybir.ActivationFunctionType

D = 1024
T = 2048
NT = 4
NB = 16
DEPTH = 2
IN_DIM = 6688
FFN = 2816
NJ = FFN // 128
C_AQ, C_AK, C_AV, C_BQ, C_BK, C_BV, C_BG, C_BR, C_CU, C_GZ = 0, 512, 1024, 1536, 1792, 2048, 2560, 3072, 3104, 3616
EPS = 1e-6
TWO_PI_LO = 6.28318
NEG = -30000.0

W_SPECS = [
    ("w_mod", (2, 1024, 6144)), ("b_mod", (2, 6144)), ("norm1_g", (2, 1024)), ("norm2_g", (2, 1024)),
    ("w_in", (2, 1024, 6688)), ("diff_qn_g", (2, 64)), ("diff_kn_g", (2, 64)), ("diff_lam", (2, 4, 64)),
    ("diff_subln_g", (2, 128)), ("gla_wa2", (2, 2, 16, 256)), ("gla_ba", (2, 2, 256)), ("gla_on_g", (2, 128)),
    ("s5_lam_re", (2, 2, 32, 64)), ("s5_lam_im", (2, 2, 32, 64)), ("s5_log_dt", (2, 2, 32)),
    ("s5_b_re", (2, 2, 32, 64, 16)), ("s5_b_im", (2, 2, 32, 64, 16)), ("s5_c_re", (2, 2, 32, 16, 64)),
    ("s5_c_im", (2, 2, 32, 16, 64)), ("s5_d", (2, 512)), ("s5_w_glu", (2, 512, 1024)), ("s5_b_glu", (2, 1024)),
    ("w_branch", (2, 3, 512, 1024)), ("w_out", (2, 1024, 1024)), ("w_ffn_gate", (2, 1024, 2816)),
    ("w_ffn_up", (2, 1024, 2816)), ("w_ffn_down", (2, 2816, 1024)),
]
IN_SPECS = [
    ("xin", (2048, 1024)), ("cond", (1024,)), ("ctxk", (2, 256, 512)), ("ctxv", (2, 256, 512)),
    ("gla0", (2, 2, 4, 64, 128)), ("s5h0", (2, 2, 2, 32, 64)), ("flags", (128, 2)), ("mq", (8, 2048)), ("mk", (8, 2304)),
]
OUT_SPECS = [
    ("y", (2048, 1024)), ("newk", (2, 2048, 512)), ("newv", (2, 2048, 512)),
    ("newgla", (2, 8, 2, 4, 64, 128)), ("news5", (2, 8, 2, 2, 32, 64)),
]


S5STOP = [None]


class _Stop(Exception):
    pass


def chk(tag):
    if S5STOP[0] == tag:
        raise _Stop()


class Trk:
    __slots__ = ("w", "r", "dsem", "x")

    def __init__(self):
        self.w = None
        self.r = {}
        self.dsem = None
        self.x = False


class KB:
    def _emit_wait(self, e, k, v):
        if k in self.dtot:
            self.engs[e].wait_ge(self.sem[k], v)
            return
        self.waited.add((k, v))
        if self.needed is None:
            self.engs[e].wait_ge(self.sem[k], v)
        else:
            self.engs[e].wait_ge(self.sem[k], self.vmap[k][v])

    def __init__(self, nc, es, needed=None):
        self.nc = nc
        self.es = es
        self.needed = needed
        self.waited = set()
        self.incs = {}
        self.vmap = {}
        self.engs = {"pe": nc.tensor, "act": nc.scalar, "dve": nc.vector, "pool": nc.gpsimd, "sp": nc.sync}
        self.sem = {}
        self.cnt = {}
        self.seen = {}
        for e in self.engs:
            self.sem[e] = es.enter_context(nc.semaphore("s_" + e))
            self.cnt[e] = 0
            self.seen[e] = {}
            self.incs[e] = 0
            self.vmap[e] = {}
        self.dtot = {}
        self.dfree = []
        self.dfree_sw = []
        self.swkeys = set()
        self.dassigned = []
        self.nds = 0
        self.uid = 0

    def name(self, p):
        self.uid += 1
        return "%s%d" % (p, self.uid)

    def _wait(self, e, r, w):
        need = {}
        for t in r:
            if t.w is not None:
                k, v = t.w
                if need.get(k, 0) < v:
                    need[k] = v
            if t.x:
                for k, v in t.r.items():
                    if k != e and need.get(k, 0) < v:
                        need[k] = v
        for t in w:
            if t.w is not None:
                k, v = t.w
                if need.get(k, 0) < v:
                    need[k] = v
            for k, v in t.r.items():
                if need.get(k, 0) < v:
                    need[k] = v
        seen = self.seen[e]
        for k, v in need.items():
            if k in self.dtot:
                v = self.dtot[k]
            if seen.get(k, 0) < v:
                self._emit_wait(e, k, v)
                seen[k] = v

    def _commit(self, ev, r, w):
        k, v = ev
        for t in r:
            if t.r.get(k, 0) < v:
                t.r[k] = v
        for t in w:
            t.w = ev
            t.r = {}

    def op(self, e, fn, r=(), w=()):
        self._wait(e, r, w)
        ins = fn(self.engs[e])
        self.cnt[e] += 1
        if self.needed is None or (e, self.cnt[e]) in self.needed:
            self.incs[e] += 1
            self.vmap[e][self.cnt[e]] = self.incs[e]
            ins.then_inc(self.sem[e], 1)
        self._commit((e, self.cnt[e]), r, w)

    def dma(self, q, out, in_, trk, r=(), w=(), **kw):
        self._wait(q, r, w)
        if trk.dsem is None:
            pool_ = self.dfree_sw if q == "pool" else self.dfree
            if pool_:
                key = pool_.pop()
            else:
                self.nds += 1
                key = "d%d" % self.nds
                self.sem[key] = self.es.enter_context(self.nc.semaphore("s_" + key))
                self.dtot[key] = 0
                if q == "pool":
                    self.swkeys.add(key)
            trk.dsem = key
            self.dassigned.append(trk)
        key = trk.dsem
        ins = self.engs[q].dma_start(out=out, in_=in_, **kw)
        ins.then_inc(self.sem[key], 16)
        self.dtot[key] += 16
        self._commit((key, self.dtot[key]), r, w)

    def barrier(self):
        for e in self.engs:
            for o in self.engs:
                if o != e and self.seen[e].get(o, 0) < self.cnt[o]:
                    self._emit_wait(e, o, self.cnt[o])
                    self.seen[e][o] = self.cnt[o]
            for k, v in self.dtot.items():
                if self.seen[e].get(k, 0) < v:
                    self.engs[e].wait_ge(self.sem[k], v)
                    self.seen[e][k] = v
        for t in self.dassigned:
            if t.dsem is not None:
                (self.dfree_sw if t.dsem in self.swkeys else self.dfree).append(t.dsem)
                t.dsem = None
        self.dassigned = []

    def final_wait(self):
        for k, v in self.dtot.items():
            if self.seen["sp"].get(k, 0) < v:
                self.nc.sync.wait_ge(self.sem[k], v)
        for o in self.engs:
            if o != "sp" and self.cnt[o] > 0:
                self._emit_wait("sp", o, self.cnt[o])


class Tile:
    def __init__(self, h):
        self.h = h
        self.t = {}

    def __getitem__(self, k):
        return self.h[k]

    def trk(self, key=0):
        t = self.t.get(key)
        if t is None:
            t = self.t[key] = Trk()
        return t


class SubView:
    def __init__(self, tile, c0):
        self.tile = tile
        self.c0 = c0

    def __getitem__(self, k):
        r, c = k
        return self.tile[r, slice(self.c0 + c.start, self.c0 + c.stop)]

    def trk(self):
        return self.tile.trk()


def build_program(stop_after=None, dbg_names=(), needed=None, want_waited=False):
    nc = bass.Bass("TRN2", target_bir_lowering=False)
    es = ExitStack()
    kb = KB(nc, es, needed)
    dr = {}
    for n, s in IN_SPECS + W_SPECS:
        dr[n] = nc.dram_tensor(n, list(s), F32, kind="ExternalInput").ap()
    for n, s in OUT_SPECS:
        dr[n] = nc.dram_tensor(n, list(s), F32, kind="ExternalOutput").ap()
    dbg_out = {}

    def sb(stack, name, shape, dt=F32):
        return Tile(stack.enter_context(nc.sbuf_tensor(kb.name(name), list(shape), dt)))

    PS = [Tile(es.enter_context(nc.psum_tensor("ps%d" % i, [128, 512], F32))) for i in range(8)]
    for t_ in PS:
        t_.trk().x = True

    def dump(name, tile_ap, shape, trk, dt=F32):
        if name not in dbg_names:
            return
        o = nc.dram_tensor("dbg_" + name, list(shape), dt, kind="ExternalOutput").ap()
        dbg_out[name] = shape
        kb.dma("sp", o, tile_ap, trk, r=[trk])
        kb.barrier()

    dump.names = dbg_names

    xT = sb(es, "xT", [128, 8, T])
    hT = sb(es, "hT", [128, 8, T], BF16)
    ident_f = sb(es, "identf", [128, 128])
    ones_f = sb(es, "onesf", [128, 128])
    ones_b = sb(es, "onesb", [128, 128], BF16)
    blk64 = sb(es, "blk64", [128, 128])
    m_le = sb(es, "mle", [128, 128])
    m_ge = sb(es, "mge", [128, 128])
    m_gt = sb(es, "mgt", [128, 128])
    m_lt = sb(es, "mlt", [128, 128])
    rrot = sb(es, "rrot", [128, 128])
    m_le_b = sb(es, "mleb", [128, 128], BF16)
    m_ge_b = sb(es, "mgeb", [128, 128], BF16)
    m_gt_b = sb(es, "mgtb", [128, 128], BF16)
    m_lt_b = sb(es, "mltb", [128, 128], BF16)
    blk64_b = sb(es, "blk64b", [128, 128], BF16)
    rrot_b = sb(es, "rrotb", [128, 128], BF16)
    mtz_f = sb(es, "mtzf", [128, 128])
    mtz_b = sb(es, "mtzb", [128, 128])
    oT = sb(es, "oT", [128, 4, T], BF16)
    OT = [[oT.trk((c, n)) for n in range(NT)] for c in range(4)]
    cst = sb(es, "cst", [128, 8])
    flg = sb(es, "flg", [128, 2])
    par = sb(es, "par", [128, DEPTH, 160])
    CONST = Trk()

    V, P_, A_ = "dve", "pool", "act"

    kb.op(P_, lambda e: e.memset(ident_f[:], 0.0), w=[CONST])
    kb.op(P_, lambda e: e.affine_select(out=ident_f[:], in_=ident_f[:], pattern=[[-1, 128]], compare_op=ALU.not_equal,
                                        fill=1.0, base=0, channel_multiplier=1), w=[CONST])
    kb.op(P_, lambda e: e.memset(ones_f[:], 1.0), w=[CONST])
    kb.op(V, lambda e: e.tensor_copy(out=ones_b[:], in_=ones_f[:]), r=[CONST], w=[CONST])
    for (mt, cmp, sgn) in ((m_le, ALU.is_ge, -1), (m_ge, ALU.is_ge, 1), (m_gt, ALU.is_gt, 1), (m_lt, ALU.is_gt, -1)):
        kb.op(P_, lambda e, mt=mt, cmp=cmp, sgn=sgn: e.affine_select(out=mt[:], in_=ones_f[:], pattern=[[-sgn, 128]], compare_op=cmp,
                                                                     fill=0.0, base=0, channel_multiplier=sgn), r=[CONST], w=[CONST])
    kb.op(P_, lambda e: e.memset(blk64[:], 0.0), w=[CONST])
    kb.op(P_, lambda e: e.memset(blk64[0:64, 0:64], 1.0), w=[CONST])
    kb.op(P_, lambda e: e.memset(blk64[64:128, 64:128], 1.0), w=[CONST])
    kb.op(P_, lambda e: e.memset(mtz_f[:], 0.0), w=[CONST])
    kb.op(P_, lambda e: e.memset(mtz_b[:], 0.0), w=[CONST])
    for s in range(4):
        kb.op(P_, lambda e, s=s: e.memset(mtz_f[32 * s:32 * s + 32, 32 * s:128], 1.0), w=[CONST])
        kb.op(P_, lambda e, s=s: e.memset(mtz_b[32 * s:32 * s + 32, 0:32 * s + 32], 1.0), w=[CONST])
    rv = rrot[:].rearrange("p (b h i) -> p b h i", h=2, i=16)
    iv = ident_f[:].rearrange("p (b h i) -> p b h i", h=2, i=16)
    kb.op(V, lambda e: e.tensor_scalar(out=rv[:, :, 0, :], in0=iv[:, :, 1, :], scalar1=-1.0, scalar2=None, op0=ALU.mult),
          r=[CONST], w=[CONST])
    kb.op(V, lambda e: e.tensor_copy(out=rv[:, :, 1, :], in_=iv[:, :, 0, :]), r=[CONST], w=[CONST])
    for (src_, dst_) in ((m_le, m_le_b), (m_ge, m_ge_b), (m_gt, m_gt_b), (m_lt, m_lt_b), (blk64, blk64_b), (rrot, rrot_b)):
        kb.op(V, lambda e, src_=src_, dst_=dst_: e.tensor_copy(out=dst_[:], in_=src_[:]), r=[CONST], w=[CONST])
    kb.op(P_, lambda e: e.memset(cst[:, 0:1], EPS), w=[CONST])
    kb.op(P_, lambda e: e.memset(cst[:, 1:2], 1.0), w=[CONST])
    kb.op(P_, lambda e: e.memset(cst[:, 2:3], 0.25), w=[CONST])
    kb.op(P_, lambda e: e.memset(cst[:, 3:4], 0.0), w=[CONST])
    kb.dma("sp", flg[:], dr["flags"], CONST, w=[CONST])

    def sincos_tmps(stack, shape):
        return (sb(stack, "sc_i", shape, I32), sb(stack, "sc_f", shape), sb(stack, "sc_q", shape), Trk())

    def sincos(tmps, out_c, out_s, turns, r, w):
        ti, tf, tq, tl = tmps
        kb.op(V, lambda e: e.tensor_copy(out=ti, in_=turns), r=r, w=tl)
        kb.op(V, lambda e: e.tensor_tensor(out=tf, in0=turns, in1=ti, op=ALU.subtract), r=list(r), w=tl)
        kb.op(A_, lambda e: e.activation(out=out_s, in_=tf, func=AF.Sin, scale=TWO_PI_LO), r=tl, w=w)
        kb.op(V, lambda e: e.tensor_scalar(out=tq, in0=turns, scalar1=0.25, scalar2=None, op0=ALU.add), r=r, w=tl)
        kb.op(V, lambda e: e.tensor_copy(out=ti, in_=tq), r=[], w=tl)
        kb.op(V, lambda e: e.tensor_tensor(out=tf, in0=tq, in1=ti, op=ALU.subtract), r=[], w=tl)
        kb.op(A_, lambda e: e.activation(out=out_c, in_=tf, func=AF.Sin, scale=TWO_PI_LO), r=tl, w=w)

    def make_rope(stack):
        ropec = sb(stack, "ropec", [128, T], BF16)
        ropes = sb(stack, "ropes", [128, T], BF16)
        RT = Trk()
        HF = T // 2
        with ExitStack() as st:
            pos_i = sb(st, "posi", [128, HF], I32)
            pos_f = sb(st, "posf", [128, HF])
            pid = sb(st, "pid", [128, 1], I32)
            pidf = sb(st, "pidf", [128, 1])
            invc = sb(st, "invc", [128, 1])
            tc_ = sb(st, "rc", [128, HF])
            ts_ = sb(st, "rs", [128, HF])
            tm = sincos_tmps(st, [128, HF])
            t1 = Trk()
            kb.op(P_, lambda e: e.iota(pid[:], pattern=[[0, 1]], base=0, channel_multiplier=1), w=[t1])
            kb.op(V, lambda e: e.tensor_single_scalar(out=pid[:], in_=pid[:], scalar=15, op=ALU.bitwise_and), r=[t1], w=[t1])
            kb.op(V, lambda e: e.tensor_copy(out=pidf[:], in_=pid[:]), r=[t1], w=[t1])
            kb.op(A_, lambda e: e.activation(out=invc[:], in_=pidf[:], func=AF.Exp, scale=-math.log(10000.0) / 16.0), r=[t1], w=[t1])
            kb.op(V, lambda e: e.tensor_scalar(out=invc[:], in0=invc[:], scalar1=flg[:, 1:2], scalar2=None, op0=ALU.mult),
                  r=[t1, CONST], w=[t1])
            kb.op(V, lambda e: e.tensor_scalar(out=invc[:], in0=invc[:], scalar1=1.0 / (2 * math.pi), scalar2=None, op0=ALU.mult), r=[t1], w=[t1])
            for hf in range(2):
                for b in range(4):
                    pat, base = ([[1, 16], [0, 64]], hf * 16) if b % 2 == 0 else ([[0, 16], [1, 64]], 0)
                    kb.op(P_, lambda e, b=b, pat=pat, base=base: e.iota(pos_i[32 * b:32 * b + 32, :].rearrange("p (r c) -> p r c", c=64), pattern=pat,
                                                                        base=base, channel_multiplier=0), w=[t1])
                kb.op(V, lambda e: e.tensor_copy(out=pos_f[:], in_=pos_i[:]), r=[t1], w=[t1])
                kb.op(V, lambda e: e.tensor_scalar(out=pos_f[:], in0=pos_f[:], scalar1=invc[:, 0:1], scalar2=None, op0=ALU.mult), r=[t1], w=[t1])
                t2 = Trk()
                sincos((tm[0][:], tm[1][:], tm[2][:], [tm[3]]), tc_[:], ts_[:], pos_f[:], [t1], [t2])
                kb.op(V, lambda e, hf=hf: e.tensor_copy(out=ropec[:, hf * HF:(hf + 1) * HF], in_=tc_[:]), r=[t2], w=[RT])
                kb.op(V, lambda e, hf=hf: e.tensor_copy(out=ropes[:, hf * HF:(hf + 1) * HF], in_=ts_[:]), r=[t2], w=[RT])
                kb.op(V, lambda e: e.tensor_copy(out=pidf[:], in_=pidf[:]), r=[t2, RT], w=[t1])
            kb.barrier()
        return ropec, ropes, RT

    XT = [[xT.trk((c, n)) for n in range(NT)] for c in range(8)]
    HT = [[hT.trk((c, n)) for n in range(NT)] for c in range(8)]
    with ExitStack() as st:
        xs = [sb(st, "xs", [128, 1024]) for _ in range(2)]
        for b in range(NB):
            s_ = xs[b % 2]
            kb.dma("sp", s_[:], dr["xin"][b * 128:(b + 1) * 128, :], s_.trk(), w=[s_.trk()])
            for half in range(2):
                ps = PS[(2 * b + half) % 4]
                kb.op("pe", lambda e, ps=ps, s_=s_, half=half: [e.transpose(out=ps[:, j * 128:(j + 1) * 128], in_=s_[:, (half * 4 + j) * 128:(half * 4 + j + 1) * 128],
                                                                             identity=ident_f[:]) for j in range(4)][-1],
                      r=[s_.trk(), CONST], w=[ps.trk()])
                eng = A_ if half == 0 else V
                outv = xT[:, half * 4:half * 4 + 4, b * 128:(b + 1) * 128]
                inv_ = ps[:].rearrange("p (j t) -> p j t", t=128)
                wl = [XT[half * 4 + j][b // 4] for j in range(4)]
                if eng == A_:
                    kb.op(A_, lambda e, outv=outv, inv_=inv_: e.activation(out=outv, in_=inv_, func=AF.Copy), r=[ps.trk()], w=wl)
                else:
                    kb.op(V, lambda e, outv=outv, inv_=inv_: e.tensor_copy(out=outv, in_=inv_), r=[ps.trk()], w=wl)
        kb.barrier()

    PAR = Trk()
    with ExitStack() as st:
        condT = sb(st, "condT", [128, 8])
        scond = sb(st, "scond", [128, 8], BF16)
        bmT = sb(st, "bmT", [128, 48])
        wm = [sb(st, "wm", [128, 8, 512], BF16) for _ in range(2)]
        tcnd = Trk()
        kb.dma("sp", condT[:], dr["cond"].rearrange("(c p) -> p c", p=128), tcnd, w=[tcnd], allow_slow_non_contiguous=True)
        kb.op(A_, lambda e: e.activation(out=scond[:], in_=condT[:], func=AF.Silu), r=[tcnd], w=[tcnd])
        for l in range(DEPTH):
            tb = Trk()
            kb.dma("sp", bmT[:], dr["b_mod"][l].rearrange("(c p) -> p c", p=128), tb, w=[tb], allow_slow_non_contiguous=True)
            psm = PS[4 + l]
            wv = dr["w_mod"][l].rearrange("(kc p) n -> p kc n", p=128)
            for cb in range(12):
                wt = wm[cb % 2]
                kb.dma("pool", wt[:], wv[:, :, cb * 512:(cb + 1) * 512], wt.trk(), w=[wt.trk()])

                def mm(e, wt=wt, cb=cb, psm=psm):
                    ins = None
                    for j in range(4):
                        col = cb * 4 + j
                        for kc in range(8):
                            ins = e.matmul(psm[:, col:col + 1], lhsT=wt[:, kc, j * 128:(j + 1) * 128], rhs=scond[:, kc:kc + 1],
                                           start=(kc == 0), stop=(kc == 7))
                    return ins
                kb.op("pe", mm, r=[wt.trk(), tcnd], w=[psm.trk()])
            kb.op(V, lambda e, l=l, psm=psm: e.tensor_tensor(out=par[:, l, 0:48], in0=psm[:, 0:48], in1=bmT[:], op=ALU.add),
                  r=[psm.trk(), tb], w=[PAR])
            tv = Trk()
            kb.dma("sp", par[:, l, 64:72], dr["norm1_g"][l].rearrange("(c p) -> p c", p=128), tv, w=[PAR], allow_slow_non_contiguous=True)
            kb.dma("sp", par[:, l, 72:80], dr["norm2_g"][l].rearrange("(c p) -> p c", p=128), tv, w=[PAR], allow_slow_non_contiguous=True)
            for mth in range(2):
                kb.dma("sp", par[64 * mth:64 * mth + 64, l, 80:81], dr["diff_qn_g"][l].rearrange("(p o) -> p o", o=1), tv, w=[PAR], allow_slow_non_contiguous=True)
                kb.dma("sp", par[64 * mth:64 * mth + 64, l, 81:82], dr["diff_kn_g"][l].rearrange("(p o) -> p o", o=1), tv, w=[PAR], allow_slow_non_contiguous=True)
            kb.dma("sp", par[:, l, 82:83], dr["diff_subln_g"][l].rearrange("(p o) -> p o", o=1), tv, w=[PAR], allow_slow_non_contiguous=True)
            kb.dma("sp", par[:, l, 84:85], dr["gla_on_g"][l].rearrange("(p o) -> p o", o=1), tv, w=[PAR], allow_slow_non_contiguous=True)
            kb.dma("sp", par[:, l, 88:96], dr["s5_b_glu"][l].rearrange("(c p) -> p c", p=128), tv, w=[PAR], allow_slow_non_contiguous=True)
            for s in range(4):
                kb.dma("sp", par[32 * s:32 * s + 32, l, 96:112], dr["s5_d"][l].rearrange("(gp jj) -> jj gp", jj=32), tv, w=[PAR], allow_slow_non_contiguous=True)
            lamt = sb(st, "lamt", [128, 256])
            lamp = sb(st, "lamp", [128, 128])
            lams = sb(st, "lams", [128, 2])
            kb.dma("sp", lamt[:], dr["diff_lam"][l].rearrange("a b -> (a b)").rearrange("(o n) -> o n", o=1).partition_broadcast(128), tv, w=[tv])
            lam_init = 0.8 - 0.6 * math.exp(-0.3 * l)
            kb.op(V, lambda e: e.tensor_tensor(out=lamp[:].rearrange("p (a b) -> p a b", b=64), in0=lamt[:].rearrange("p (a t b) -> p a t b", t=2, b=64)[:, :, 0, :],
                                               in1=lamt[:].rearrange("p (a t b) -> p a t b", t=2, b=64)[:, :, 1, :], op=ALU.mult), r=[tv], w=[tv])
            kb.op(V, lambda e: e.reduce_sum(out=lams[:], in_=lamp[:].rearrange("p (a b) -> p a b", b=64), axis=mybir.AxisListType.X), r=[tv], w=[tv])
            kb.op(A_, lambda e: e.activation(out=lams[:], in_=lams[:], func=AF.Exp), r=[tv], w=[tv])
            kb.op(V, lambda e, l=l, lam_init=lam_init: e.scalar_tensor_tensor(out=par[:, l, 83:84], in0=lams[:, 1:2], scalar=-lam_init, in1=lams[:, 0:1],
                                                                             op0=ALU.add, op1=ALU.subtract), r=[tv], w=[PAR])
            kb.op(V, lambda e, l=l, lam_init=lam_init: e.tensor_scalar(out=par[:, l, 82:83], in0=par[:, l, 82:83], scalar1=1.0 - lam_init, scalar2=None, op0=ALU.mult),
                  r=[tv, PAR], w=[PAR])
            kb.op(V, lambda e, l=l: e.scalar_tensor_tensor(out=par[:, l, 48:56], in0=par[:, l, 8:16], scalar=1.0, in1=par[:, l, 64:72], op0=ALU.add, op1=ALU.mult),
                  r=[PAR, tv], w=[PAR])
            kb.op(V, lambda e, l=l: e.scalar_tensor_tensor(out=par[:, l, 56:64], in0=par[:, l, 32:40], scalar=1.0, in1=par[:, l, 72:80], op0=ALU.add, op1=ALU.mult),
                  r=[PAR, tv], w=[PAR])
        kb.barrier()
    dump("par", par[:].rearrange("p l c -> p (l c)"), [128, DEPTH * 160], PAR)

    def rstd_from_ps(ps, tmp, out, scale):
        kb.op(A_, lambda e: e.activation(out=tmp[0], in_=ps[0], func=AF.Ln, scale=scale, bias=cst[0:tmp[2], 0:1]), r=[ps[1], CONST], w=[tmp[1]])
        kb.op(A_, lambda e: e.activation(out=out[0], in_=tmp[0], func=AF.Exp, scale=-0.5), r=[tmp[1]], w=[out[1]])

    def norm(l, scol, shcol):
        with ExitStack() as st:
            sq = [sb(st, "nsq", [128, 512]) for _ in range(4)]
            lnv = sb(st, "nln", [128, 512])
            rstd = [sb(st, "nrs", [128, 512]) for _ in range(2)]
            tmp = [sb(st, "ntm", [128, 512]) for _ in range(2)]

            def stA(n):
                tsl = slice(n * 512, (n + 1) * 512)
                ps = PS[n % 2]
                for c in range(8):
                    q = sq[c % 4]
                    if c % 2 == 0:
                        kb.op(A_, lambda e: e.activation(out=q[:], in_=xT[:, c, tsl], func=AF.Square), r=[XT[c][n]], w=[q.trk()])
                    else:
                        kb.op(V, lambda e: e.tensor_tensor(out=q[:], in0=xT[:, c, tsl], in1=xT[:, c, tsl], op=ALU.mult), r=[XT[c][n]], w=[q.trk()])
                    yield
                    kb.op("pe", lambda e: e.matmul(ps[:], lhsT=ones_f[:], rhs=q[:], start=(c == 0), stop=(c == 7)), r=[q.trk(), CONST], w=[ps.trk()])
                    yield
                rs = rstd[n % 2]
                kb.op(A_, lambda e: e.activation(out=lnv[:], in_=ps[:], func=AF.Ln, scale=1.0 / D, bias=cst[:, 0:1]), r=[ps.trk(), CONST], w=[lnv.trk()])
                yield
                kb.op(A_, lambda e: e.activation(out=rs[:], in_=lnv[:], func=AF.Exp, scale=-0.5), r=[lnv.trk()], w=[rs.trk()])
                yield

            def stB(n):
                tsl = slice(n * 512, (n + 1) * 512)
                rs = rstd[n % 2]
                for c in range(8):
                    tm = tmp[c % 2]
                    kb.op(V, lambda e: e.scalar_tensor_tensor(out=tm[:], in0=xT[:, c, tsl], scalar=par[:, l, scol + c:scol + c + 1], in1=rs[:],
                                                              op0=ALU.mult, op1=ALU.mult), r=[XT[c][n], rs.trk(), PAR], w=[tm.trk()])
                    yield
                    kb.op(A_, lambda e: e.activation(out=hT[:, c, tsl], in_=tm[:], func=AF.Identity, bias=par[:, l, shcol + c:shcol + c + 1]),
                          r=[tm.trk(), PAR], w=[HT[c][n]])
                    yield

            for i in range(NT + 1):
                gens = []
                if i >= 1:
                    gens.append(stB(i - 1))
                if i < NT:
                    gens.append(stA(i))
                while gens:
                    for g_ in list(gens):
                        try:
                            next(g_)
                        except StopIteration:
                            gens.remove(g_)
            kb.barrier()

    WIN = [dr["w_in"][l].rearrange("(kc p) n -> p kc n", p=128) for l in range(DEPTH)]

    def load_cols(wt, src3, c0, nc_):
        kb.dma("pool", wt[:, :, 0:nc_], src3[:, :, c0:c0 + nc_], wt.trk(), w=[wt.trk()])

    def proj_fm(ps, wt, m0, m, n, rows=None, extra_r=()):
        tsl = slice(n * 512, (n + 1) * 512)

        def f(e):
            ins = None
            for kc in range(8):
                ins = e.matmul(ps[0:m, :], lhsT=wt[:, kc, m0:m0 + m], rhs=hT[:, kc, tsl], start=(kc == 0), stop=(kc == 7))
            return ins
        kb.op("pe", f, r=[wt.trk()] + [HT[c][n] for c in range(8)] + list(extra_r), w=[ps.trk()])

    for l in range(DEPTH):
        lam_init = 0.8 - 0.6 * math.exp(-0.3 * l)
        norm(l, 48, 0)
        dump("h%d" % l, hT[:].rearrange("p c t -> p (c t)"), [128, 8 * T], HT[7][3], BF16)
        if stop_after == "norm1" and l == 0:
            break
        with ExitStack() as lst:
            s5_phase(nc, kb, sb, dr, PS, l, hT, HT, oT, OT, par, PAR, cst, CONST, flg, ident_f, mtz_f, mtz_b, sincos, sincos_tmps, WIN, dump)
            if S5STOP[0] is not None:
                break
            merged = sb(lst, "merged", [128, 8, T], BF16)
            MG = [[merged.trk((c, n)) for n in range(NT)] for c in range(8)]
            def merge_branch(r, first):
                with ExitStack() as st:
                    wg = [sb(st, "wg", [128, 8, 128], BF16) for _ in range(2)]
                    wb = [sb(st, "wb", [128, 4, 128], BF16) for _ in range(2)]
                    sg = [sb(st, "sg", [128, 512]) for _ in range(2)]
                    tm = [sb(st, "mtm", [128, 512]) for _ in range(2)]
                    wbv = dr["w_branch"][l, r].rearrange("(kc p) n -> p kc n", p=128)
                    it = 0
                    for dc in range(8):
                        g_ = wg[dc % 2]
                        b_ = wb[dc % 2]
                        load_cols(g_, WIN[l], C_GZ + r * 1024 + dc * 128, 128)
                        kb.dma("pool", b_[:], wbv[:, :, dc * 128:(dc + 1) * 128], b_.trk(), w=[b_.trk()])
                        for n in range(NT):
                            tsl = slice(n * 512, (n + 1) * 512)
                            pg = PS[(it * 2) % 8]
                            pb = PS[(it * 2 + 1) % 8]
                            it += 1
                            proj_fm(pg, g_, 0, 128, n)

                            def f(e, pb=pb, b_=b_, tsl=tsl):
                                ins = None
                                for kc in range(4):
                                    ins = e.matmul(pb[:], lhsT=b_[:, kc, :], rhs=oT[:, kc, tsl], start=(kc == 0), stop=(kc == 3))
                                return ins
                            kb.op("pe", f, r=[b_.trk()] + [OT[kc][n] for kc in range(4)], w=[pb.trk()])
                            s_ = sg[it % 2]
                            kb.op(A_, lambda e, s_=s_, pg=pg: e.activation(out=s_[:], in_=pg[:], func=AF.Sigmoid), r=[pg.trk()], w=[s_.trk()])
                            if first:
                                kb.op(V, lambda e, s_=s_, pb=pb, dc=dc, tsl=tsl: e.tensor_tensor(out=merged[:, dc, tsl], in0=pb[:], in1=s_[:], op=ALU.mult),
                                      r=[pb.trk(), s_.trk()], w=[MG[dc][n]])
                            else:
                                t_ = tm[it % 2]
                                kb.op(V, lambda e, s_=s_, pb=pb, t_=t_: e.tensor_tensor(out=t_[:], in0=pb[:], in1=s_[:], op=ALU.mult),
                                      r=[pb.trk(), s_.trk()], w=[t_.trk()])
                                kb.op(V, lambda e, t_=t_, dc=dc, tsl=tsl: e.tensor_tensor(out=merged[:, dc, tsl], in0=merged[:, dc, tsl], in1=t_[:], op=ALU.add),
                                      r=[t_.trk(), MG[dc][n]], w=[MG[dc][n]])
                    kb.barrier()

            merge_branch(2, True)
            dump("mergedc%d" % l, merged[:].rearrange("p c t -> p (c t)"), [128, 8 * T], MG[7][3], BF16)
            if stop_after == "s5":
                break
            gla_phase(nc, kb, sb, dr, PS, l, hT, HT, oT, OT, par, PAR, cst, CONST, flg, ones_f, m_le, m_ge, (m_le_b, m_ge_b, m_gt_b, m_lt_b), None, WIN, rstd_from_ps, proj_fm, load_cols, dump)
            merge_branch(1, False)
            if stop_after == "gla":
                break
            attn_phase(nc, kb, sb, dr, PS, l, hT, HT, oT, OT, par, PAR, cst, CONST, None, ident_f, ones_f, ones_b, blk64_b, rrot, make_rope,
                       WIN, rstd_from_ps, proj_fm, load_cols, dump)
            merge_branch(0, False)
            dump("merged%d" % l, merged[:].rearrange("p c t -> p (c t)"), [128, 8 * T], MG[7][3], BF16)
            with ExitStack() as st:
                wo = [sb(st, "wo", [128, 8, 128], BF16) for _ in range(2)]
                wov = dr["w_out"][l].rearrange("(kc p) n -> p kc n", p=128)
                it = 0
                for dc in range(8):
                    w_ = wo[dc % 2]
                    kb.dma("pool", w_[:], wov[:, :, dc * 128:(dc + 1) * 128], w_.trk(), w=[w_.trk()])
                    for n in range(NT):
                        tsl = slice(n * 512, (n + 1) * 512)
                        ps = PS[it % 8]
                        it += 1

                        def f(e, ps=ps, w_=w_, tsl=tsl):
                            ins = None
                            for kc in range(8):
                                ins = e.matmul(ps[:], lhsT=w_[:, kc, :], rhs=merged[:, kc, tsl], start=(kc == 0), stop=(kc == 7))
                            return ins
                        kb.op("pe", f, r=[w_.trk()] + [MG[kc][n] for kc in range(8)], w=[ps.trk()])
                        kb.op(V, lambda e, ps=ps, dc=dc, tsl=tsl: e.scalar_tensor_tensor(out=xT[:, dc, tsl], in0=ps[:], scalar=par[:, l, 16 + dc:17 + dc], in1=xT[:, dc, tsl],
                                                                                       op0=ALU.mult, op1=ALU.add), r=[ps.trk(), PAR, XT[dc][n]], w=[XT[dc][n]])
                kb.barrier()
        dump("xmid%d" % l, xT[:].rearrange("p c t -> p (c t)"), [128, 8 * T], XT[7][3])
        norm(l, 56, 24)
        with ExitStack() as st:
            aT = sb(st, "aT", [128, NJ, 1024], BF16)
            AT = [[aT.trk((j, n)) for n in range(2)] for j in range(NJ)]
            wgt = [sb(st, "fwg", [128, 8, 128], BF16) for _ in range(2)]
            wut = [sb(st, "fwu", [128, 8, 128], BF16) for _ in range(2)]
            wdt = [sb(st, "fwd", [128, NJ, 128], BF16) for _ in range(2)]
            sl = [sb(st, "fsl", [128, 512]) for _ in range(2)]
            wgv = dr["w_ffn_gate"][l].rearrange("(kc p) n -> p kc n", p=128)
            wuv = dr["w_ffn_up"][l].rearrange("(kc p) n -> p kc n", p=128)
            wdv = dr["w_ffn_down"][l].rearrange("(j p) n -> p j n", p=128)
            it = 0
            for half in range(2):
                for j in range(NJ):
                    g_ = wgt[j % 2]
                    u_ = wut[j % 2]
                    kb.dma("pool", g_[:], wgv[:, :, j * 128:(j + 1) * 128], g_.trk(), w=[g_.trk()])
                    kb.dma("pool", u_[:], wuv[:, :, j * 128:(j + 1) * 128], u_.trk(), w=[u_.trk()])
                    for nl in range(2):
                        n = half * 2 + nl
                        pg = PS[(it * 2) % 8]
                        pu = PS[(it * 2 + 1) % 8]
                        it += 1
                        proj_fm(pg, g_, 0, 128, n)
                        proj_fm(pu, u_, 0, 128, n)
                        s_ = sl[it % 2]
                        kb.op(A_, lambda e, s_=s_, pg=pg: e.activation(out=s_[:], in_=pg[:], func=AF.Silu), r=[pg.trk()], w=[s_.trk()])
                        kb.op(V, lambda e, s_=s_, pu=pu, j=j, nl=nl: e.tensor_tensor(out=aT[:, j, nl * 512:(nl + 1) * 512], in0=pu[:], in1=s_[:], op=ALU.mult),
                              r=[pu.trk(), s_.trk()], w=[AT[j][nl]])
                for dc in range(8):
                    w_ = wdt[dc % 2]
                    kb.dma("pool", w_[:], wdv[:, :, dc * 128:(dc + 1) * 128], w_.trk(), w=[w_.trk()])
                    for nl in range(2):
                        n = half * 2 + nl
                        tsl = slice(n * 512, (n + 1) * 512)
                        ps = PS[it % 8]
                        it += 1

                        def f(e, ps=ps, w_=w_, nl=nl):
                            ins = None
                            for j in range(NJ):
                                ins = e.matmul(ps[:], lhsT=w_[:, j, :], rhs=aT[:, j, nl * 512:(nl + 1) * 512], start=(j == 0), stop=(j == NJ - 1))
                            return ins
                        kb.op("pe", f, r=[w_.trk()] + [AT[j][nl] for j in range(NJ)], w=[ps.trk()])
                        kb.op(V, lambda e, ps=ps, dc=dc, tsl=tsl: e.scalar_tensor_tensor(out=xT[:, dc, tsl], in0=ps[:], scalar=par[:, l, 40 + dc:41 + dc], in1=xT[:, dc, tsl],
                                                                                       op0=ALU.mult, op1=ALU.add), r=[ps.trk(), PAR, XT[dc][n]], w=[XT[dc][n]])
            kb.barrier()
        dump("xout%d" % l, xT[:].rearrange("p c t -> p (c t)"), [128, 8 * T], XT[7][3])

    with ExitStack() as st:
        ys = [sb(st, "ys", [128, 1024]) for _ in range(2)]
        for b in range(NB):
            s_ = ys[b % 2]
            for half in range(2):
                ps = PS[(2 * b + half) % 4]
                kb.op("pe", lambda e, ps=ps, b=b, half=half: [e.transpose(out=ps[:, j * 128:(j + 1) * 128], in_=xT[:, half * 4 + j, b * 128:(b + 1) * 128],
                                                                         identity=ident_f[:]) for j in range(4)][-1],
                      r=[XT[half * 4 + j][b // 4] for j in range(4)] + [CONST], w=[ps.trk()])
                if half == 0:
                    kb.op(A_, lambda e, ps=ps, s_=s_: e.activation(out=s_[:, 0:512], in_=ps[:], func=AF.Copy), r=[ps.trk()], w=[s_.trk()])
                else:
                    kb.op(V, lambda e, ps=ps, s_=s_: e.tensor_copy(out=s_[:, 512:1024], in_=ps[:]), r=[ps.trk()], w=[s_.trk()])
            kb.dma("sp", dr["y"][b * 128:(b + 1) * 128, :], s_[:], s_.trk(), r=[s_.trk()])
        kb.barrier()
    kb.final_wait()
    es.close()
    if want_waited:
        return kb.waited
    return nc, dbg_out


def s5_phase(nc, kb, sb, dr, PS, l, hT, HT, ocT, OC, par, PAR, cst, CONST, flg, ident_f, mtz_f, mtz_b, sincos, sincos_tmps, WIN, dump):
    V, P_, A_ = "dve", "pool", "act"
    with ExitStack() as st:
        ycT = sb(st, "ycT", [128, 16, 512], BF16)
        YC = [ycT.trk(g) for g in range(16)]
        hfin = sb(st, "hfin", [128, 512])
        HF = hfin.trk()
        ciota = sb(st, "ciota", [128, 512])
        ones512 = sb(st, "ones512", [128, 512])
        ci_i = sb(st, "cii", [128, 512], I32)
        C5 = Trk()
        kb.op(P_, lambda e: e.iota(ci_i[:], pattern=[[1, 512]], base=0, channel_multiplier=0), w=[C5])
        kb.op(V, lambda e: e.tensor_copy(out=ciota[:], in_=ci_i[:]), r=[C5], w=[C5])
        kb.op(P_, lambda e: e.memset(ones512[:], 1.0), w=[C5])
        PR = Trk()
        NPR = 40
        pr = [sb(st, "s5pr", [128, NPR, 16]) for _ in range(2)]
        bc = [[sb(st, "bcm", [128, 16, 16]) for _ in range(2)] for _ in range(2)]
        cc = [[sb(st, "ccm", [128, 16, 16]) for _ in range(2)] for _ in range(2)]
        pwL = [[sb(st, "pwL", [128, 16, 4]) for _ in range(2)] for _ in range(2)]
        pwR = [[sb(st, "pwR", [128, 16, 4]) for _ in range(2)] for _ in range(2)]
        (LR, LI, LDT, DTt, LRDT, TH, MAG, TURN, CS, SN, AR, AI, A2R, A2I, A3R, A3I, A4R, A4I, IM2, Q1R, Q1I, Q2R, Q2I, Q3R, Q3I,
         DEN, AM1, FR, FI, R4, PHT, T0, T1, T2, H0R, H0I, ONE, ZERO, PXR, PXI) = range(40)
        with ExitStack() as st2:
            braw = [sb(st2, "braw", [128, 16, 16]) for _ in range(2)]
            btmp = sb(st2, "btmp", [128, 16, 16])
            craw = sb(st2, "craw", [16, 2048])
            tms = sincos_tmps(st2, [128, 16])
            for d in range(2):
                p = pr[d]

                def S(i, p=p):
                    return p[:, i, :]

                def tt(o, a, b, op, S=S):
                    kb.op(V, lambda e: e.tensor_tensor(out=S(o), in0=S(a), in1=S(b), op=op), r=[PR], w=[PR])

                def ts(o, a, s1, op0, S=S):
                    kb.op(V, lambda e: e.tensor_scalar(out=S(o), in0=S(a), scalar1=s1, scalar2=None, op0=op0), r=[PR], w=[PR])

                def act(o, a, func, scale=1.0, S=S):
                    kb.op(A_, lambda e: e.activation(out=S(o), in_=S(a), func=func, scale=scale), r=[PR], w=[PR])

                def cmul(o_r, o_i, a_r, a_i, b_r, b_i):
                    tt(T0, a_i, b_i, ALU.mult)
                    tt(T1, a_r, b_r, ALU.mult)
                    tt(T2, a_r, b_i, ALU.mult)
                    tt(o_i, a_i, b_r, ALU.mult)
                    tt(o_i, o_i, T2, ALU.add)
                    tt(o_r, T1, T0, ALU.subtract)

                kb.dma("sp", S(LR), dr["s5_lam_re"][l, d].rearrange("(gp g2) p -> (g2 p) gp", g2=2), PR, w=[PR], allow_slow_non_contiguous=True)
                kb.dma("sp", S(LI), dr["s5_lam_im"][l, d].rearrange("(gp g2) p -> (g2 p) gp", g2=2), PR, w=[PR], allow_slow_non_contiguous=True)
                ldv = dr["s5_log_dt"][l, d].rearrange("(gp g2) -> g2 gp", g2=2)
                for g2 in range(2):
                    kb.dma("sp", p[64 * g2:64 * g2 + 64, LDT, :], ldv[g2:g2 + 1, :].partition_broadcast(64), PR, w=[PR], allow_slow_non_contiguous=True)
                for ri in range(2):
                    kb.dma("sp", S(H0R + ri), dr["s5h0"][l, d, ri].rearrange("(gp g2) p -> (g2 p) gp", g2=2), PR, w=[PR], allow_slow_non_contiguous=True)
                kb.op(P_, lambda e, S=S: e.memset(S(ONE), 1.0), w=[PR])
                kb.op(P_, lambda e, S=S: e.memset(S(ZERO), 0.0), w=[PR])
                act(DTt, LDT, AF.Exp)
                tt(LRDT, LR, DTt, ALU.mult)
                tt(TH, LI, DTt, ALU.mult)
                act(MAG, LRDT, AF.Exp)
                ts(TURN, TH, 1.0 / (2 * math.pi), ALU.mult)
                sincos((tms[0][:], tms[1][:], tms[2][:], [tms[3]]), S(CS), S(SN), S(TURN), [PR], [PR])
                tt(AR, MAG, CS, ALU.mult)
                tt(AI, MAG, SN, ALU.mult)
                cmul(A2R, A2I, AR, AI, AR, AI)
                cmul(A3R, A3I, A2R, A2I, AR, AI)
                cmul(A4R, A4I, A2R, A2I, A2R, A2I)
                act(IM2, LRDT, AF.Exp, -2.0)
                tt(Q1R, AR, IM2, ALU.mult)
                tt(Q1I, AI, IM2, ALU.mult)
                ts(Q1I, Q1I, -1.0, ALU.mult)
                cmul(Q2R, Q2I, Q1R, Q1I, Q1R, Q1I)
                cmul(Q3R, Q3I, Q2R, Q2I, Q1R, Q1I)
                tt(T0, LR, LR, ALU.mult)
                tt(T1, LI, LI, ALU.mult)
                tt(DEN, T0, T1, ALU.add)
                kb.op(V, lambda e, S=S: e.reciprocal(out=S(DEN), in_=S(DEN)), r=[PR], w=[PR])
                ts(AM1, AR, -1.0, ALU.add)
                tt(T0, AM1, LR, ALU.mult)
                tt(T1, AI, LI, ALU.mult)
                tt(T0, T0, T1, ALU.add)
                tt(FR, T0, DEN, ALU.mult)
                tt(T0, AI, LR, ALU.mult)
                tt(T1, AM1, LI, ALU.mult)
                tt(T0, T0, T1, ALU.subtract)
                tt(FI, T0, DEN, ALU.mult)
                act(R4, LRDT, AF.Exp, 4.0)
                ts(PHT, TURN, 4.0, ALU.mult)
                pos = [(ONE, ZERO), (AR, AI), (A2R, A2I), (A3R, A3I)]
                neg = [(ONE, ZERO), (Q1R, Q1I), (Q2R, Q2I), (Q3R, Q3I)]
                Lp, Rp = (neg, pos) if d == 0 else (pos, neg)
                for s in range(4):
                    for ri in range(2):
                        kb.op(V, lambda e, s=s, ri=ri, S=S, Lp=Lp: e.tensor_copy(out=pwL[d][ri][:, :, s], in_=S(Lp[s][ri])), r=[PR], w=[PR])
                        kb.op(V, lambda e, s=s, ri=ri, S=S, Rp=Rp: e.tensor_copy(out=pwR[d][ri][:, :, s], in_=S(Rp[s][ri])), r=[PR], w=[PR])
                for ri, nm in enumerate(("s5_b_re", "s5_b_im")):
                    kb.dma("sp", braw[ri][:], dr[nm][l, d].rearrange("(gp g2) p j -> (g2 p) gp j", g2=2), PR, w=[PR])
                frb = S(FR).unsqueeze(2).to_broadcast([128, 16, 16])
                fib = S(FI).unsqueeze(2).to_broadcast([128, 16, 16])
                bb = bc[d]
                kb.op(V, lambda e, bb=bb, frb=frb: e.tensor_tensor(out=bb[0][:], in0=braw[0][:], in1=frb, op=ALU.mult), r=[PR], w=[PR])
                kb.op(V, lambda e, fib=fib: e.tensor_tensor(out=btmp[:], in0=braw[1][:], in1=fib, op=ALU.mult), r=[PR], w=[PR])
                kb.op(V, lambda e, bb=bb: e.tensor_tensor(out=bb[0][:], in0=bb[0][:], in1=btmp[:], op=ALU.subtract), r=[PR], w=[PR])
                kb.op(V, lambda e, bb=bb, frb=frb: e.tensor_tensor(out=bb[1][:], in0=braw[1][:], in1=frb, op=ALU.mult), r=[PR], w=[PR])
                kb.op(V, lambda e, fib=fib: e.tensor_tensor(out=btmp[:], in0=braw[0][:], in1=fib, op=ALU.mult), r=[PR], w=[PR])
                kb.op(V, lambda e, bb=bb: e.tensor_tensor(out=bb[1][:], in0=bb[1][:], in1=btmp[:], op=ALU.add), r=[PR], w=[PR])
                for ri, nm in enumerate(("s5_c_re", "s5_c_im")):
                    kb.dma("sp", craw[:].rearrange("i (g p) -> i g p", p=64), dr[nm][l, d].rearrange("g i p -> i g p"), PR, w=[PR])
                    ps = PS[6 + ri]
                    kb.op("pe", lambda e, ps=ps: [e.transpose(out=ps[:, gp * 16:(gp + 1) * 16], in_=craw[:, gp * 128:(gp + 1) * 128], identity=ident_f[0:16, 0:16])
                                                  for gp in range(16)][-1], r=[PR, CONST], w=[ps.trk()])
                    kb.op(V, lambda e, ri=ri, ps=ps, d=d: e.tensor_copy(out=cc[d][ri][:], in_=ps[:, 0:256].rearrange("p (g i) -> p g i", i=16)),
                          r=[ps.trk(), PR], w=[PR])
                if d == 0:
                    kb.op(V, lambda e, S=S: e.tensor_copy(out=S(PXR), in_=S(A3R)), r=[PR], w=[PR])
                    kb.op(V, lambda e, S=S: e.tensor_copy(out=S(PXI), in_=S(A3I)), r=[PR], w=[PR])
            kb.barrier()
        dump("s5pr%d" % l, pr[0][:].rearrange("p a b -> p (a b)"), [128, NPR * 16], PR)
        if S5STOP[0] == "prep":
            kb.barrier()
            return
        QX = [(AR, AI), (A4R, A4I)]

        with ExitStack() as st3:
            wcu = [sb(st3, "wcu", [128, 8, 32], BF16) for _ in range(2)]
            u4b = [sb(st3, "u4b", [128, 512], BF16) for _ in range(2)]
            u4f = sb(st3, "u4f", [128, 512])
            zsrc = [[sb(st3, "zs", [128, 32]) for _ in range(2)] for _ in range(2)]
            Lt = [sb(st3, "Lt", [128, 4, 32]) for _ in range(2)]
            Rt = [sb(st3, "Rt", [128, 4, 32]) for _ in range(2)]
            L3 = [sb(st3, "L3", [128, 128]) for _ in range(2)]
            Qm = [[sb(st3, "Qm", [128, 128]) for _ in range(2)] for _ in range(2)]
            ctmp = sb(st3, "ctmp", [128, 4, 32])
            c128 = sb(st3, "c128", [128, 128])
            tzb = [sb(st3, "tzb", [128, 128], BF16) for _ in range(2)]
            pmb = [[sb(st3, "pmb", [128, 128], BF16) for _ in range(2)] for _ in range(2)]
            cosT = sb(st3, "cosT", [128, 512])
            sinT = sb(st3, "sinT", [128, 512])
            xr = [sb(st3, "xr", [128, 512]) for _ in range(2)]
            gg = [sb(st3, "gg", [128, 512]) for _ in range(2)]
            tA = sb(st3, "tA", [128, 512])
            tB = sb(st3, "tB", [128, 512])
            dec = sb(st3, "dec", [128, 512])
            hs = [sb(st3, "hs", [128, 513]) for _ in range(2)]
            ini = sb(st3, "ini", [128, 4])
            TT_ = Trk()
            ZS = Trk()
            for a_ in range(2):
                for b_ in range(2):
                    kb.op(P_, lambda e, a_=a_, b_=b_: e.memset(zsrc[a_][b_][:], 0.0), w=[ZS])
            hfv = hfin[:].rearrange("p (q d r g) -> p q d r g", d=2, r=2, g=16)
            wcv = WIN[l]
            try:
                def prep_gen(gp, d):
                    ub = u4b[gp % 2]
                    if d == 0:
                        wc_ = wcu[gp % 2]
                        yield
                        kb.dma("pool", wc_[:], wcv[:, :, C_CU + gp * 32:C_CU + gp * 32 + 32], wc_.trk(), w=[wc_.trk()])
                        pu = PS[0]
                        def fu(e, wc_=wc_):
                            ins = None
                            for kc in range(8):
                                for s in range(4):
                                    ins = e.matmul(pu[32 * s:32 * s + 32, :], lhsT=wc_[:, kc, :], rhs=hT[:, kc, s::4], start=(kc == 0), stop=(kc == 7),
                                                   tile_position=(0, 32 * s))
                            return ins
                        yield
                        kb.op("pe", fu, r=[wc_.trk()] + [HT[c][n] for c in range(8) for n in range(NT)], w=[pu.trk()])
                        yield
                        kb.op(A_, lambda e, ub=ub: e.activation(out=ub[:], in_=pu[:], func=AF.Copy), r=[pu.trk()], w=[ub.trk()])
                        yield
                    p = pr[d]
                    for a_, srcs in ((0, bc[d]), (1, cc[d])):
                        for ri in range(2):
                            for g2 in range(2):
                                kb.op(V, lambda e, a_=a_, ri=ri, g2=g2, srcs=srcs: e.tensor_copy(out=zsrc[a_][ri][64 * g2:64 * g2 + 64, 16 * g2:16 * g2 + 16],
                                                                                                 in_=srcs[ri][64 * g2:64 * g2 + 64, gp, :]), r=[PR], w=[ZS])
                    for (dst, src, pw, neg_im) in ((Lt, zsrc[0], pwL[d], False), (Rt, zsrc[1], pwR[d], True)):
                        s_re = src[0][:, :].unsqueeze(1).to_broadcast([128, 4, 32])
                        s_im = src[1][:, :].unsqueeze(1).to_broadcast([128, 4, 32])
                        w_re = pw[0][:, gp, :].unsqueeze(2).to_broadcast([128, 4, 32])
                        w_im = pw[1][:, gp, :].unsqueeze(2).to_broadcast([128, 4, 32])
                        yield
                        kb.op(V, lambda e, dst=dst, s_re=s_re, w_re=w_re: e.tensor_tensor(out=dst[0][:], in0=s_re, in1=w_re, op=ALU.mult), r=[PR, ZS], w=[dst[0].trk()])
                        yield
                        kb.op(V, lambda e, s_im=s_im, w_im=w_im: e.tensor_tensor(out=ctmp[:], in0=s_im, in1=w_im, op=ALU.mult), r=[PR, ZS], w=[ctmp.trk()])
                        yield
                        kb.op(V, lambda e, dst=dst: e.tensor_tensor(out=dst[0][:], in0=dst[0][:], in1=ctmp[:], op=ALU.subtract), r=[ctmp.trk()], w=[dst[0].trk()])
                        yield
                        kb.op(V, lambda e, dst=dst, s_re=s_re, w_im=w_im: e.tensor_tensor(out=dst[1][:], in0=s_re, in1=w_im, op=ALU.mult), r=[PR, ZS], w=[dst[1].trk()])
                        yield
                        kb.op(V, lambda e, s_im=s_im, w_re=w_re: e.tensor_tensor(out=ctmp[:], in0=s_im, in1=w_re, op=ALU.mult), r=[PR, ZS], w=[ctmp.trk()])
                        if neg_im:
                            yield
                            kb.op(V, lambda e, dst=dst: e.scalar_tensor_tensor(out=dst[1][:], in0=dst[1][:], scalar=-1.0, in1=ctmp[:], op0=ALU.mult, op1=ALU.subtract),
                                  r=[ctmp.trk()], w=[dst[1].trk()])
                        else:
                            kb.op(V, lambda e, dst=dst: e.tensor_tensor(out=dst[1][:], in0=dst[1][:], in1=ctmp[:], op=ALU.add), r=[ctmp.trk()], w=[dst[1].trk()])
                    L2 = [Lt[i][:].rearrange("p s j -> p (s j)") for i in range(2)]
                    R2 = [Rt[i][:].rearrange("p s j -> p (s j)") for i in range(2)]
                    LT_ = [Lt[0].trk(), Lt[1].trk()]
                    RT_ = [Rt[0].trk(), Rt[1].trk()]
                    pt = PS[2]
                    yield
                    kb.op("pe", lambda e, L2=L2, R2=R2: [e.matmul(pt[:, 0:128], lhsT=L2[0], rhs=R2[0], start=True, stop=False),
                                                        e.matmul(pt[:, 0:128], lhsT=L2[1], rhs=R2[1], start=False, stop=True)][-1],
                          r=LT_ + RT_, w=[pt.trk()])
                    mk = mtz_f if d == 0 else mtz_b
                    yield
                    kb.op(V, lambda e, d=d, mk=mk: e.tensor_tensor(out=tzb[d][:], in0=pt[:, 0:128], in1=mk[:], op=ALU.mult), r=[pt.trk(), CONST], w=[tzb[d].trk()])
                    if d == 0:
                        c_r = p[:, PXR, gp:gp + 1]
                        c_i = p[:, PXI, gp:gp + 1]
                        yield
                        kb.op(V, lambda e, L2=L2, c_i=c_i: e.tensor_scalar(out=c128[:], in0=L2[1], scalar1=c_i, scalar2=None, op0=ALU.mult), r=[LT_[1], PR], w=[c128.trk()])
                        yield
                        kb.op(V, lambda e, L2=L2, c_r=c_r: e.scalar_tensor_tensor(out=L3[0][:], in0=L2[0], scalar=c_r, in1=c128[:], op0=ALU.mult, op1=ALU.subtract),
                              r=[LT_[0], c128.trk(), PR], w=[L3[0].trk()])
                        yield
                        kb.op(V, lambda e, L2=L2, c_i=c_i: e.tensor_scalar(out=c128[:], in0=L2[0], scalar1=c_i, scalar2=None, op0=ALU.mult), r=[LT_[0], PR], w=[c128.trk()])
                        yield
                        kb.op(V, lambda e, L2=L2, c_r=c_r: e.scalar_tensor_tensor(out=L3[1][:], in0=L2[1], scalar=c_r, in1=c128[:], op0=ALU.mult, op1=ALU.add),
                              r=[LT_[1], c128.trk(), PR], w=[L3[1].trk()])
                        Lsrc = [L3[0][:], L3[1][:]]
                        ltr = [L3[0].trk(), L3[1].trk()]
                    else:
                        Lsrc = L2
                        ltr = LT_
                    pp = PS[3]
                    yield
                    kb.op("pe", lambda e, Lsrc=Lsrc: [e.transpose(out=pp[:, 0:128], in_=Lsrc[0], identity=ident_f[:]),
                                                      e.transpose(out=pp[:, 128:256], in_=Lsrc[1], identity=ident_f[:])][-1], r=ltr + [CONST], w=[pp.trk()])
                    yield
                    kb.op(A_, lambda e, d=d: e.activation(out=pmb[d][0][:], in_=pp[:, 0:128], func=AF.Copy), r=[pp.trk()], w=[pmb[d][0].trk()])
                    yield
                    kb.op(A_, lambda e, d=d: e.activation(out=pmb[d][1][:], in_=pp[:, 128:256], func=AF.Copy), r=[pp.trk()], w=[pmb[d][1].trk()])
                    c_r = p[:, QX[d][0], gp:gp + 1]
                    c_i = p[:, QX[d][1], gp:gp + 1]
                    yield
                    kb.op(V, lambda e, R2=R2, c_i=c_i: e.tensor_scalar(out=c128[:], in0=R2[1], scalar1=c_i, scalar2=None, op0=ALU.mult), r=[RT_[1], PR], w=[c128.trk()])
                    yield
                    kb.op(V, lambda e, d=d, R2=R2, c_r=c_r: e.scalar_tensor_tensor(out=Qm[d][0][:], in0=R2[0], scalar=c_r, in1=c128[:], op0=ALU.mult, op1=ALU.add),
                          r=[RT_[0], c128.trk(), PR], w=[Qm[d][0].trk()])
                    yield
                    kb.op(V, lambda e, R2=R2, c_i=c_i: e.tensor_scalar(out=c128[:], in0=R2[0], scalar1=c_i, scalar2=None, op0=ALU.mult), r=[RT_[0], PR], w=[c128.trk()])
                    yield
                    kb.op(V, lambda e, d=d, R2=R2, c_r=c_r: e.scalar_tensor_tensor(out=Qm[d][1][:], in0=R2[1], scalar=c_r, in1=c128[:], op0=ALU.mult, op1=ALU.subtract),
                          r=[RT_[1], c128.trk(), PR], w=[Qm[d][1].trk()])
                    yield

                def recur_gen(gp, d):
                    ub = u4b[gp % 2]
                    py = PS[1]
                    p = pr[d]
                    px = [PS[4], PS[5]]
                    for ri in range(2):
                        kb.op("pe", lambda e, ri=ri, d=d, ub=ub: e.matmul(px[ri][:], lhsT=pmb[d][ri][:], rhs=ub[:], start=True, stop=True),
                              r=[pmb[d][ri].trk(), ub.trk()], w=[px[ri].trk()])
                    tur = gg[1]
                    yield
                    kb.op(V, lambda e, p=p: e.tensor_scalar(out=tur[:], in0=ciota[:], scalar1=p[:, PHT, gp:gp + 1], scalar2=None, op0=ALU.mult),
                          r=[C5, PR], w=[tur.trk()])
                    yield
                    sincos((xr[0][:].bitcast(I32), xr[1][:], gg[0][:], [xr[0].trk(), xr[1].trk(), gg[0].trk()]), cosT[:], sinT[:], tur[:], [tur.trk()], [TT_])
                    def pv(ap, d=d):
                        return ap if d == 0 else ap[:, ::-1]
                    X_r, X_i = pv(px[0][:]), pv(px[1][:])
                    yield
                    kb.op(V, lambda e, X_i=X_i: e.tensor_tensor(out=tA[:], in0=X_i, in1=sinT[:], op=ALU.mult), r=[px[1].trk(), TT_], w=[tA.trk()])
                    yield
                    kb.op(V, lambda e, X_r=X_r: e.tensor_tensor(out=xr[0][:], in0=X_r, in1=cosT[:], op=ALU.mult), r=[px[0].trk(), TT_], w=[xr[0].trk()])
                    yield
                    kb.op(V, lambda e: e.tensor_tensor(out=xr[0][:], in0=xr[0][:], in1=tA[:], op=ALU.add), r=[tA.trk()], w=[xr[0].trk()])
                    yield
                    kb.op(V, lambda e, X_r=X_r: e.tensor_tensor(out=tB[:], in0=X_r, in1=sinT[:], op=ALU.mult), r=[px[0].trk(), TT_], w=[tB.trk()])
                    yield
                    kb.op(V, lambda e, X_i=X_i: e.tensor_tensor(out=xr[1][:], in0=X_i, in1=cosT[:], op=ALU.mult), r=[px[1].trk(), TT_], w=[xr[1].trk()])
                    yield
                    kb.op(V, lambda e: e.tensor_tensor(out=xr[1][:], in0=xr[1][:], in1=tB[:], op=ALU.subtract), r=[tB.trk()], w=[xr[1].trk()])
                    yield
                    kb.op(A_, lambda e, p=p: e.activation(out=dec[:], in_=ciota[:], func=AF.Identity, scale=0.0, bias=p[:, R4, gp:gp + 1]), r=[C5, PR], w=[dec.trk()])
                    yield
                    kb.op(V, lambda e: e.tensor_scalar(out=dec[:, ::64], in0=dec[:, ::64], scalar1=flg[:, 0:1], scalar2=None, op0=ALU.mult), r=[CONST], w=[dec.trk()])
                    h0r = p[:, H0R, gp:gp + 1]
                    h0i = p[:, H0I, gp:gp + 1]
                    c1 = cosT[:, 1:2]
                    s1 = sinT[:, 1:2]
                    IN_ = ini.trk()
                    yield
                    kb.op(V, lambda e, h0i=h0i, s1=s1: e.tensor_tensor(out=ini[:, 2:3], in0=h0i, in1=s1, op=ALU.mult), r=[PR, TT_], w=[IN_])
                    yield
                    kb.op(V, lambda e, h0r=h0r, c1=c1: e.scalar_tensor_tensor(out=ini[:, 0:1], in0=h0r, scalar=c1, in1=ini[:, 2:3], op0=ALU.mult, op1=ALU.subtract), r=[PR, TT_], w=[IN_])
                    yield
                    kb.op(V, lambda e, h0r=h0r, s1=s1: e.tensor_tensor(out=ini[:, 3:4], in0=h0r, in1=s1, op=ALU.mult), r=[PR, TT_], w=[IN_])
                    yield
                    kb.op(V, lambda e, h0i=h0i, c1=c1: e.scalar_tensor_tensor(out=ini[:, 1:2], in0=h0i, scalar=c1, in1=ini[:, 3:4], op0=ALU.mult, op1=ALU.add), r=[PR, TT_], w=[IN_])
                    for ri in range(2):
                        kb.op(V, lambda e, ri=ri: e.tensor_tensor_scan(out=gg[ri][:], data0=dec[:], data1=xr[ri][:], initial=ini[:, ri:ri + 1], op0=ALU.mult, op1=ALU.add),
                              r=[dec.trk(), xr[ri].trk(), IN_], w=[gg[ri].trk()])
                    if d == 0:
                        Hre, Him = hs[0][:, 1:513], hs[1][:, 1:513]
                    else:
                        Hre, Him = hs[0][:, 0:512][:, ::-1], hs[1][:, 0:512][:, ::-1]
                    HS = [hs[0].trk(), hs[1].trk()]
                    yield
                    kb.op(V, lambda e: e.tensor_tensor(out=tA[:], in0=gg[1][:], in1=sinT[:], op=ALU.mult), r=[gg[1].trk(), TT_], w=[tA.trk()])
                    yield
                    kb.op(V, lambda e: e.tensor_tensor(out=tB[:], in0=gg[0][:], in1=cosT[:], op=ALU.mult), r=[gg[0].trk(), TT_], w=[tB.trk()])
                    yield
                    kb.op(V, lambda e, Hre=Hre: e.tensor_tensor(out=Hre, in0=tB[:], in1=tA[:], op=ALU.subtract), r=[tA.trk(), tB.trk()], w=[HS[0]])
                    yield
                    kb.op(V, lambda e: e.tensor_tensor(out=tA[:], in0=gg[0][:], in1=sinT[:], op=ALU.mult), r=[gg[0].trk(), TT_], w=[tA.trk()])
                    yield
                    kb.op(V, lambda e: e.tensor_tensor(out=tB[:], in0=gg[1][:], in1=cosT[:], op=ALU.mult), r=[gg[1].trk(), TT_], w=[tB.trk()])
                    yield
                    kb.op(V, lambda e, Him=Him: e.tensor_tensor(out=Him, in0=tB[:], in1=tA[:], op=ALU.add), r=[tA.trk(), tB.trk()], w=[HS[1]])
                    for ri in range(2):
                        h_ = hs[ri]
                        if d == 0:
                            fin = h_[:, 64:513:64]
                            icol = h_[:, 0:1]
                        else:
                            fin = h_[:, 0:512:64]
                            icol = h_[:, 512:513]
                        yield
                        kb.op(V, lambda e, ri=ri, d=d, fin=fin: e.tensor_copy(out=hfv[:, :, d, ri, gp], in_=fin), r=[HS[ri]], w=[HF])
                        yield
                        kb.op(V, lambda e, h_=h_: e.tensor_scalar(out=h_[:, 64:512:64], in0=h_[:, 64:512:64], scalar1=flg[:, 0:1], scalar2=None, op0=ALU.mult),
                              r=[CONST, HF], w=[HS[ri]])
                        yield
                        kb.op(V, lambda e, icol=icol, ri=ri, p=p: e.tensor_copy(out=icol, in_=p[:, H0R + ri, gp:gp + 1]), r=[PR], w=[HS[ri]])
                    hv = [hs[0][:, 0:512], hs[1][:, 0:512]] if d == 0 else [hs[0][:, 1:513], hs[1][:, 1:513]]
                    def fy(e, d=d, hv=hv, ub=ub):
                        e.matmul(py[:], lhsT=tzb[d][:], rhs=ub[:], start=(d == 0), stop=False)
                        e.matmul(py[:], lhsT=Qm[d][0][:], rhs=hv[0], start=False, stop=False)
                        return e.matmul(py[:], lhsT=Qm[d][1][:], rhs=hv[1], start=False, stop=(d == 1))
                    yield
                    kb.op("pe", fy, r=[tzb[d].trk(), ub.trk(), Qm[d][0].trk(), Qm[d][1].trk()] + HS, w=[py.trk()])
                    if d == 1:
                        ytm = xr[0]
                        yield
                        kb.op(V, lambda e: e.scalar_tensor_tensor(out=ytm[:], in0=ub[:], scalar=par[:, l, 96 + gp:97 + gp], in1=py[:], op0=ALU.mult, op1=ALU.add),
                              r=[ub.trk(), PAR, py.trk()], w=[ytm.trk()])
                        if gp == 0:
                            dump("s5y%d" % l, ytm[:], [128, 512], ytm.trk())
                        yield
                        kb.op(A_, lambda e: e.activation(out=tA[:], in_=ytm[:], func=AF.Square), r=[ytm.trk()], w=[tA.trk()])
                        yield
                        kb.op(V, lambda e: e.tensor_scalar(out=tA[:], in0=tA[:], scalar1=0.044715, scalar2=1.0, op0=ALU.mult, op1=ALU.add), r=[], w=[tA.trk()])
                        yield
                        kb.op(V, lambda e: e.tensor_tensor(out=tA[:], in0=tA[:], in1=ytm[:], op=ALU.mult), r=[ytm.trk()], w=[tA.trk()])
                        yield
                        kb.op(A_, lambda e: e.activation(out=tA[:], in_=tA[:], func=AF.Tanh, scale=0.7978845608), r=[], w=[tA.trk()])
                        yield
                        kb.op(V, lambda e: e.tensor_scalar(out=tA[:], in0=tA[:], scalar1=1.0, scalar2=0.5, op0=ALU.add, op1=ALU.mult), r=[], w=[tA.trk()])
                        yield
                        kb.op(V, lambda e, gp=gp: e.tensor_tensor(out=ycT[:, gp, :], in0=tA[:], in1=ytm[:], op=ALU.mult), r=[ytm.trk(), tA.trk()], w=[YC[gp]])
                    yield

                NK = 32
                g0 = prep_gen(0, 0)
                for _ in g0:
                    pass
                for k in range(NK):
                    gens = [recur_gen(k // 2, k % 2)]
                    if k + 1 < NK:
                        gens.append(prep_gen((k + 1) // 2, (k + 1) % 2))
                    while gens:
                        for g_ in list(gens):
                            try:
                                next(g_)
                            except StopIteration:
                                gens.remove(g_)
            except _Stop:
                pass
            kb.barrier()
        if S5STOP[0] not in (None, "fin", "glu"):
            return
        with ExitStack() as st5:
            hfo = sb(st5, "hfo", [128, 512])
            ps = PS[0]
            kb.op("pe", lambda e: [e.transpose(out=ps[:, j * 128:(j + 1) * 128], in_=hfin[:, j * 128:(j + 1) * 128], identity=ident_f[:]) for j in range(4)][-1],
                  r=[HF, CONST], w=[ps.trk()])
            kb.op(V, lambda e: e.tensor_copy(out=hfo[:], in_=ps[:]), r=[ps.trk()], w=[hfo.trk()])
            ov = dr["news5"][l].rearrange("q d r g p -> (q d r g p)").rearrange("(j x y) -> x j y", x=128, y=128)
            kb.dma("sp", ov, hfo[:].rearrange("p (j y) -> p j y", y=128), hfo.trk(), r=[hfo.trk()])
            kb.barrier()
        if S5STOP[0] == "fin":
            return
        with ExitStack() as st6:
            wrep = [[sb(st6, "wrep", [128, 16, 128], BF16) for _ in range(2)] for _ in range(2)]
            sgt = [sb(st6, "sgt", [128, 512]) for _ in range(2)]
            wgv = dr["s5_w_glu"][l].rearrange("(gp ii) n -> ii gp n", ii=32)
            it = 0
            for j in range(4):
                for ab in range(2):
                    w_ = wrep[j % 2][ab]
                    c0 = (ab * 4 + j) * 128
                    for t in range(4):
                        kb.dma("pool", w_[32 * t:32 * t + 32, :, :], wgv[:, :, c0:c0 + 128], w_.trk(), w=[w_.trk()])
                bk = [[PS[ab * 4 + t] for t in range(4)] for ab in range(2)]
                for ab in range(2):
                    w_ = wrep[j % 2][ab]

                    def fg(e, w_=w_, ab=ab):
                        ins = None
                        for gp in range(16):
                            for t in range(4):
                                ins = e.matmul(bk[ab][t][:], lhsT=w_[32 * t:32 * t + 32, gp, :], rhs=ycT[32 * t:32 * t + 32, gp, :], start=(gp == 0), stop=(gp == 15),
                                               tile_position=(32 * t, 0))
                        return ins
                    kb.op("pe", fg, r=[w_.trk()] + YC, w=[bk[ab][t].trk() for t in range(4)])
                for t in range(4):
                    pa, pb = bk[0][t], bk[1][t]
                    it += 1
                    s_ = sgt[it % 2]
                    kb.op(A_, lambda e, s_=s_, pb=pb, j=j: e.activation(out=s_[:], in_=pb[:], func=AF.Sigmoid, bias=par[:, l, 92 + j:93 + j]), r=[pb.trk(), PAR], w=[s_.trk()])
                    kb.op(V, lambda e, s_=s_, pa=pa, j=j, t=t: e.scalar_tensor_tensor(out=ocT[:, j, t::4], in0=pa[:], scalar=par[:, l, 88 + j:89 + j], in1=s_[:], op0=ALU.add, op1=ALU.mult),
                          r=[pa.trk(), s_.trk(), PAR], w=[OC[j][n] for n in range(NT)])
            kb.barrier()
    dump("oc%d" % l, ocT[:].rearrange("p c t -> p (c t)"), [128, 4 * T], OC[3][3], BF16)


def gla_phase(nc, kb, sb, dr, PS, l, hT, HT, oT, OT, par, PAR, cst, CONST, flg, ones_f, m_le, m_ge, m_gt, m_lt, WIN, rstd_from_ps, proj_fm, load_cols, dump):
    V, P_, A_ = "dve", "pool", "act"
    with ExitStack() as st:
        rTa = [sb(st, "rTa", [17, T], BF16) for _ in range(2)]
        wa2a = [sb(st, "wa2a", [17, 256], BF16) for _ in range(2)]
        wbr = sb(st, "wbr", [128, 8, 32], BF16)
        load_cols(wbr, WIN[l], C_BR, 32)
        RT = [Trk(), Trk()]
        for d in range(2):
            kb.op(P_, lambda e, d=d: e.memset(rTa[d][:], 1.0), w=[RT[d]])
            kb.dma("pool", wa2a[d][0:16, :], dr["gla_wa2"][l, d], wa2a[d].trk(), w=[wa2a[d].trk()])
            kb.dma("pool", wa2a[d][16:17, :], dr["gla_ba"][l, d].rearrange("(o n) -> o n", o=1), wa2a[d].trk(), w=[wa2a[d].trk()])
            for n in range(NT):
                ps = PS[n % 2]
                proj_fm(ps, wbr, d * 16, 16, n)
                kb.op(A_, lambda e, d=d, n=n, ps=ps: e.activation(out=rTa[d][0:16, n * 512:(n + 1) * 512], in_=ps[0:16, :], func=AF.Copy), r=[ps.trk()], w=[RT[d]])
        wq = sb(st, "gwq", [128, 8, 64], BF16)
        wk = sb(st, "gwk", [128, 8, 64], BF16)
        wv = sb(st, "gwv", [128, 8, 128], BF16)
        wg = sb(st, "gwg", [128, 8, 128], BF16)
        bqT = sb(st, "bqT", [64, T], BF16)
        bkT = sb(st, "bkT", [64, T], BF16)
        bkt = sb(st, "bkt", [128, NB, 64], BF16)
        bvt = sb(st, "bvt", [128, NB, 128], BF16)
        sgT = sb(st, "sgT", [128, T], BF16)
        obuf = sb(st, "obuf", [128, T])
        OBF = [obuf.trk(b) for b in range(NB)]
        Sd = [sb(st, "gS", [64, 128]) for _ in range(2)]
        Sbd = [sb(st, "gSb", [64, 128], BF16) for _ in range(2)]
        stg = [sb(st, "gstg", [64, 128]) for _ in range(2)]
        step = 0
        nst_ = [0]
        for h in range(4):
            load_cols(wq, WIN[l], C_BQ + h * 64, 64)
            load_cols(wk, WIN[l], C_BK + h * 64, 64)
            load_cols(wv, WIN[l], C_BV + h * 128, 128)
            load_cols(wg, WIN[l], C_BG + h * 128, 128)
            for n in range(NT):
                tsl = slice(n * 512, (n + 1) * 512)
                pa, pb = PS[0], PS[1]
                proj_fm(pa, wq, 0, 64, n)
                kb.op(A_, lambda e, tsl=tsl, pa=pa: e.activation(out=bqT[:, tsl], in_=pa[0:64, :], func=AF.Copy, scale=0.125), r=[pa.trk()], w=[bqT.trk()])
                proj_fm(pb, wk, 0, 64, n)
                kb.op(V, lambda e, tsl=tsl, pb=pb: e.tensor_copy(out=bkT[:, tsl], in_=pb[0:64, :]), r=[pb.trk()], w=[bkT.trk()])
                proj_fm(pa, wg, 0, 128, n)
                kb.op(A_, lambda e, tsl=tsl, pa=pa: e.activation(out=sgT[:, tsl], in_=pa[:], func=AF.Silu), r=[pa.trk()], w=[sgT.trk()])
            for g4 in range(4):
                pk, pv = PS[0], PS[1]

                def fk(e, g4=g4, pk=pk):
                    ins = None
                    for j in range(4):
                        blk = g4 * 4 + j
                        for kc in range(8):
                            ins = e.matmul(pk[:, j * 64:(j + 1) * 64], lhsT=hT[:, kc, blk * 128:(blk + 1) * 128], rhs=wk[:, kc, :], start=(kc == 0), stop=(kc == 7))
                    return ins
                kb.op("pe", fk, r=[wk.trk()] + [HT[c][g4] for c in range(8)], w=[pk.trk()])
                kb.op(V, lambda e, g4=g4, pk=pk: e.tensor_copy(out=bkt[:, g4 * 4:(g4 + 1) * 4, :], in_=pk[:, 0:256].rearrange("p (j f) -> p j f", f=64)), r=[pk.trk()], w=[bkt.trk()])

                def fv(e, g4=g4, pv=pv):
                    ins = None
                    for j in range(4):
                        blk = g4 * 4 + j
                        for kc in range(8):
                            ins = e.matmul(pv[:, j * 128:(j + 1) * 128], lhsT=hT[:, kc, blk * 128:(blk + 1) * 128], rhs=wv[:, kc, :], start=(kc == 0), stop=(kc == 7))
                    return ins
                kb.op("pe", fv, r=[wv.trk()] + [HT[c][g4] for c in range(8)], w=[pv.trk()])
                kb.op(A_, lambda e, g4=g4, pv=pv: e.activation(out=bvt[:, g4 * 4:(g4 + 1) * 4, :], in_=pv[:].rearrange("p (j f) -> p j f", f=128), func=AF.Copy), r=[pv.trk()], w=[bvt.trk()])
            for d in range(2):
                kb.dma("sp", Sd[d][:], dr["gla0"][l, d, h], Sd[d].trk(), w=[Sd[d].trk()])
                kb.op(A_, lambda e, d=d: e.activation(out=Sbd[d][:], in_=Sd[d][:], func=AF.Copy), r=[Sd[d].trk()], w=[Sbd[d].trk()])
            with ExitStack() as lt:
                def tn(nm, shape, dt=F32, n=4):
                    return [sb(lt, nm, shape, dt) for _ in range(n)]
                e1, spt, eD = tn("ge1", [128, 64], F32, 2), tn("gsp", [128, 64], BF16, 6), tn("ged", [128, 64], BF16, 6)
                eGT, enGT = tn("geg", [64, 128], F32, 6), tn("gen", [64, 128], F32, 6)
                qt, kt = tn("gqt", [64, 128], BF16), tn("gkt", [64, 128], BF16)
                kp, am = tn("gkp", [128, 64], BF16), tn("gam", [128, 128], BF16)

                def la_gen(d, i_):
                    trib = m_gt[0] if d == 0 else m_gt[1]
                    strict = m_gt[2] if d == 0 else m_gt[3]
                    b = i_ if d == 0 else NB - 1 - i_
                    i3 = d * 3 + i_ % 3
                    bsl = slice(b * 128, (b + 1) * 128)
                    bA = PS[2 * d + i_ % 2]
                    pla, pd, pgt = SubView(bA, 0), SubView(bA, 64), SubView(bA, 128)
                    kb.op("pe", lambda e: e.matmul(pla[:, 0:64], lhsT=rTa[d][:, bsl], rhs=wa2a[d][:, h * 64:(h + 1) * 64], start=True, stop=True),
                          r=[RT[d], wa2a[d].trk()], w=[pla.trk()])
                    yield
                    kb.op(A_, lambda e: e.activation(out=e1[d][:], in_=pla[:, 0:64], func=AF.Exp, scale=-1.0), r=[pla.trk()], w=[e1[d].trk()])
                    yield
                    kb.op(A_, lambda e: e.activation(out=spt[i3][:], in_=e1[d][:], func=AF.Ln, bias=cst[:, 1:2]), r=[e1[d].trk(), CONST], w=[spt[i3].trk()])
                    yield
                    kb.op("pe", lambda e: [e.matmul(pgt[0:64, 0:128], lhsT=spt[i3][:], rhs=trib[:], start=True, stop=True),
                                          e.matmul(pd[:, 0:64], lhsT=strict[:], rhs=spt[i3][:], start=True, stop=True)][-1], r=[spt[i3].trk(), CONST], w=[bA.trk()])
                    yield
                    kb.op(A_, lambda e: e.activation(out=eGT[i3][:], in_=pgt[0:64, 0:128], func=AF.Exp, scale=-1.0 / 16.0), r=[bA.trk()], w=[eGT[i3].trk()])
                    yield
                    kb.op(A_, lambda e: e.activation(out=enGT[i3][:], in_=pgt[0:64, 0:128], func=AF.Exp, scale=1.0 / 16.0), r=[bA.trk()], w=[enGT[i3].trk()])
                    yield
                    kb.op(A_, lambda e: e.activation(out=eD[i3][:], in_=pd[:, 0:64], func=AF.Exp, scale=-1.0 / 16.0), r=[bA.trk()], w=[eD[i3].trk()])
                    yield

                def prod_gen(d, i_):
                    tri = m_le if d == 0 else m_ge
                    b = i_ if d == 0 else NB - 1 - i_
                    i3 = d * 3 + i_ % 3
                    i2 = d * 2 + i_ % 2
                    bsl = slice(b * 128, (b + 1) * 128)
                    pat = SubView(PS[4 + d], 0)
                    kb.op(V, lambda e: e.tensor_tensor(out=qt[i2][:], in0=bqT[:, bsl], in1=eGT[i3][:], op=ALU.mult), r=[bqT.trk(), eGT[i3].trk()], w=[qt[i2].trk()])
                    yield
                    kb.op(P_, lambda e: e.tensor_tensor(out=kt[i2][:], in0=bkT[:, bsl], in1=enGT[i3][:], op=ALU.mult), r=[bkT.trk(), enGT[i3].trk()], w=[kt[i2].trk()])
                    yield
                    kb.op(P_, lambda e: e.tensor_tensor(out=kp[i2][:], in0=bkt[:, b, :], in1=eD[i3][:], op=ALU.mult), r=[bkt.trk(), eD[i3].trk()], w=[kp[i2].trk()])
                    yield
                    kb.op("pe", lambda e: e.matmul(pat[:, 0:128], lhsT=kt[i2][:], rhs=qt[i2][:], start=True, stop=True), r=[kt[i2].trk(), qt[i2].trk()], w=[pat.trk()])
                    yield
                    kb.op(V, lambda e: e.tensor_tensor(out=am[i2][:], in0=pat[:, 0:128], in1=tri[:], op=ALU.mult), r=[pat.trk(), CONST], w=[am[i2].trk()])
                    yield

                def fin_gen(d, i_):
                    edge = 127 if d == 0 else 0
                    S, Sb = Sd[d], Sbd[d]
                    b = i_ if d == 0 else NB - 1 - i_
                    first_touch = i_ < NB // 2
                    i2 = d * 2 + i_ % 2
                    i3 = d * 3 + i_ % 3
                    bsl = slice(b * 128, (b + 1) * 128)
                    bC = PS[6 + d]
                    pS, po = SubView(bC, 0), SubView(bC, 128)
                    kb.op("pe", lambda e: [e.matmul(po[:, 0:128], lhsT=bvt[:, b, :], rhs=am[i2][:], start=True, stop=False),
                                          e.matmul(po[:, 0:128], lhsT=Sb[:], rhs=qt[i2][:], start=False, stop=True),
                                          e.matmul(pS[0:64, 0:128], lhsT=kp[i2][:], rhs=bvt[:, b, :], start=True, stop=True)][-1],
                          r=[bvt.trk(), am[i2].trk(), Sb.trk(), qt[i2].trk(), kp[i2].trk()], w=[bC.trk()])
                    yield
                    kb.op(V, lambda e: e.scalar_tensor_tensor(out=S[:], in0=S[:], scalar=eGT[i3][:, edge:edge + 1], in1=pS[0:64, 0:128], op0=ALU.mult, op1=ALU.add),
                          r=[S.trk(), eGT[i3].trk(), bC.trk(), Sb.trk()], w=[S.trk()])
                    yield
                    seq_end = (b % 2 == 1) if d == 0 else (b % 2 == 0)
                    if seq_end:
                        sg_ = stg[d]
                        nst_[0] += 1
                        kb.op(P_, lambda e: e.tensor_copy(out=sg_[:], in_=S[:]), r=[S.trk()], w=[sg_.trk()])
                        yield
                        kb.dma("sp", dr["newgla"][l, b // 2, d, h], sg_[:], sg_.trk(), r=[sg_.trk()])
                        yield
                        kb.op(V, lambda e: e.tensor_scalar(out=S[:], in0=S[:], scalar1=flg[0:64, 0:1], scalar2=None, op0=ALU.mult), r=[S.trk(), CONST], w=[S.trk()])
                        yield
                    kb.op(A_, lambda e: e.activation(out=Sb[:], in_=S[:], func=AF.Copy), r=[S.trk()], w=[Sb.trk()])
                    yield
                    if first_touch:
                        kb.op(V, lambda e: e.tensor_copy(out=obuf[:, bsl], in_=po[:, 0:128]), r=[bC.trk()], w=[OBF[b]])
                    else:
                        kb.op(V, lambda e: e.tensor_tensor(out=obuf[:, bsl], in0=obuf[:, bsl], in1=po[:, 0:128], op=ALU.add), r=[bC.trk(), OBF[b]], w=[OBF[b]])
                    yield

                for i_ in range(NB + 2):
                    gens = []
                    if i_ >= 2:
                        gens += [fin_gen(0, i_ - 2), fin_gen(1, i_ - 2)]
                    if 1 <= i_ <= NB:
                        gens += [prod_gen(0, i_ - 1), prod_gen(1, i_ - 1)]
                    if i_ < NB:
                        gens += [la_gen(0, i_), la_gen(1, i_)]
                    while gens:
                        for g_ in list(gens):
                            try:
                                next(g_)
                            except StopIteration:
                                gens.remove(g_)
                kb.barrier()
            with ExitStack() as nt_:
                nsq = sb(nt_, "gnsq", [128, 512])
                nln = sb(nt_, "gnln", [128, 512])
                nrs = sb(nt_, "gnrs", [128, 512])
                ntm = sb(nt_, "gntm", [128, 512])
                for n in range(NT):
                    tsl = slice(n * 512, (n + 1) * 512)
                    ob_tr = [OBF[n * 4 + j] for j in range(4)]
                    ps = PS[n % 2]
                    kb.op(A_, lambda e, tsl=tsl: e.activation(out=nsq[:], in_=obuf[:, tsl], func=AF.Square), r=ob_tr, w=[nsq.trk()])
                    kb.op("pe", lambda e, ps=ps: e.matmul(ps[:], lhsT=ones_f[:], rhs=nsq[:], start=True, stop=True), r=[nsq.trk(), CONST], w=[ps.trk()])
                    rstd_from_ps((ps[:], ps.trk()), (nln[:], nln.trk(), 128), (nrs[:], nrs.trk()), 1.0 / 128.0)
                    kb.op(V, lambda e, tsl=tsl: e.scalar_tensor_tensor(out=ntm[:], in0=obuf[:, tsl], scalar=par[:, l, 84:85], in1=nrs[:], op0=ALU.mult, op1=ALU.mult),
                          r=ob_tr + [nrs.trk(), PAR], w=[ntm.trk()])
                    kb.op(P_, lambda e, tsl=tsl, h=h: e.tensor_tensor(out=oT[:, h, tsl], in0=ntm[:], in1=sgT[:, tsl], op=ALU.mult), r=[ntm.trk(), sgT.trk()], w=[OT[h][n]])
                kb.barrier()
        kb.barrier()
    dump("ob%d" % l, oT[:].rearrange("p c t -> p (c t)"), [128, 4 * T], OT[3][3], BF16)


def attn_phase(nc, kb, sb, dr, PS, l, hT, HT, oT, OT, par, PAR, cst, CONST, maskb, ident_f, ones_f, ones_b, blk64, rrot, make_rope,
               WIN, rstd_from_ps, proj_fm, load_cols, dump):
    V, P_, A_ = "dve", "pool", "act"
    with ExitStack() as st:
        ropec, ropes, RT = make_rope(st)
        wq = sb(st, "awq", [128, 8, 128], BF16)
        wk = sb(st, "awk", [128, 8, 128], BF16)
        wv = sb(st, "awv", [128, 8, 128], BF16)
        qP = [sb(st, "aqP", [128, T], BF16) for _ in range(2)]
        kP = [sb(st, "akP", [128, 256 + T], BF16) for _ in range(2)]
        ZQ = Trk()
        kb.op(P_, lambda e: e.memset(qP[0][64:128, :], 0.0), w=[ZQ])
        kb.op(P_, lambda e: e.memset(qP[1][0:64, :], 0.0), w=[ZQ])
        kb.op(P_, lambda e: e.memset(kP[0][64:128, :], 0.0), w=[ZQ])
        kb.op(P_, lambda e: e.memset(kP[1][0:64, :], 0.0), w=[ZQ])
        kb.dma("pool", qP[0][64:72, :], dr["mq"], ZQ, w=[ZQ])
        kb.dma("pool", qP[1][0:8, :], dr["mq"], ZQ, w=[ZQ])
        kb.dma("pool", kP[0][64:72, :], dr["mk"], ZQ, w=[ZQ])
        kb.dma("pool", kP[1][0:8, :], dr["mk"], ZQ, w=[ZQ])
        vt = sb(st, "avt", [128, 18, 128], BF16)
        ckst = sb(st, "ackst", [128, 2, 128])
        sq = sb(st, "asq", [128, 512])
        sqb = sb(st, "asqb", [128, 512], BF16)
        lnv = sb(st, "aln", [128, 512])
        lnv2 = sb(st, "aln2", [128, 512])
        rstd = sb(st, "ars", [128, 512])
        qn = sb(st, "aqn", [128, 512])
        t1 = lnv
        pT = [sb(st, "apT", [128, 512], BF16) for _ in range(4)]

        acc = sb(st, "aacc", [128, 512])
        rs = sq
        tmpo = qn
        kst = [sb(st, "akst", [128, 512]) for _ in range(1)]
        vst = kst
        QT = [Trk() for n in range(NT)]
        KT = [Trk() for n in range(NT + 1)]
        VT = [vt.trk(g) for g in range(5)]
        nst = 0
        npt = 0
        for h in range(4):
            load_cols(wq, WIN[l], C_AQ + h * 128, 128)
            load_cols(wk, WIN[l], C_AK + h * 128, 128)
            load_cols(wv, WIN[l], C_AV + h * 128, 128)
            kb.dma("sp", ckst[:], dr["ctxk"][l, :, h * 128:(h + 1) * 128].rearrange("(b p) f -> p b f", p=128), ckst.trk(), w=[ckst.trk()])
            pc = PS[2]
            kb.op("pe", lambda e: [e.transpose(out=pc[:, b * 128:(b + 1) * 128], in_=ckst[:, b, :], identity=ident_f[:]) for b in range(2)][-1],
                  r=[ckst.trk(), CONST], w=[pc.trk()])
            kb.op(V, lambda e: e.tensor_copy(out=kP[0][0:64, 0:256], in_=pc[0:64, 0:256]), r=[pc.trk(), ZQ], w=[KT[0]])
            kb.op(V, lambda e: e.tensor_copy(out=kP[1][64:128, 0:256], in_=pc[64:128, 0:256]), r=[pc.trk(), ZQ], w=[KT[0]])
            kb.dma("pool", vt[:, 0:2, :], dr["ctxv"][l, :, h * 128:(h + 1) * 128].rearrange("(b p) f -> p b f", p=128), VT[0], w=[VT[0]])
            items = [(is_k, n) for is_k in (False, True) for n in range(NT)]
            psb = [PS[0], PS[4]]
            sqbb = [sqb, pT[0]]
            lnvb = [lnv, lnv2]
            rstb = [rstd, acc]

            def st1(i):
                is_k, n = items[i]
                w_ = wk if is_k else wq
                ps, pss = psb[i % 2], PS[1]
                sb_, ln_, rs_ = sqbb[i % 2], lnvb[i % 2], rstb[i % 2]
                proj_fm(ps, w_, 0, 128, n)
                yield
                kb.op(A_, lambda e: e.activation(out=sb_[:], in_=ps[:], func=AF.Square), r=[ps.trk()], w=[sb_.trk()])
                yield
                kb.op("pe", lambda e: e.matmul(pss[:], lhsT=blk64[:], rhs=sb_[:], start=True, stop=True), r=[sb_.trk(), CONST], w=[pss.trk()])
                yield
                kb.op(A_, lambda e: e.activation(out=ln_[:], in_=pss[:], func=AF.Ln, scale=1.0 / 64.0, bias=cst[:, 0:1]), r=[pss.trk(), CONST], w=[ln_.trk()])
                yield
                kb.op(A_, lambda e: e.activation(out=rs_[:], in_=ln_[:], func=AF.Exp, scale=-0.5), r=[ln_.trk()], w=[rs_.trk()])
                yield

            def st2(i):
                is_k, n = items[i]
                gcol = 81 if is_k else 80
                tsl = slice(n * 512, (n + 1) * 512)
                ps, prq = psb[i % 2], PS[3]
                rs_, t1_ = rstb[i % 2], lnvb[i % 2]
                kb.op(V, lambda e: e.scalar_tensor_tensor(out=qn[:], in0=ps[:], scalar=par[:, l, gcol:gcol + 1], in1=rs_[:], op0=ALU.mult, op1=ALU.mult),
                      r=[ps.trk(), rs_.trk(), PAR], w=[qn.trk()])
                yield
                if is_k:
                    ko = kst[0]
                    pk = PS[2]
                    kb.op("pe", lambda e: [e.transpose(out=pk[:, j * 128:(j + 1) * 128], in_=qn[:, j * 128:(j + 1) * 128], identity=ident_f[:]) for j in range(4)][-1],
                          r=[qn.trk(), CONST], w=[pk.trk()])
                    yield
                    kb.op(A_, lambda e: e.activation(out=ko[:], in_=pk[:], func=AF.Copy), r=[pk.trk()], w=[ko.trk()])
                    yield
                    kb.dma("sp", dr["newk"][l, n * 512:(n + 1) * 512, h * 128:(h + 1) * 128].rearrange("(j p) f -> p j f", p=128),
                           ko[:].rearrange("p (j f) -> p j f", f=128), ko.trk(), r=[ko.trk()])
                    yield
                kb.op("pe", lambda e: e.matmul(prq[:], lhsT=rrot[:], rhs=qn[:], start=True, stop=True), r=[qn.trk(), CONST], w=[prq.trk()])
                yield
                kb.op(P_, lambda e: e.tensor_tensor(out=t1_[:], in0=qn[:], in1=ropec[:, tsl], op=ALU.mult), r=[qn.trk(), RT], w=[t1_.trk()])
                yield
                kb.op(V, lambda e: e.tensor_tensor(out=sq[:], in0=prq[:], in1=ropes[:, tsl], op=ALU.mult), r=[prq.trk(), RT], w=[sq.trk()])
                yield
                if is_k:
                    kb.op(V, lambda e: e.tensor_tensor(out=kP[0][0:64, 256 + n * 512:256 + (n + 1) * 512], in0=t1_[0:64, :], in1=sq[0:64, :], op=ALU.add), r=[t1_.trk(), sq.trk(), ZQ], w=[KT[n + 1]])
                    yield
                    kb.op(V, lambda e: e.tensor_tensor(out=kP[1][64:128, 256 + n * 512:256 + (n + 1) * 512], in0=t1_[64:128, :], in1=sq[64:128, :], op=ALU.add), r=[t1_.trk(), sq.trk(), ZQ], w=[KT[n + 1]])
                else:
                    kb.op(V, lambda e: e.tensor_tensor(out=qP[0][0:64, tsl], in0=t1_[0:64, :], in1=sq[0:64, :], op=ALU.add), r=[t1_.trk(), sq.trk(), ZQ], w=[QT[n]])
                    yield
                    kb.op(V, lambda e: e.tensor_tensor(out=qP[1][64:128, tsl], in0=t1_[64:128, :], in1=sq[64:128, :], op=ALU.add), r=[t1_.trk(), sq.trk(), ZQ], w=[QT[n]])
                yield

            for i in range(len(items) + 1):
                gens = []
                if i >= 1:
                    gens.append(st2(i - 1))
                if i < len(items):
                    gens.append(st1(i))
                while gens:
                    for g_ in list(gens):
                        try:
                            next(g_)
                        except StopIteration:
                            gens.remove(g_)
            for g4 in range(4):
                pv = PS[4]

                def fv(e, g4=g4, pv=pv):
                    ins = None
                    for j in range(4):
                        blk = g4 * 4 + j
                        for kc in range(8):
                            ins = e.matmul(pv[:, j * 128:(j + 1) * 128], lhsT=hT[:, kc, blk * 128:(blk + 1) * 128], rhs=wv[:, kc, :], start=(kc == 0), stop=(kc == 7))
                    return ins
                kb.op("pe", fv, r=[wv.trk()] + [HT[c][g4] for c in range(8)], w=[pv.trk()])
                kb.op(A_, lambda e, g4=g4, pv=pv: e.activation(out=vt[:, 2 + g4 * 4:2 + (g4 + 1) * 4, :], in_=pv[:].rearrange("p (j f) -> p j f", f=128), func=AF.Copy),
                      r=[pv.trk()], w=[VT[g4 + 1]])
                vo = vst[0]
                kb.op(V, lambda e, vo=vo, pv=pv: e.tensor_copy(out=vo[:], in_=pv[:]), r=[pv.trk()], w=[vo.trk()])
                kb.dma("sp", dr["newv"][l, g4 * 512:(g4 + 1) * 512, h * 128:(h + 1) * 128].rearrange("(j p) f -> p j f", p=128),
                       vo[:].rearrange("p (j f) -> p j f", f=128), vo.trk(), r=[vo.trk()])
            units = [(qt, m, u) for qt in range(NT) for m in range(2) for u in range(9)]

            def emit_S(i):
                qt, m, u = units[i]
                qsl = slice(qt * 512, (qt + 1) * 512)
                banks = [PS[(2 * i) % 4], PS[(2 * i + 1) % 4]]
                ktrs = list({KT[0] if kc < 2 else KT[1 + (kc - 2) // 4] for kc in (2 * u, 2 * u + 1)})

                def f(e):
                    ins = None
                    for j in range(2):
                        kc = 2 * u + j
                        ins = e.matmul(banks[j][:], lhsT=kP[m][:, kc * 128:(kc + 1) * 128], rhs=qP[m][:, qsl], start=True, stop=True)
                    return ins
                kb.op("pe", f, r=ktrs + [QT[qt], ZQ], w=[banks[0].trk(), banks[1].trk()])
                for j in range(2):
                    p_ = pT[(2 * i + j) % 4]
                    kb.op(A_, lambda e, p_=p_, j=j: e.activation(out=p_[:], in_=banks[j][:], func=AF.Exp, scale=0.125), r=[banks[j].trk()], w=[p_.trk()])

            def emit_PV(i):
                qt, m, u = units[i]
                qsl = slice(qt * 512, (qt + 1) * 512)
                g = (qt * 2 + m) % 2
                po, psm = PS[4 + 2 * g], PS[5 + 2 * g]
                ps_ = [pT[(2 * i) % 4], pT[(2 * i + 1) % 4]]
                vtrs = list({VT[0] if kc < 2 else VT[1 + (kc - 2) // 4] for kc in (2 * u, 2 * u + 1)})

                def f(e):
                    ins = None
                    for j in range(2):
                        kc = 2 * u + j
                        e.matmul(po[:], lhsT=vt[:, kc, :], rhs=ps_[j][:], start=(kc == 0), stop=(kc == 17))
                        ins = e.matmul(psm[:], lhsT=ones_b[:], rhs=ps_[j][:], start=(kc == 0), stop=(kc == 17))
                    return ins
                kb.op("pe", f, r=[ps_[0].trk(), ps_[1].trk(), CONST] + vtrs, w=[po.trk(), psm.trk()])
                if u != 8:
                    return
                kb.op(V, lambda e: e.reciprocal(out=rs[:], in_=psm[:]), r=[psm.trk()], w=[rs.trk()])
                if m == 0:
                    kb.op(V, lambda e: e.tensor_tensor(out=acc[:], in0=po[:], in1=rs[:], op=ALU.mult), r=[po.trk(), rs.trk()], w=[acc.trk()])
                    return
                kb.op(V, lambda e: e.tensor_tensor(out=tmpo[:], in0=po[:], in1=rs[:], op=ALU.mult), r=[po.trk(), rs.trk()], w=[tmpo.trk()])
                kb.op(V, lambda e: e.scalar_tensor_tensor(out=acc[:], in0=tmpo[:], scalar=par[:, l, 83:84], in1=acc[:], op0=ALU.mult, op1=ALU.add),
                      r=[tmpo.trk(), PAR], w=[acc.trk()])
                pn = PS[4 + 2 * g]
                kb.op(A_, lambda e: e.activation(out=sq[:], in_=acc[:], func=AF.Square), r=[acc.trk()], w=[sq.trk()])
                kb.op("pe", lambda e: e.matmul(pn[:], lhsT=ones_f[:], rhs=sq[:], start=True, stop=True), r=[sq.trk(), CONST], w=[pn.trk()])
                rstd_from_ps((pn[:], pn.trk()), (lnv[:], lnv.trk(), 128), (rstd[:], rstd.trk()), 1.0 / 128.0)
                kb.op(V, lambda e: e.scalar_tensor_tensor(out=oT[:, h, qsl], in0=acc[:], scalar=par[:, l, 82:83], in1=rstd[:], op0=ALU.mult, op1=ALU.mult),
                      r=[acc.trk(), rstd.trk(), PAR], w=[OT[h][qt]])

            LA = 1
            for i in range(len(units) + LA):
                if i < len(units):
                    emit_S(i)
                if i - LA >= 0:
                    emit_PV(i - LA)
        kb.barrier()
    dump("oa%d" % l, oT[:].rearrange("p c t -> p (c t)"), [128, 4 * T], OT[3][3], BF16)


_CACHE = {}


def make_in_maps(inputs):
    f32 = np.float32
    xs = np.ascontiguousarray(inputs["x_sample"], dtype=f32)
    xp = np.ascontiguousarray(inputs["x_prompt"], dtype=f32)
    maps = []
    wts = {n: np.ascontiguousarray(inputs[n], dtype=f32) for n, _ in W_SPECS}
    for core in range(8):
        m = dict(wts)
        if core < 4:
            b = core
            m["xin"] = xs[b]
            m["cond"] = np.ascontiguousarray(inputs["c"][b], dtype=f32)
            m["ctxk"] = np.ascontiguousarray(inputs["cache_diff_k"][b], dtype=f32).reshape(2, 256, 512)
            m["ctxv"] = np.ascontiguousarray(inputs["cache_diff_v"][b], dtype=f32).reshape(2, 256, 512)
            m["gla0"] = np.ascontiguousarray(inputs["state_gla"][b], dtype=f32)
            m["s5h0"] = np.ascontiguousarray(inputs["state_s5"][b], dtype=f32)
            m["flags"] = np.ones((128, 2), f32)
            m["mq"] = np.zeros((8, 2048), f32)
            m["mk"] = np.zeros((8, 2304), f32)
        else:
            q0 = (core - 4) * 8
            m["xin"] = xp[q0:q0 + 8].reshape(2048, 1024)
            m["cond"] = np.ascontiguousarray(inputs["c_ctx"], dtype=f32)
            m["ctxk"] = np.zeros((2, 256, 512), f32)
            m["ctxv"] = np.zeros((2, 256, 512), f32)
            m["gla0"] = np.zeros((2, 2, 4, 64, 128), f32)
            m["s5h0"] = np.zeros((2, 2, 2, 32, 64), f32)
            m["flags"] = np.zeros((128, 2), f32)
            mq = np.zeros((8, 2048), f32)
            mk = np.full((8, 2304), NEG * 8.0, f32)
            for j in range(8):
                mq[j, j * 256:(j + 1) * 256] = 1.0
                mk[j, 256 + j * 256:256 + (j + 1) * 256] = 0.0
            m["mq"] = mq
            m["mk"] = mk
        maps.append(m)
    return maps


def kernel(**inputs):
    if "nc" not in _CACHE:
        waited = build_program(want_waited=True)
        _CACHE["nc"] = build_program(needed=waited)[0]
    nc = _CACHE["nc"]
    maps = make_in_maps(inputs)
    res = run_bass_kernel_spmd(nc, maps, core_ids=list(range(8))).results
    f32 = np.float32
    y_sample = np.stack([res[b]["y"] for b in range(4)]).astype(f32)
    y_prompt = np.concatenate([res[c]["y"].reshape(8, 256, 1024) for c in range(4, 8)]).astype(f32)
    nk = np.concatenate([res[c]["newk"].reshape(2, 8, 256, 4, 2, 64).transpose(1, 0, 2, 3, 4, 5) for c in range(4, 8)]).astype(f32)
    nv = np.concatenate([res[c]["newv"].reshape(2, 8, 256, 4, 128).transpose(1, 0, 2, 3, 4) for c in range(4, 8)]).astype(f32)
    ng = np.concatenate([res[c]["newgla"].transpose(1, 0, 2, 3, 4, 5) for c in range(4, 8)]).astype(f32)
    n5 = np.concatenate([res[c]["news5"].transpose(1, 0, 2, 3, 4, 5) for c in range(4, 8)]).astype(f32)
    return (y_prompt, y_sample, nk, nv, ng, n5)
```

```python
import math
from contextlib import ExitStack

import numpy as np
import concourse.bass as bass
import concourse.mybir as mybir
from concourse.bass_utils import run_bass_kernel_spmd

F32 = mybir.dt.float32
BF16 = mybir.dt.bfloat16
I32 = mybir.dt.int32
ALU = mybir.AluOpType
AF = mybir.ActivationFunctionType

D = 1024
T = 2048
NT = 4
NB = 16
DEPTH = 2
IN_DIM = 6688
FFN = 2816
NJ = FFN // 128
C_AQ, C_AK, C_AV, C_BQ, C_BK, C_BV, C_BG, C_BR, C_CU, C_GZ = 0, 512, 1024, 1536, 1792, 2048, 2560, 3072, 3104, 3616
EPS = 1e-6
TWO_PI_LO = 6.28318
NEG = -30000.0

W_SPECS = [
    ("w_mod", (2, 1024, 6144)), ("b_mod", (2, 6144)), ("norm1_g", (2, 1024)), ("norm2_g", (2, 1024)),
    ("w_in", (2, 1024, 6688)), ("diff_qn_g", (2, 64)), ("diff_kn_g", (2, 64)), ("diff_lam", (2, 4, 64)),
    ("diff_subln_g", (2, 128)), ("gla_wa2", (2, 2, 16, 256)), ("gla_ba", (2, 2, 256)), ("gla_on_g", (2, 128)),
    ("s5_lam_re", (2, 2, 32, 64)), ("s5_lam_im", (2, 2, 32, 64)), ("s5_log_dt", (2, 2, 32)),
    ("s5_b_re", (2, 2, 32, 64, 16)), ("s5_b_im", (2, 2, 32, 64, 16)), ("s5_c_re", (2, 2, 32, 16, 64)),
    ("s5_c_im", (2, 2, 32, 16, 64)), ("s5_d", (2, 512)), ("s5_w_glu", (2, 512, 1024)), ("s5_b_glu", (2, 1024)),
    ("w_branch", (2, 3, 512, 1024)), ("w_out", (2, 1024, 1024)), ("w_ffn_gate", (2, 1024, 2816)),
    ("w_ffn_up", (2, 1024, 2816)), ("w_ffn_down", (2, 2816, 1024)),
]
IN_SPECS = [
    ("xin", (2048, 1024)), ("cond", (1024,)), ("ctxk", (2, 256, 512)), ("ctxv", (2, 256, 512)),
    ("gla0", (2, 2, 4, 64, 128)), ("s5h0", (2, 2, 2, 32, 64)), ("flags", (128, 2)), ("mq", (8, 2048)), ("mk", (8, 2304)),
]
OUT_SPECS = [
    ("y", (2048, 1024)), ("newk", (2, 2048, 512)), ("newv", (2, 2048, 512)),
    ("newgla", (2, 8, 2, 4, 64, 128)), ("news5", (2, 8, 2, 2, 32, 64)),
]


S5STOP = [None]


class _Stop(Exception):
    pass


def chk(tag):
    if S5STOP[0] == tag:
        raise _Stop()


class Trk:
    __slots__ = ("w", "r", "dsem", "x")

    def __init__(self):
        self.w = None
        self.r = {}
        self.dsem = None
        self.x = False


class KB:
    def _emit_wait(self, e, k, v):
        if k in self.dtot:
            self.engs[e].wait_ge(self.sem[k], v)
            return
        self.waited.add((k, v))
        if self.needed is None:
            self.engs[e].wait_ge(self.sem[k], v)
        else:
            self.engs[e].wait_ge(self.sem[k], self.vmap[k][v])

    def __init__(self, nc, es, needed=None):
        self.nc = nc
        self.es = es
        self.needed = needed
        self.waited = set()
        self.incs = {}
        self.vmap = {}
        self.engs = {"pe": nc.tensor, "act": nc.scalar, "dve": nc.vector, "pool": nc.gpsimd, "sp": nc.sync}
        self.sem = {}
        self.cnt = {}
        self.seen = {}
        for e in self.engs:
            self.sem[e] = es.enter_context(nc.semaphore("s_" + e))
            self.cnt[e] = 0
            self.seen[e] = {}
            self.incs[e] = 0
            self.vmap[e] = {}
        self.dtot = {}
        self.dfree = []
        self.dfree_sw = []
        self.swkeys = set()
        self.dassigned = []
        self.nds = 0
        self.uid = 0

    def name(self, p):
        self.uid += 1
        return "%s%d" % (p, self.uid)

    def _wait(self, e, r, w):
        need = {}
        for t in r:
            if t.w is not None:
                k, v = t.w
                if need.get(k, 0) < v:
                    need[k] = v
            if t.x:
                for k, v in t.r.items():
                    if k != e and need.get(k, 0) < v:
                        need[k] = v
        for t in w:
            if t.w is not None:
                k, v = t.w
                if need.get(k, 0) < v:
                    need[k] = v
            for k, v in t.r.items():
                if need.get(k, 0) < v:
                    need[k] = v
        seen = self.seen[e]
        for k, v in need.items():
            if k in self.dtot:
                v = self.dtot[k]
            if seen.get(k, 0) < v:
                self._emit_wait(e, k, v)
                seen[k] = v

    def _commit(self, ev, r, w):
        k, v = ev
        for t in r:
            if t.r.get(k, 0) < v:
                t.r[k] = v
        for t in w:
            t.w = ev
            t.r = {}

    def op(self, e, fn, r=(), w=()):
        self._wait(e, r, w)
        ins = fn(self.engs[e])
        self.cnt[e] += 1
        if self.needed is None or (e, self.cnt[e]) in self.needed:
            self.incs[e] += 1
            self.vmap[e][self.cnt[e]] = self.incs[e]
            ins.then_inc(self.sem[e], 1)
        self._commit((e, self.cnt[e]), r, w)

    def dma(self, q, out, in_, trk, r=(), w=(), **kw):
        self._wait(q, r, w)
        if trk.dsem is None:
            pool_ = self.dfree_sw if q == "pool" else self.dfree
            if pool_:
                key = pool_.pop()
            else:
                self.nds += 1
                key = "d%d" % self.nds
                self.sem[key] = self.es.enter_context(self.nc.semaphore("s_" + key))
                self.dtot[key] = 0
                if q == "pool":
                    self.swkeys.add(key)
            trk.dsem = key
            self.dassigned.append(trk)
        key = trk.dsem
        ins = self.engs[q].dma_start(out=out, in_=in_, **kw)
        ins.then_inc(self.sem[key], 16)
        self.dtot[key] += 16
        self._commit((key, self.dtot[key]), r, w)

    def barrier(self):
        for e in self.engs:
            for o in self.engs:
                if o != e and self.seen[e].get(o, 0) < self.cnt[o]:
                    self._emit_wait(e, o, self.cnt[o])
                    self.seen[e][o] = self.cnt[o]
            for k, v in self.dtot.items():
                if self.seen[e].get(k, 0) < v:
                    self.engs[e].wait_ge(self.sem[k], v)
                    self.seen[e][k] = v
        for t in self.dassigned:
            if t.dsem is not None:
                (self.dfree_sw if t.dsem in self.swkeys else self.dfree).append(t.dsem)
                t.dsem = None
        self.dassigned = []

    def final_wait(self):
        for k, v in self.dtot.items():
            if self.seen["sp"].get(k, 0) < v:
                self.nc.sync.wait_ge(self.sem[k], v)
        for o in self.engs:
            if o != "sp" and self.cnt[o] > 0:
                self._emit_wait("sp", o, self.cnt[o])


class Tile:
    def __init__(self, h):
        self.h = h
        self.t = {}

    def __getitem__(self, k):
        return self.h[k]

    def trk(self, key=0):
        t = self.t.get(key)
        if t is None:
            t = self.t[key] = Trk()
        return t


class SubView:
    def __init__(self, tile, c0):
        self.tile = tile
        self.c0 = c0

    def __getitem__(self, k):
        r, c = k
        return self.tile[r, slice(self.c0 + c.start, self.c0 + c.stop)]

    def trk(self):
        return self.tile.trk()


def build_program(stop_after=None, dbg_names=(), needed=None, want_waited=False):
    nc = bass.Bass("TRN2", target_bir_lowering=False)
    es = ExitStack()
    kb = KB(nc, es, needed)
    dr = {}
    for n, s in IN_SPECS + W_SPECS:
        dr[n] = nc.dram_tensor(n, list(s), F32, kind="ExternalInput").ap()
    for n, s in OUT_SPECS:
        dr[n] = nc.dram_tensor(n, list(s), F32, kind="ExternalOutput").ap()
    dbg_out = {}

    def sb(stack, name, shape, dt=F32):
        return Tile(stack.enter_context(nc.sbuf_tensor(kb.name(name), list(shape), dt)))

    PS = [Tile(es.enter_context(nc.psum_tensor("ps%d" % i, [128, 512], F32))) for i in range(8)]
    for t_ in PS:
        t_.trk().x = True

    def dump(name, tile_ap, shape, trk, dt=F32):
        if name not in dbg_names:
            return
        o = nc.dram_tensor("dbg_" + name, list(shape), dt, kind="ExternalOutput").ap()
        dbg_out[name] = shape
        kb.dma("sp", o, tile_ap, trk, r=[trk])
        kb.barrier()

    dump.names = dbg_names

    xT = sb(es, "xT", [128, 8, T])
    hT = sb(es, "hT", [128, 8, T], BF16)
    ident_f = sb(es, "identf", [128, 128])
    ones_f = sb(es, "onesf", [128, 128])
    ones_b = sb(es, "onesb", [128, 128], BF16)
    blk64 = sb(es, "blk64", [128, 128])
    m_le = sb(es, "mle", [128, 128])
    m_ge = sb(es, "mge", [128, 128])
    m_gt = sb(es, "mgt", [128, 128])
    m_lt = sb(es, "mlt", [128, 128])
    rrot = sb(es, "rrot", [128, 128])
    m_le_b = sb(es, "mleb", [128, 128], BF16)
    m_ge_b = sb(es, "mgeb", [128, 128], BF16)
    m_gt_b = sb(es, "mgtb", [128, 128], BF16)
    m_lt_b = sb(es, "mltb", [128, 128], BF16)
    blk64_b = sb(es, "blk64b", [128, 128], BF16)
    rrot_b = sb(es, "rrotb", [128, 128], BF16)
    mtz_f = sb(es, "mtzf", [128, 128])
    mtz_b = sb(es, "mtzb", [128, 128])
    oT = sb(es, "oT", [128, 4, T], BF16)
    OT = [[oT.trk((c, n)) for n in range(NT)] for c in range(4)]
    cst = sb(es, "cst", [128, 8])
    flg = sb(es, "flg", [128, 2])
    par = sb(es, "par", [128, DEPTH, 160])
    CONST = Trk()

    V, P_, A_ = "dve", "pool", "act"

    kb.op(P_, lambda e: e.memset(ident_f[:], 0.0), w=[CONST])
    kb.op(P_, lambda e: e.affine_select(out=ident_f[:], in_=ident_f[:], pattern=[[-1, 128]], compare_op=ALU.not_equal,
                                        fill=1.0, base=0, channel_multiplier=1), w=[CONST])
    kb.op(P_, lambda e: e.memset(ones_f[:], 1.0), w=[CONST])
    kb.op(V, lambda e: e.tensor_copy(out=ones_b[:], in_=ones_f[:]), r=[CONST], w=[CONST])
    for (mt, cmp, sgn) in ((m_le, ALU.is_ge, -1), (m_ge, ALU.is_ge, 1), (m_gt, ALU.is_gt, 1), (m_lt, ALU.is_gt, -1)):
        kb.op(P_, lambda e, mt=mt, cmp=cmp, sgn=sgn: e.affine_select(out=mt[:], in_=ones_f[:], pattern=[[-sgn, 128]], compare_op=cmp,
                                                                     fill=0.0, base=0, channel_multiplier=sgn), r=[CONST], w=[CONST])
    kb.op(P_, lambda e: e.memset(blk64[:], 0.0), w=[CONST])
    kb.op(P_, lambda e: e.memset(blk64[0:64, 0:64], 1.0), w=[CONST])
    kb.op(P_, lambda e: e.memset(blk64[64:128, 64:128], 1.0), w=[CONST])
    kb.op(P_, lambda e: e.memset(mtz_f[:], 0.0), w=[CONST])
    kb.op(P_, lambda e: e.memset(mtz_b[:], 0.0), w=[CONST])
    for s in range(4):
        kb.op(P_, lambda e, s=s: e.memset(mtz_f[32 * s:32 * s + 32, 32 * s:128], 1.0), w=[CONST])
        kb.op(P_, lambda e, s=s: e.memset(mtz_b[32 * s:32 * s + 32, 0:32 * s + 32], 1.0), w=[CONST])
    rv = rrot[:].rearrange("p (b h i) -> p b h i", h=2, i=16)
    iv = ident_f[:].rearrange("p (b h i) -> p b h i", h=2, i=16)
    kb.op(V, lambda e: e.tensor_scalar(out=rv[:, :, 0, :], in0=iv[:, :, 1, :], scalar1=-1.0, scalar2=None, op0=ALU.mult),
          r=[CONST], w=[CONST])
    kb.op(V, lambda e: e.tensor_copy(out=rv[:, :, 1, :], in_=iv[:, :, 0, :]), r=[CONST], w=[CONST])
    for (src_, dst_) in ((m_le, m_le_b), (m_ge, m_ge_b), (m_gt, m_gt_b), (m_lt, m_lt_b), (blk64, blk64_b), (rrot, rrot_b)):
        kb.op(V, lambda e, src_=src_, dst_=dst_: e.tensor_copy(out=dst_[:], in_=src_[:]), r=[CONST], w=[CONST])
    kb.op(P_, lambda e: e.memset(cst[:, 0:1], EPS), w=[CONST])
    kb.op(P_, lambda e: e.memset(cst[:, 1:2], 1.0), w=[CONST])
    kb.op(P_, lambda e: e.memset(cst[:, 2:3], 0.25), w=[CONST])
    kb.op(P_, lambda e: e.memset(cst[:, 3:4], 0.0), w=[CONST])
    kb.dma("sp", flg[:], dr["flags"], CONST, w=[CONST])

    def sincos_tmps(stack, shape):
        return (sb(stack, "sc_i", shape, I32), sb(stack, "sc_f", shape), sb(stack, "sc_q", shape), Trk())

    def sincos(tmps, out_c, out_s, turns, r, w):
        ti, tf, tq, tl = tmps
        kb.op(V, lambda e: e.tensor_copy(out=ti, in_=turns), r=r, w=tl)
        kb.op(V, lambda e: e.tensor_tensor(out=tf, in0=turns, in1=ti, op=ALU.subtract), r=list(r), w=tl)
        kb.op(A_, lambda e: e.activation(out=out_s, in_=tf, func=AF.Sin, scale=TWO_PI_LO), r=tl, w=w)
        kb.op(V, lambda e: e.tensor_scalar(out=tq, in0=turns, scalar1=0.25, scalar2=None, op0=ALU.add), r=r, w=tl)
        kb.op(V, lambda e: e.tensor_copy(out=ti, in_=tq), r=[], w=tl)
        kb.op(V, lambda e: e.tensor_tensor(out=tf, in0=tq, in1=ti, op=ALU.subtract), r=[], w=tl)
        kb.op(A_, lambda e: e.activation(out=out_c, in_=tf, func=AF.Sin, scale=TWO_PI_LO), r=tl, w=w)

    def make_rope(stack):
        ropec = sb(stack, "ropec", [128, T], BF16)
        ropes = sb(stack, "ropes", [128, T], BF16)
        RT = Trk()
        HF = T // 2
        with ExitStack() as st:
            pos_i = sb(st, "posi", [128, HF], I32)
            pos_f = sb(st, "posf", [128, HF])
            pid = sb(st, "pid", [128, 1], I32)
            pidf = sb(st, "pidf", [128, 1])
            invc = sb(st, "invc", [128, 1])
            tc_ = sb(st, "rc", [128, HF])
            ts_ = sb(st, "rs", [128, HF])
            tm = sincos_tmps(st, [128, HF])
            t1 = Trk()
            kb.op(P_, lambda e: e.iota(pid[:], pattern=[[0, 1]], base=0, channel_multiplier=1), w=[t1])
            kb.op(V, lambda e: e.tensor_single_scalar(out=pid[:], in_=pid[:], scalar=15, op=ALU.bitwise_and), r=[t1], w=[t1])
            kb.op(V, lambda e: e.tensor_copy(out=pidf[:], in_=pid[:]), r=[t1], w=[t1])
            kb.op(A_, lambda e: e.activation(out=invc[:], in_=pidf[:], func=AF.Exp, scale=-math.log(10000.0) / 16.0), r=[t1], w=[t1])
            kb.op(V, lambda e: e.tensor_scalar(out=invc[:], in0=invc[:], scalar1=flg[:, 1:2], scalar2=None, op0=ALU.mult),
                  r=[t1, CONST], w=[t1])
            kb.op(V, lambda e: e.tensor_scalar(out=invc[:], in0=invc[:], scalar1=1.0 / (2 * math.pi), scalar2=None, op0=ALU.mult), r=[t1], w=[t1])
            for hf in range(2):
                for b in range(4):
                    pat, base = ([[1, 16], [0, 64]], hf * 16) if b % 2 == 0 else ([[0, 16], [1, 64]], 0)
                    kb.op(P_, lambda e, b=b, pat=pat, base=base: e.iota(pos_i[32 * b:32 * b + 32, :].rearrange("p (r c) -> p r c", c=64), pattern=pat,
                                                                        base=base, channel_multiplier=0), w=[t1])
                kb.op(V, lambda e: e.tensor_copy(out=pos_f[:], in_=pos_i[:]), r=[t1], w=[t1])
                kb.op(V, lambda e: e.tensor_scalar(out=pos_f[:], in0=pos_f[:], scalar1=invc[:, 0:1], scalar2=None, op0=ALU.mult), r=[t1], w=[t1])
                t2 = Trk()
                sincos((tm[0][:], tm[1][:], tm[2][:], [tm[3]]), tc_[:], ts_[:], pos_f[:], [t1], [t2])
                kb.op(V, lambda e, hf=hf: e.tensor_copy(out=ropec[:, hf * HF:(hf + 1) * HF], in_=tc_[:]), r=[t2], w=[RT])
                kb.op(V, lambda e, hf=hf: e.tensor_copy(out=ropes[:, hf * HF:(hf + 1) * HF], in_=ts_[:]), r=[t2], w=[RT])
                kb.op(V, lambda e: e.tensor_copy(out=pidf[:], in_=pidf[:]), r=[t2, RT], w=[t1])
            kb.barrier()
        return ropec, ropes, RT

    XT = [[xT.trk((c, n)) for n in range(NT)] for c in range(8)]
    HT = [[hT.trk((c, n)) for n in range(NT)] for c in range(8)]
    with ExitStack() as st:
        xs = [sb(st, "xs", [128, 1024]) for _ in range(2)]
        for b in range(NB):
            s_ = xs[b % 2]
            kb.dma("sp", s_[:], dr["xin"][b * 128:(b + 1) * 128, :], s_.trk(), w=[s_.trk()])
            for half in range(2):
                ps = PS[(2 * b + half) % 4]
                kb.op("pe", lambda e, ps=ps, s_=s_, half=half: [e.transpose(out=ps[:, j * 128:(j + 1) * 128], in_=s_[:, (half * 4 + j) * 128:(half * 4 + j + 1) * 128],
                                                                             identity=ident_f[:]) for j in range(4)][-1],
                      r=[s_.trk(), CONST], w=[ps.trk()])
                eng = A_ if half == 0 else V
                outv = xT[:, half * 4:half * 4 + 4, b * 128:(b + 1) * 128]
                inv_ = ps[:].rearrange("p (j t) -> p j t", t=128)
                wl = [XT[half * 4 + j][b // 4] for j in range(4)]
                if eng == A_:
                    kb.op(A_, lambda e, outv=outv, inv_=inv_: e.activation(out=outv, in_=inv_, func=AF.Copy), r=[ps.trk()], w=wl)
                else:
                    kb.op(V, lambda e, outv=outv, inv_=inv_: e.tensor_copy(out=outv, in_=inv_), r=[ps.trk()], w=wl)
        kb.barrier()

    PAR = Trk()
    with ExitStack() as st:
        condT = sb(st, "condT", [128, 8])
        scond = sb(st, "scond", [128, 8], BF16)
        bmT = sb(st, "bmT", [128, 48])
        wm = [sb(st, "wm", [128, 8, 512], BF16) for _ in range(2)]
        tcnd = Trk()
        kb.dma("sp", condT[:], dr["cond"].rearrange("(c p) -> p c", p=128), tcnd, w=[tcnd], allow_slow_non_contiguous=True)
        kb.op(A_, lambda e: e.activation(out=scond[:], in_=condT[:], func=AF.Silu), r=[tcnd], w=[tcnd])
        for l in range(DEPTH):
            tb = Trk()
            kb.dma("sp", bmT[:], dr["b_mod"][l].rearrange("(c p) -> p c", p=128), tb, w=[tb], allow_slow_non_contiguous=True)
            psm = PS[4 + l]
            wv = dr["w_mod"][l].rearrange("(kc p) n -> p kc n", p=128)
            for cb in range(12):
                wt = wm[cb % 2]
                kb.dma("pool", wt[:], wv[:, :, cb * 512:(cb + 1) * 512], wt.trk(), w=[wt.trk()])

                def mm(e, wt=wt, cb=cb, psm=psm):
                    ins = None
                    for j in range(4):
                        col = cb * 4 + j
                        for kc in range(8):
                            ins = e.matmul(psm[:, col:col + 1], lhsT=wt[:, kc, j * 128:(j + 1) * 128], rhs=scond[:, kc:kc + 1],
                                           start=(kc == 0), stop=(kc == 7))
                    return ins
                kb.op("pe", mm, r=[wt.trk(), tcnd], w=[psm.trk()])
            kb.op(V, lambda e, l=l, psm=psm: e.tensor_tensor(out=par[:, l, 0:48], in0=psm[:, 0:48], in1=bmT[:], op=ALU.add),
                  r=[psm.trk(), tb], w=[PAR])
            tv = Trk()
            kb.dma("sp", par[:, l, 64:72], dr["norm1_g"][l].rearrange("(c p) -> p c", p=128), tv, w=[PAR], allow_slow_non_contiguous=True)
            kb.dma("sp", par[:, l, 72:80], dr["norm2_g"][l].rearrange("(c p) -> p c", p=128), tv, w=[PAR], allow_slow_non_contiguous=True)
            for mth in range(2):
                kb.dma("sp", par[64 * mth:64 * mth + 64, l, 80:81], dr["diff_qn_g"][l].rearrange("(p o) -> p o", o=1), tv, w=[PAR], allow_slow_non_contiguous=True)
                kb.dma("sp", par[64 * mth:64 * mth + 64, l, 81:82], dr["diff_kn_g"][l].rearrange("(p o) -> p o", o=1), tv, w=[PAR], allow_slow_non_contiguous=True)
            kb.dma("sp", par[:, l, 82:83], dr["diff_subln_g"][l].rearrange("(p o) -> p o", o=1), tv, w=[PAR], allow_slow_non_contiguous=True)
            kb.dma("sp", par[:, l, 84:85], dr["gla_on_g"][l].rearrange("(p o) -> p o", o=1), tv, w=[PAR], allow_slow_non_contiguous=True)
            kb.dma("sp", par[:, l, 88:96], dr["s5_b_glu"][l].rearrange("(c p) -> p c", p=128), tv, w=[PAR], allow_slow_non_contiguous=True)
            for s in range(4):
                kb.dma("sp", par[32 * s:32 * s + 32, l, 96:112], dr["s5_d"][l].rearrange("(gp jj) -> jj gp", jj=32), tv, w=[PAR], allow_slow_non_contiguous=True)
            lamt = sb(st, "lamt", [128, 256])
            lamp = sb(st, "lamp", [128, 128])
            lams = sb(st, "lams", [128, 2])
            kb.dma("sp", lamt[:], dr["diff_lam"][l].rearrange("a b -> (a b)").rearrange("(o n) -> o n", o=1).partition_broadcast(128), tv, w=[tv])
            lam_init = 0.8 - 0.6 * math.exp(-0.3 * l)
            kb.op(V, lambda e: e.tensor_tensor(out=lamp[:].rearrange("p (a b) -> p a b", b=64), in0=lamt[:].rearrange("p (a t b) -> p a t b", t=2, b=64)[:, :, 0, :],
                                               in1=lamt[:].rearrange("p (a t b) -> p a t b", t=2, b=64)[:, :, 1, :], op=ALU.mult), r=[tv], w=[tv])
            kb.op(V, lambda e: e.reduce_sum(out=lams[:], in_=lamp[:].rearrange("p (a b) -> p a b", b=64), axis=mybir.AxisListType.X), r=[tv], w=[tv])
            kb.op(A_, lambda e: e.activation(out=lams[:], in_=lams[:], func=AF.Exp), r=[tv], w=[tv])
            kb.op(V, lambda e, l=l, lam_init=lam_init: e.scalar_tensor_tensor(out=par[:, l, 83:84], in0=lams[:, 1:2], scalar=-lam_init, in1=lams[:, 0:1],
                                                                             op0=ALU.add, op1=ALU.subtract), r=[tv], w=[PAR])
            kb.op(V, lambda e, l=l, lam_init=lam_init: e.tensor_scalar(out=par[:, l, 82:83], in0=par[:, l, 82:83], scalar1=1.0 - lam_init, scalar2=None, op0=ALU.mult),
                  r=[tv, PAR], w=[PAR])
            kb.op(V, lambda e, l=l: e.scalar_tensor_tensor(out=par[:, l, 48:56], in0=par[:, l, 8:16], scalar=1.0, in1=par[:, l, 64:72], op0=ALU.add, op1=ALU.mult),
                  r=[PAR, tv], w=[PAR])
            kb.op(V, lambda e, l=l: e.scalar_tensor_tensor(out=par[:, l, 56:64], in0=par[:, l, 32:40], scalar=1.0, in1=par[:, l, 72:80], op0=ALU.add, op1=ALU.mult),
                  r=[PAR, tv], w=[PAR])
        kb.barrier()
    dump("par", par[:].rearrange("p l c -> p (l c)"), [128, DEPTH * 160], PAR)

    def rstd_from_ps(ps, tmp, out, scale):
        kb.op(A_, lambda e: e.activation(out=tmp[0], in_=ps[0], func=AF.Ln, scale=scale, bias=cst[0:tmp[2], 0:1]), r=[ps[1], CONST], w=[tmp[1]])
        kb.op(A_, lambda e: e.activation(out=out[0], in_=tmp[0], func=AF.Exp, scale=-0.5), r=[tmp[1]], w=[out[1]])

    def norm(l, scol, shcol):
        with ExitStack() as st:
            sq = [sb(st, "nsq", [128, 512]) for _ in range(4)]
            lnv = sb(st, "nln", [128, 512])
            rstd = [sb(st, "nrs", [128, 512]) for _ in range(2)]
            tmp = [sb(st, "ntm", [128, 512]) for _ in range(2)]

            def stA(n):
                tsl = slice(n * 512, (n + 1) * 512)
                ps = PS[n % 2]
                for c in range(8):
                    q = sq[c % 4]
                    if c % 2 == 0:
                        kb.op(A_, lambda e: e.activation(out=q[:], in_=xT[:, c, tsl], func=AF.Square), r=[XT[c][n]], w=[q.trk()])
                    else:
                        kb.op(V, lambda e: e.tensor_tensor(out=q[:], in0=xT[:, c, tsl], in1=xT[:, c, tsl], op=ALU.mult), r=[XT[c][n]], w=[q.trk()])
                    yield
                    kb.op("pe", lambda e: e.matmul(ps[:], lhsT=ones_f[:], rhs=q[:], start=(c == 0), stop=(c == 7)), r=[q.trk(), CONST], w=[ps.trk()])
                    yield
                rs = rstd[n % 2]
                kb.op(A_, lambda e: e.activation(out=lnv[:], in_=ps[:], func=AF.Ln, scale=1.0 / D, bias=cst[:, 0:1]), r=[ps.trk(), CONST], w=[lnv.trk()])
                yield
                kb.op(A_, lambda e: e.activation(out=rs[:], in_=lnv[:], func=AF.Exp, scale=-0.5), r=[lnv.trk()], w=[rs.trk()])
                yield

            def stB(n):
                tsl = slice(n * 512, (n + 1) * 512)
                rs = rstd[n % 2]
                for c in range(8):
                    tm = tmp[c % 2]
                    kb.op(V, lambda e: e.scalar_tensor_tensor(out=tm[:], in0=xT[:, c, tsl], scalar=par[:, l, scol + c:scol + c + 1], in1=rs[:],
                                                              op0=ALU.mult, op1=ALU.mult), r=[XT[c][n], rs.trk(), PAR], w=[tm.trk()])
                    yield
                    kb.op(A_, lambda e: e.activation(out=hT[:, c, tsl], in_=tm[:], func=AF.Identity, bias=par[:, l, shcol + c:shcol + c + 1]),
                          r=[tm.trk(), PAR], w=[HT[c][n]])
                    yield

            for i in range(NT + 1):
                gens = []
                if i >= 1:
                    gens.append(stB(i - 1))
                if i < NT:
                    gens.append(stA(i))
                while gens:
                    for g_ in list(gens):
                        try:
                            next(g_)
                        except StopIteration:
                            gens.remove(g_)
            kb.barrier()

    WIN = [dr["w_in"][l].rearrange("(kc p) n -> p kc n", p=128) for l in range(DEPTH)]

    def load_cols(wt, src3, c0, nc_):
        kb.dma("pool", wt[:, :, 0:nc_], src3[:, :, c0:c0 + nc_], wt.trk(), w=[wt.trk()])

    def proj_fm(ps, wt, m0, m, n, rows=None, extra_r=()):
        tsl = slice(n * 512, (n + 1) * 512)

        def f(e):
            ins = None
            for kc in range(8):
                ins = e.matmul(ps[0:m, :], lhsT=wt[:, kc, m0:m0 + m], rhs=hT[:, kc, tsl], start=(kc == 0), stop=(kc == 7))
            return ins
        kb.op("pe", f, r=[wt.trk()] + [HT[c][n] for c in range(8)] + list(extra_r), w=[ps.trk()])

    for l in range(DEPTH):
        lam_init = 0.8 - 0.6 * math.exp(-0.3 * l)
        norm(l, 48, 0)
        dump("h%d" % l, hT[:].rearrange("p c t -> p (c t)"), [128, 8 * T], HT[7][3], BF16)
        if stop_after == "norm1" and l == 0:
            break
        with ExitStack() as lst:
            s5_phase(nc, kb, sb, dr, PS, l, hT, HT, oT, OT, par, PAR, cst, CONST, flg, ident_f, mtz_f, mtz_b, sincos, sincos_tmps, WIN, dump)
            if S5STOP[0] is not None:
                break
            merged = sb(lst, "merged", [128, 8, T], BF16)
            MG = [[merged.trk((c, n)) for n in range(NT)] for c in range(8)]
            def merge_branch(r, first):
                with ExitStack() as st:
                    wg = [sb(st, "wg", [128, 8, 128], BF16) for _ in range(2)]
                    wb = [sb(st, "wb", [128, 4, 128], BF16) for _ in range(2)]
                    sg = [sb(st, "sg", [128, 512]) for _ in range(2)]
                    tm = [sb(st, "mtm", [128, 512]) for _ in range(2)]
                    wbv = dr["w_branch"][l, r].rearrange("(kc p) n -> p kc n", p=128)
                    it = 0
                    for dc in range(8):
                        g_ = wg[dc % 2]
                        b_ = wb[dc % 2]
                        load_cols(g_, WIN[l], C_GZ + r * 1024 + dc * 128, 128)
                        kb.dma("pool", b_[:], wbv[:, :, dc * 128:(dc + 1) * 128], b_.trk(), w=[b_.trk()])
                        for n in range(NT):
                            tsl = slice(n * 512, (n + 1) * 512)
                            pg = PS[(it * 2) % 8]
                            pb = PS[(it * 2 + 1) % 8]
                            it += 1
                            proj_fm(pg, g_, 0, 128, n)

                            def f(e, pb=pb, b_=b_, tsl=tsl):
                                ins = None
                                for kc in range(4):
                                    ins = e.matmul(pb[:], lhsT=b_[:, kc, :], rhs=oT[:, kc, tsl], start=(kc == 0), stop=(kc == 3))
                                return ins
                            kb.op("pe", f, r=[b_.trk()] + [OT[kc][n] for kc in range(4)], w=[pb.trk()])
                            s_ = sg[it % 2]
                            kb.op(A_, lambda e, s_=s_, pg=pg: e.activation(out=s_[:], in_=pg[:], func=AF.Sigmoid), r=[pg.trk()], w=[s_.trk()])
                            if first:
                                kb.op(V, lambda e, s_=s_, pb=pb, dc=dc, tsl=tsl: e.tensor_tensor(out=merged[:, dc, tsl], in0=pb[:], in1=s_[:], op=ALU.mult),
                                      r=[pb.trk(), s_.trk()], w=[MG[dc][n]])
                            else:
                                t_ = tm[it % 2]
                                kb.op(V, lambda e, s_=s_, pb=pb, t_=t_: e.tensor_tensor(out=t_[:], in0=pb[:], in1=s_[:], op=ALU.mult),
                                      r=[pb.trk(), s_.trk()], w=[t_.trk()])
                                kb.op(V, lambda e, t_=t_, dc=dc, tsl=tsl: e.tensor_tensor(out=merged[:, dc, tsl], in0=merged[:, dc, tsl], in1=t_[:], op=ALU.add),
                                      r=[t_.trk(), MG[dc][n]], w=[MG[dc][n]])
                    kb.barrier()

            merge_branch(2, True)
            dump("mergedc%d" % l, merged[:].rearrange("p c t -> p (c t)"), [128, 8 * T], MG[7][3], BF16)
            if stop_after == "s5":
                break
            gla_phase(nc, kb, sb, dr, PS, l, hT, HT, oT, OT, par, PAR, cst, CONST, flg, ones_f, m_le, m_ge, (m_le_b, m_ge_b, m_gt_b, m_lt_b), None, WIN, rstd_from_ps, proj_fm, load_cols, dump)
            merge_branch(1, False)
            if stop_after == "gla":
                break
            attn_phase(nc, kb, sb, dr, PS, l, hT, HT, oT, OT, par, PAR, cst, CONST, None, ident_f, ones_f, ones_b, blk64_b, rrot, make_rope,
                       WIN, rstd_from_ps, proj_fm, load_cols, dump)
            merge_branch(0, False)
            dump("merged%d" % l, merged[:].rearrange("p c t -> p (c t)"), [128, 8 * T], MG[7][3], BF16)
            with ExitStack() as st:
                wo = [sb(st, "wo", [128, 8, 128], BF16) for _ in range(2)]
                wov = dr["w_out"][l].rearrange("(kc p) n -> p kc n", p=128)
                it = 0
                for dc in range(8):
                    w_ = wo[dc % 2]
                    kb.dma("pool", w_[:], wov[:, :, dc * 128:(dc + 1) * 128], w_.trk(), w=[w_.trk()])
                    for n in range(NT):
                        tsl = slice(n * 512, (n + 1) * 512)
                        ps = PS[it % 8]
                        it += 1

                        def f(e, ps=ps, w_=w_, tsl=tsl):
                            ins = None
                            for kc in range(8):
                                ins = e.matmul(ps[:], lhsT=w_[:, kc, :], rhs=merged[:, kc, tsl], start=(kc == 0), stop=(kc == 7))
                            return ins
                        kb.op("pe", f, r=[w_.trk()] + [MG[kc][n] for kc in range(8)], w=[ps.trk()])
                        kb.op(V, lambda e, ps=ps, dc=dc, tsl=tsl: e.scalar_tensor_tensor(out=xT[:, dc, tsl], in0=ps[:], scalar=par[:, l, 16 + dc:17 + dc], in1=xT[:, dc, tsl],
                                                                                       op0=ALU.mult, op1=ALU.add), r=[ps.trk(), PAR, XT[dc][n]], w=[XT[dc][n]])
                kb.barrier()
        dump("xmid%d" % l, xT[:].rearrange("p c t -> p (c t)"), [128, 8 * T], XT[7][3])
        norm(l, 56, 24)
        with ExitStack() as st:
            aT = sb(st, "aT", [128, NJ, 1024], BF16)
            AT = [[aT.trk((j, n)) for n in range(2)] for j in range(NJ)]
            wgt = [sb(st, "fwg", [128, 8, 128], BF16) for _ in range(2)]
            wut = [sb(st, "fwu", [128, 8, 128], BF16) for _ in range(2)]
            wdt = [sb(st, "fwd", [128, NJ, 128], BF16) for _ in range(2)]
            sl = [sb(st, "fsl", [128, 512]) for _ in range(2)]
            wgv = dr["w_ffn_gate"][l].rearrange("(kc p) n -> p kc n", p=128)
            wuv = dr["w_ffn_up"][l].rearrange("(kc p) n -> p kc n", p=128)
            wdv = dr["w_ffn_down"][l].rearrange("(j p) n -> p j n", p=128)
            it = 0
            for half in range(2):
                for j in range(NJ):
                    g_ = wgt[j % 2]
                    u_ = wut[j % 2]
                    kb.dma("pool", g_[:], wgv[:, :, j * 128:(j + 1) * 128], g_.trk(), w=[g_.trk()])
                    kb.dma("pool", u_[:], wuv[:, :, j * 128:(j + 1) * 128], u_.trk(), w=[u_.trk()])
                    for nl in range(2):
                        n = half * 2 + nl
                        pg = PS[(it * 2) % 8]
                        pu = PS[(it * 2 + 1) % 8]
                        it += 1
                        proj_fm(pg, g_, 0, 128, n)
                        proj_fm(pu, u_, 0, 128, n)
                        s_ = sl[it % 2]
                        kb.op(A_, lambda e, s_=s_, pg=pg: e.activation(out=s_[:], in_=pg[:], func=AF.Silu), r=[pg.trk()], w=[s_.trk()])
                        kb.op(V, lambda e, s_=s_, pu=pu, j=j, nl=nl: e.tensor_tensor(out=aT[:, j, nl * 512:(nl + 1) * 512], in0=pu[:], in1=s_[:], op=ALU.mult),
                              r=[pu.trk(), s_.trk()], w=[AT[j][nl]])
                for dc in range(8):
                    w_ = wdt[dc % 2]
                    kb.dma("pool", w_[:], wdv[:, :, dc * 128:(dc + 1) * 128], w_.trk(), w=[w_.trk()])
                    for nl in range(2):
                        n = half * 2 + nl
                        tsl = slice(n * 512, (n + 1) * 512)
                        ps = PS[it % 8]
                        it += 1

                        def f(e, ps=ps, w_=w_, nl=nl):
                            ins = None
                            for j in range(NJ):
                                ins = e.matmul(ps[:], lhsT=w_[:, j, :], rhs=aT[:, j, nl * 512:(nl + 1) * 512], start=(j == 0), stop=(j == NJ - 1))
                            return ins
                        kb.op("pe", f, r=[w_.trk()] + [AT[j][nl] for j in range(NJ)], w=[ps.trk()])
                        kb.op(V, lambda e, ps=ps, dc=dc, tsl=tsl: e.scalar_tensor_tensor(out=xT[:, dc, tsl], in0=ps[:], scalar=par[:, l, 40 + dc:41 + dc], in1=xT[:, dc, tsl],
                                                                                       op0=ALU.mult, op1=ALU.add), r=[ps.trk(), PAR, XT[dc][n]], w=[XT[dc][n]])
            kb.barrier()
        dump("xout%d" % l, xT[:].rearrange("p c t -> p (c t)"), [128, 8 * T], XT[7][3])

    with ExitStack() as st:
        ys = [sb(st, "ys", [128, 1024]) for _ in range(2)]
        for b in range(NB):
            s_ = ys[b % 2]
            for half in range(2):
                ps = PS[(2 * b + half) % 4]
                kb.op("pe", lambda e, ps=ps, b=b, half=half: [e.transpose(out=ps[:, j * 128:(j + 1) * 128], in_=xT[:, half * 4 + j, b * 128:(b + 1) * 128],
                                                                         identity=ident_f[:]) for j in range(4)][-1],
                      r=[XT[half * 4 + j][b // 4] for j in range(4)] + [CONST], w=[ps.trk()])
                if half == 0:
                    kb.op(A_, lambda e, ps=ps, s_=s_: e.activation(out=s_[:, 0:512], in_=ps[:], func=AF.Copy), r=[ps.trk()], w=[s_.trk()])
                else:
                    kb.op(V, lambda e, ps=ps, s_=s_: e.tensor_copy(out=s_[:, 512:1024], in_=ps[:]), r=[ps.trk()], w=[s_.trk()])
            kb.dma("sp", dr["y"][b * 128:(b + 1) * 128, :], s_[:], s_.trk(), r=[s_.trk()])
        kb.barrier()
    kb.final_wait()
    es.close()
    if want_waited:
        return kb.waited
    return nc, dbg_out


def s5_phase(nc, kb, sb, dr, PS, l, hT, HT, ocT, OC, par, PAR, cst, CONST, flg, ident_f, mtz_f, mtz_b, sincos, sincos_tmps, WIN, dump):
    V, P_, A_ = "dve", "pool", "act"
    with ExitStack() as st:
        ycT = sb(st, "ycT", [128, 16, 512], BF16)
        YC = [ycT.trk(g) for g in range(16)]
        hfin = sb(st, "hfin", [128, 512])
        HF = hfin.trk()
        ciota = sb(st, "ciota", [128, 512])
        ones512 = sb(st, "ones512", [128, 512])
        ci_i = sb(st, "cii", [128, 512], I32)
        C5 = Trk()
        kb.op(P_, lambda e: e.iota(ci_i[:], pattern=[[1, 512]], base=0, channel_multiplier=0), w=[C5])
        kb.op(V, lambda e: e.tensor_copy(out=ciota[:], in_=ci_i[:]), r=[C5], w=[C5])
        kb.op(P_, lambda e: e.memset(ones512[:], 1.0), w=[C5])
        PR = Trk()
        NPR = 40
        pr = [sb(st, "s5pr", [128, NPR, 16]) for _ in range(2)]
        bc = [[sb(st, "bcm", [128, 16, 16]) for _ in range(2)] for _ in range(2)]
        cc = [[sb(st, "ccm", [128, 16, 16]) for _ in range(2)] for _ in range(2)]
        pwL = [[sb(st, "pwL", [128, 16, 4]) for _ in range(2)] for _ in range(2)]
        pwR = [[sb(st, "pwR", [128, 16, 4]) for _ in range(2)] for _ in range(2)]
        (LR, LI, LDT, DTt, LRDT, TH, MAG, TURN, CS, SN, AR, AI, A2R, A2I, A3R, A3I, A4R, A4I, IM2, Q1R, Q1I, Q2R, Q2I, Q3R, Q3I,
         DEN, AM1, FR, FI, R4, PHT, T0, T1, T2, H0R, H0I, ONE, ZERO, PXR, PXI) = range(40)
        with ExitStack() as st2:
            braw = [sb(st2, "braw", [128, 16, 16]) for _ in range(2)]
            btmp = sb(st2, "btmp", [128, 16, 16])
            craw = sb(st2, "craw", [16, 2048])
            tms = sincos_tmps(st2, [128, 16])
            for d in range(2):
                p = pr[d]

                def S(i, p=p):
                    return p[:, i, :]

                def tt(o, a, b, op, S=S):
                    kb.op(V, lambda e: e.tensor_tensor(out=S(o), in0=S(a), in1=S(b), op=op), r=[PR], w=[PR])

                def ts(o, a, s1, op0, S=S):
                    kb.op(V, lambda e: e.tensor_scalar(out=S(o), in0=S(a), scalar1=s1, scalar2=None, op0=op0), r=[PR], w=[PR])

                def act(o, a, func, scale=1.0, S=S):
                    kb.op(A_, lambda e: e.activation(out=S(o), in_=S(a), func=func, scale=scale), r=[PR], w=[PR])

                def cmul(o_r, o_i, a_r, a_i, b_r, b_i):
                    tt(T0, a_i, b_i, ALU.mult)
                    tt(T1, a_r, b_r, ALU.mult)
                    tt(T2, a_r, b_i, ALU.mult)
                    tt(o_i, a_i, b_r, ALU.mult)
                    tt(o_i, o_i, T2, ALU.add)
                    tt(o_r, T1, T0, ALU.subtract)

                kb.dma("sp", S(LR), dr["s5_lam_re"][l, d].rearrange("(gp g2) p -> (g2 p) gp", g2=2), PR, w=[PR], allow_slow_non_contiguous=True)
                kb.dma("sp", S(LI), dr["s5_lam_im"][l, d].rearrange("(gp g2) p -> (g2 p) gp", g2=2), PR, w=[PR], allow_slow_non_contiguous=True)
                ldv = dr["s5_log_dt"][l, d].rearrange("(gp g2) -> g2 gp", g2=2)
                for g2 in range(2):
                    kb.dma("sp", p[64 * g2:64 * g2 + 64, LDT, :], ldv[g2:g2 + 1, :].partition_broadcast(64), PR, w=[PR], allow_slow_non_contiguous=True)
                for ri in range(2):
                    kb.dma("sp", S(H0R + ri), dr["s5h0"][l, d, ri].rearrange("(gp g2) p -> (g2 p) gp", g2=2), PR, w=[PR], allow_slow_non_contiguous=True)
                kb.op(P_, lambda e, S=S: e.memset(S(ONE), 1.0), w=[PR])
                kb.op(P_, lambda e, S=S: e.memset(S(ZERO), 0.0), w=[PR])
                act(DTt, LDT, AF.Exp)
                tt(LRDT, LR, DTt, ALU.mult)
                tt(TH, LI, DTt, ALU.mult)
                act(MAG, LRDT, AF.Exp)
                ts(TURN, TH, 1.0 / (2 * math.pi), ALU.mult)
                sincos((tms[0][:], tms[1][:], tms[2][:], [tms[3]]), S(CS), S(SN), S(TURN), [PR], [PR])
                tt(AR, MAG, CS, ALU.mult)
                tt(AI, MAG, SN, ALU.mult)
                cmul(A2R, A2I, AR, AI, AR, AI)
                cmul(A3R, A3I, A2R, A2I, AR, AI)
                cmul(A4R, A4I, A2R, A2I, A2R, A2I)
                act(IM2, LRDT, AF.Exp, -2.0)
                tt(Q1R, AR, IM2, ALU.mult)
                tt(Q1I, AI, IM2, ALU.mult)
                ts(Q1I, Q1I, -1.0, ALU.mult)
                cmul(Q2R, Q2I, Q1R, Q1I, Q1R, Q1I)
                cmul(Q3R, Q3I, Q2R, Q2I, Q1R, Q1I)
                tt(T0, LR, LR, ALU.mult)
                tt(T1, LI, LI, ALU.mult)
                tt(DEN, T0, T1, ALU.add)
                kb.op(V, lambda e, S=S: e.reciprocal(out=S(DEN), in_=S(DEN)), r=[PR], w=[PR])
                ts(AM1, AR, -1.0, ALU.add)
                tt(T0, AM1, LR, ALU.mult)
                tt(T1, AI, LI, ALU.mult)
                tt(T0, T0, T1, ALU.add)
                tt(FR, T0, DEN, ALU.mult)
                tt(T0, AI, LR, ALU.mult)
                tt(T1, AM1, LI, ALU.mult)
                tt(T0, T0, T1, ALU.subtract)
                tt(FI, T0, DEN, ALU.mult)
                act(R4, LRDT, AF.Exp, 4.0)
                ts(PHT, TURN, 4.0, ALU.mult)
                pos = [(ONE, ZERO), (AR, AI), (A2R, A2I), (A3R, A3I)]
                neg = [(ONE, ZERO), (Q1R, Q1I), (Q2R, Q2I), (Q3R, Q3I)]
                Lp, Rp = (neg, pos) if d == 0 else (pos, neg)
                for s in range(4):
                    for ri in range(2):
                        kb.op(V, lambda e, s=s, ri=ri, S=S, Lp=Lp: e.tensor_copy(out=pwL[d][ri][:, :, s], in_=S(Lp[s][ri])), r=[PR], w=[PR])
                        kb.op(V, lambda e, s=s, ri=ri, S=S, Rp=Rp: e.tensor_copy(out=pwR[d][ri][:, :, s], in_=S(Rp[s][ri])), r=[PR], w=[PR])
                for ri, nm in enumerate(("s5_b_re", "s5_b_im")):
                    kb.dma("sp", braw[ri][:], dr[nm][l, d].rearrange("(gp g2) p j -> (g2 p) gp j", g2=2), PR, w=[PR])
                frb = S(FR).unsqueeze(2).to_broadcast([128, 16, 16])
                fib = S(FI).unsqueeze(2).to_broadcast([128, 16, 16])
                bb = bc[d]
                kb.op(V, lambda e, bb=bb, frb=frb: e.tensor_tensor(out=bb[0][:], in0=braw[0][:], in1=frb, op=ALU.mult), r=[PR], w=[PR])
                kb.op(V, lambda e, fib=fib: e.tensor_tensor(out=btmp[:], in0=braw[1][:], in1=fib, op=ALU.mult), r=[PR], w=[PR])
                kb.op(V, lambda e, bb=bb: e.tensor_tensor(out=bb[0][:], in0=bb[0][:], in1=btmp[:], op=ALU.subtract), r=[PR], w=[PR])
                kb.op(V, lambda e, bb=bb, frb=frb: e.tensor_tensor(out=bb[1][:], in0=braw[1][:], in1=frb, op=ALU.mult), r=[PR], w=[PR])
                kb.op(V, lambda e, fib=fib: e.tensor_tensor(out=btmp[:], in0=braw[0][:], in1=fib, op=ALU.mult), r=[PR], w=[PR])
                kb.op(V, lambda e, bb=bb: e.tensor_tensor(out=bb[1][:], in0=bb[1][:], in1=btmp[:], op=ALU.add), r=[PR], w=[PR])
                for ri, nm in enumerate(("s5_c_re", "s5_c_im")):
                    kb.dma("sp", craw[:].rearrange("i (g p) -> i g p", p=64), dr[nm][l, d].rearrange("g i p -> i g p"), PR, w=[PR])
                    ps = PS[6 + ri]
                    kb.op("pe", lambda e, ps=ps: [e.transpose(out=ps[:, gp * 16:(gp + 1) * 16], in_=craw[:, gp * 128:(gp + 1) * 128], identity=ident_f[0:16, 0:16])
                                                  for gp in range(16)][-1], r=[PR, CONST], w=[ps.trk()])
                    kb.op(V, lambda e, ri=ri, ps=ps, d=d: e.tensor_copy(out=cc[d][ri][:], in_=ps[:, 0:256].rearrange("p (g i) -> p g i", i=16)),
                          r=[ps.trk(), PR], w=[PR])
                if d == 0:
                    kb.op(V, lambda e, S=S: e.tensor_copy(out=S(PXR), in_=S(A3R)), r=[PR], w=[PR])
                    kb.op(V, lambda e, S=S: e.tensor_copy(out=S(PXI), in_=S(A3I)), r=[PR], w=[PR])
            kb.barrier()
        dump("s5pr%d" % l, pr[0][:].rearrange("p a b -> p (a b)"), [128, NPR * 16], PR)
        if S5STOP[0] == "prep":
            kb.barrier()
            return
        QX = [(AR, AI), (A4R, A4I)]

        with ExitStack() as st3:
            wcu = [sb(st3, "wcu", [128, 8, 32], BF16) for _ in range(2)]
            u4b = [sb(st3, "u4b", [128, 512], BF16) for _ in range(2)]
            u4f = sb(st3, "u4f", [128, 512])
            zsrc = [[sb(st3, "zs", [128, 32]) for _ in range(2)] for _ in range(2)]
            Lt = [sb(st3, "Lt", [128, 4, 32]) for _ in range(2)]
            Rt = [sb(st3, "Rt", [128, 4, 32]) for _ in range(2)]
            L3 = [sb(st3, "L3", [128, 128]) for _ in range(2)]
            Qm = [[sb(st3, "Qm", [128, 128]) for _ in range(2)] for _ in range(2)]
            ctmp = sb(st3, "ctmp", [128, 4, 32])
            c128 = sb(st3, "c128", [128, 128])
            tzb = [sb(st3, "tzb", [128, 128], BF16) for _ in range(2)]
            pmb = [[sb(st3, "pmb", [128, 128], BF16) for _ in range(2)] for _ in range(2)]
            cosT = sb(st3, "cosT", [128, 512])
            sinT = sb(st3, "sinT", [128, 512])
            xr = [sb(st3, "xr", [128, 512]) for _ in range(2)]
            gg = [sb(st3, "gg", [128, 512]) for _ in range(2)]
            tA = sb(st3, "tA", [128, 512])
            tB = sb(st3, "tB", [128, 512])
            dec = sb(st3, "dec", [128, 512])
            hs = [sb(st3, "hs", [128, 513]) for _ in range(2)]
            ini = sb(st3, "ini", [128, 4])
            TT_ = Trk()
            ZS = Trk()
            for a_ in range(2):
                for b_ in range(2):
                    kb.op(P_, lambda e, a_=a_, b_=b_: e.memset(zsrc[a_][b_][:], 0.0), w=[ZS])
            hfv = hfin[:].rearrange("p (q d r g) -> p q d r g", d=2, r=2, g=16)
            wcv = WIN[l]
            try:
                def prep_gen(gp, d):
                    ub = u4b[gp % 2]
                    if d == 0:
                        wc_ = wcu[gp % 2]
                        yield
                        kb.dma("pool", wc_[:], wcv[:, :, C_CU + gp * 32:C_CU + gp * 32 + 32], wc_.trk(), w=[wc_.trk()])
                        pu = PS[0]
                        def fu(e, wc_=wc_):
                            ins = None
                            for kc in range(8):
                                for s in range(4):
                                    ins = e.matmul(pu[32 * s:32 * s + 32, :], lhsT=wc_[:, kc, :], rhs=hT[:, kc, s::4], start=(kc == 0), stop=(kc == 7),
                                                   tile_position=(0, 32 * s))
                            return ins
                        yield
                        kb.op("pe", fu, r=[wc_.trk()] + [HT[c][n] for c in range(8) for n in range(NT)], w=[pu.trk()])
                        yield
                        kb.op(A_, lambda e, ub=ub: e.activation(out=ub[:], in_=pu[:], func=AF.Copy), r=[pu.trk()], w=[ub.trk()])
                        yield
                    p = pr[d]
                    for a_, srcs in ((0, bc[d]), (1, cc[d])):
                        for ri in range(2):
                            for g2 in range(2):
                                kb.op(V, lambda e, a_=a_, ri=ri, g2=g2, srcs=srcs: e.tensor_copy(out=zsrc[a_][ri][64 * g2:64 * g2 + 64, 16 * g2:16 * g2 + 16],
                                                                                                 in_=srcs[ri][64 * g2:64 * g2 + 64, gp, :]), r=[PR], w=[ZS])
                    for (dst, src, pw, neg_im) in ((Lt, zsrc[0], pwL[d], False), (Rt, zsrc[1], pwR[d], True)):
                        s_re = src[0][:, :].unsqueeze(1).to_broadcast([128, 4, 32])
                        s_im = src[1][:, :].unsqueeze(1).to_broadcast([128, 4, 32])
                        w_re = pw[0][:, gp, :].unsqueeze(2).to_broadcast([128, 4, 32])
                        w_im = pw[1][:, gp, :].unsqueeze(2).to_broadcast([128, 4, 32])
                        yield
                        kb.op(V, lambda e, dst=dst, s_re=s_re, w_re=w_re: e.tensor_tensor(out=dst[0][:], in0=s_re, in1=w_re, op=ALU.mult), r=[PR, ZS], w=[dst[0].trk()])
                        yield
                        kb.op(V, lambda e, s_im=s_im, w_im=w_im: e.tensor_tensor(out=ctmp[:], in0=s_im, in1=w_im, op=ALU.mult), r=[PR, ZS], w=[ctmp.trk()])
                        yield
                        kb.op(V, lambda e, dst=dst: e.tensor_tensor(out=dst[0][:], in0=dst[0][:], in1=ctmp[:], op=ALU.subtract), r=[ctmp.trk()], w=[dst[0].trk()])
                        yield
                        kb.op(V, lambda e, dst=dst, s_re=s_re, w_im=w_im: e.tensor_tensor(out=dst[1][:], in0=s_re, in1=w_im, op=ALU.mult), r=[PR, ZS], w=[dst[1].trk()])
                        yield
                        kb.op(V, lambda e, s_im=s_im, w_re=w_re: e.tensor_tensor(out=ctmp[:], in0=s_im, in1=w_re, op=ALU.mult), r=[PR, ZS], w=[ctmp.trk()])
                        if neg_im:
                            yield
                            kb.op(V, lambda e, dst=dst: e.scalar_tensor_tensor(out=dst[1][:], in0=dst[1][:], scalar=-1.0, in1=ctmp[:], op0=ALU.mult, op1=ALU.subtract),
                                  r=[ctmp.trk()], w=[dst[1].trk()])
                        else:
                            kb.op(V, lambda e, dst=dst: e.tensor_tensor(out=dst[1][:], in0=dst[1][:], in1=ctmp[:], op=ALU.add), r=[ctmp.trk()], w=[dst[1].trk()])
                    L2 = [Lt[i][:].rearrange("p s j -> p (s j)") for i in range(2)]
                    R2 = [Rt[i][:].rearrange("p s j -> p (s j)") for i in range(2)]
                    LT_ = [Lt[0].trk(), Lt[1].trk()]
                    RT_ = [Rt[0].trk(), Rt[1].trk()]
                    pt = PS[2]
                    yield
                    kb.op("pe", lambda e, L2=L2, R2=R2: [e.matmul(pt[:, 0:128], lhsT=L2[0], rhs=R2[0], start=True, stop=False),
                                                        e.matmul(pt[:, 0:128], lhsT=L2[1], rhs=R2[1], start=False, stop=True)][-1],
                          r=LT_ + RT_, w=[pt.trk()])
                    mk = mtz_f if d == 0 else mtz_b
                    yield
                    kb.op(V, lambda e, d=d, mk=mk: e.tensor_tensor(out=tzb[d][:], in0=pt[:, 0:128], in1=mk[:], op=ALU.mult), r=[pt.trk(), CONST], w=[tzb[d].trk()])
                    if d == 0:
                        c_r = p[:, PXR, gp:gp + 1]
                        c_i = p[:, PXI, gp:gp + 1]
                        yield
                        kb.op(V, lambda e, L2=L2, c_i=c_i: e.tensor_scalar(out=c128[:], in0=L2[1], scalar1=c_i, scalar2=None, op0=ALU.mult), r=[LT_[1], PR], w=[c128.trk()])
                        yield
                        kb.op(V, lambda e, L2=L2, c_r=c_r: e.scalar_tensor_tensor(out=L3[0][:], in0=L2[0], scalar=c_r, in1=c128[:], op0=ALU.mult, op1=ALU.subtract),
                              r=[LT_[0], c128.trk(), PR], w=[L3[0].trk()])
                        yield
                        kb.op(V, lambda e, L2=L2, c_i=c_i: e.tensor_scalar(out=c128[:], in0=L2[0], scalar1=c_i, scalar2=None, op0=ALU.mult), r=[LT_[0], PR], w=[c128.trk()])
                        yield
                        kb.op(V, lambda e, L2=L2, c_r=c_r: e.scalar_tensor_tensor(out=L3[1][:], in0=L2[1], scalar=c_r, in1=c128[:], op0=ALU.mult, op1=ALU.add),
                              r=[LT_[1], c128.trk(), PR], w=[L3[1].trk()])
                        Lsrc = [L3[0][:], L3[1][:]]
                        ltr = [L3[0].trk(), L3[1].trk()]
                    else:
                        Lsrc = L2
                        ltr = LT_
                    pp = PS[3]
                    yield
                    kb.op("pe", lambda e, Lsrc=Lsrc: [e.transpose(out=pp[:, 0:128], in_=Lsrc[0], identity=ident_f[:]),
                                                      e.transpose(out=pp[:, 128:256], in_=Lsrc[1], identity=ident_f[:])][-1], r=ltr + [CONST], w=[pp.trk()])
                    yield
                    kb.op(A_, lambda e, d=d: e.activation(out=pmb[d][0][:], in_=pp[:, 0:128], func=AF.Copy), r=[pp.trk()], w=[pmb[d][0].trk()])
                    yield
                    kb.op(A_, lambda e, d=d: e.activation(out=pmb[d][1][:], in_=pp[:, 128:256], func=AF.Copy), r=[pp.trk()], w=[pmb[d][1].trk()])
                    c_r = p[:, QX[d][0], gp:gp + 1]
                    c_i = p[:, QX[d][1], gp:gp + 1]
                    yield
                    kb.op(V, lambda e, R2=R2, c_i=c_i: e.tensor_scalar(out=c128[:], in0=R2[1], scalar1=c_i, scalar2=None, op0=ALU.mult), r=[RT_[1], PR], w=[c128.trk()])
                    yield
                    kb.op(V, lambda e, d=d, R2=R2, c_r=c_r: e.scalar_tensor_tensor(out=Qm[d][0][:], in0=R2[0], scalar=c_r, in1=c128[:], op0=ALU.mult, op1=ALU.add),
                          r=[RT_[0], c128.trk(), PR], w=[Qm[d][0].trk()])
                    yield
                    kb.op(V, lambda e, R2=R2, c_i=c_i: e.tensor_scalar(out=c128[:], in0=R2[0], scalar1=c_i, scalar2=None, op0=ALU.mult), r=[RT_[0], PR], w=[c128.trk()])
                    yield
                    kb.op(V, lambda e, d=d, R2=R2, c_r=c_r: e.scalar_tensor_tensor(out=Qm[d][1][:], in0=R2[1], scalar=c_r, in1=c128[:], op0=ALU.mult, op1=ALU.subtract),
                          r=[RT_[1], c128.trk(), PR], w=[Qm[d][1].trk()])
                    yield

                def recur_gen(gp, d):
                    ub = u4b[gp % 2]
                    py = PS[1]
                    p = pr[d]
                    px = [PS[4], PS[5]]
                    for ri in range(2):
                        kb.op("pe", lambda e, ri=ri, d=d, ub=ub: e.matmul(px[ri][:], lhsT=pmb[d][ri][:], rhs=ub[:], start=True, stop=True),
                              r=[pmb[d][ri].trk(), ub.trk()], w=[px[ri].trk()])
                    tur = gg[1]
                    yield
                    kb.op(V, lambda e, p=p: e.tensor_scalar(out=tur[:], in0=ciota[:], scalar1=p[:, PHT, gp:gp + 1], scalar2=None, op0=ALU.mult),
                          r=[C5, PR], w=[tur.trk()])
                    yield
                    sincos((xr[0][:].bitcast(I32), xr[1][:], gg[0][:], [xr[0].trk(), xr[1].trk(), gg[0].trk()]), cosT[:], sinT[:], tur[:], [tur.trk()], [TT_])
                    def pv(ap, d=d):
                        return ap if d == 0 else ap[:, ::-1]
                    X_r, X_i = pv(px[0][:]), pv(px[1][:])
                    yield
                    kb.op(V, lambda e, X_i=X_i: e.tensor_tensor(out=tA[:], in0=X_i, in1=sinT[:], op=ALU.mult), r=[px[1].trk(), TT_], w=[tA.trk()])
                    yield
                    kb.op(V, lambda e, X_r=X_r: e.tensor_tensor(out=xr[0][:], in0=X_r, in1=cosT[:], op=ALU.mult), r=[px[0].trk(), TT_], w=[xr[0].trk()])
                    yield
                    kb.op(V, lambda e: e.tensor_tensor(out=xr[0][:], in0=xr[0][:], in1=tA[:], op=ALU.add), r=[tA.trk()], w=[xr[0].trk()])
                    yield
                    kb.op(V, lambda e, X_r=X_r: e.tensor_tensor(out=tB[:], in0=X_r, in1=sinT[:], op=ALU.mult), r=[px[0].trk(), TT_], w=[tB.trk()])
                    yield
                    kb.op(V, lambda e, X_i=X_i: e.tensor_tensor(out=xr[1][:], in0=X_i, in1=cosT[:], op=ALU.mult), r=[px[1].trk(), TT_], w=[xr[1].trk()])
                    yield
                    kb.op(V, lambda e: e.tensor_tensor(out=xr[1][:], in0=xr[1][:], in1=tB[:], op=ALU.subtract), r=[tB.trk()], w=[xr[1].trk()])
                    yield
                    kb.op(A_, lambda e, p=p: e.activation(out=dec[:], in_=ciota[:], func=AF.Identity, scale=0.0, bias=p[:, R4, gp:gp + 1]), r=[C5, PR], w=[dec.trk()])
                    yield
                    kb.op(V, lambda e: e.tensor_scalar(out=dec[:, ::64], in0=dec[:, ::64], scalar1=flg[:, 0:1], scalar2=None, op0=ALU.mult), r=[CONST], w=[dec.trk()])
                    h0r = p[:, H0R, gp:gp + 1]
                    h0i = p[:, H0I, gp:gp + 1]
                    c1 = cosT[:, 1:2]
                    s1 = sinT[:, 1:2]
                    IN_ = ini.trk()
                    yield
                    kb.op(V, lambda e, h0i=h0i, s1=s1: e.tensor_tensor(out=ini[:, 2:3], in0=h0i, in1=s1, op=ALU.mult), r=[PR, TT_], w=[IN_])
                    yield
                    kb.op(V, lambda e, h0r=h0r, c1=c1: e.scalar_tensor_tensor(out=ini[:, 0:1], in0=h0r, scalar=c1, in1=ini[:, 2:3], op0=ALU.mult, op1=ALU.subtract), r=[PR, TT_], w=[IN_])
                    yield
                    kb.op(V, lambda e, h0r=h0r, s1=s1: e.tensor_tensor(out=ini[:, 3:4], in0=h0r, in1=s1, op=ALU.mult), r=[PR, TT_], w=[IN_])
                    yield
                    kb.op(V, lambda e, h0i=h0i, c1=c1: e.scalar_tensor_tensor(out=ini[:, 1:2], in0=h0i, scalar=c1, in1=ini[:, 3:4], op0=ALU.mult, op1=ALU.add), r=[PR, TT_], w=[IN_])
                    for ri in range(2):
                        kb.op(V, lambda e, ri=ri: e.tensor_tensor_scan(out=gg[ri][:], data0=dec[:], data1=xr[ri][:], initial=ini[:, ri:ri + 1], op0=ALU.mult, op1=ALU.add),
                              r=[dec.trk(), xr[ri].trk(), IN_], w=[gg[ri].trk()])
                    if d == 0:
                        Hre, Him = hs[0][:, 1:513], hs[1][:, 1:513]
                    else:
                        Hre, Him = hs[0][:, 0:512][:, ::-1], hs[1][:, 0:512][:, ::-1]
                    HS = [hs[0].trk(), hs[1].trk()]
                    yield
                    kb.op(V, lambda e: e.tensor_tensor(out=tA[:], in0=gg[1][:], in1=sinT[:], op=ALU.mult), r=[gg[1].trk(), TT_], w=[tA.trk()])
                    yield
                    kb.op(V, lambda e: e.tensor_tensor(out=tB[:], in0=gg[0][:], in1=cosT[:], op=ALU.mult), r=[gg[0].trk(), TT_], w=[tB.trk()])
                    yield
                    kb.op(V, lambda e, Hre=Hre: e.tensor_tensor(out=Hre, in0=tB[:], in1=tA[:], op=ALU.subtract), r=[tA.trk(), tB.trk()], w=[HS[0]])
                    yield
                    kb.op(V, lambda e: e.tensor_tensor(out=tA[:], in0=gg[0][:], in1=sinT[:], op=ALU.mult), r=[gg[0].trk(), TT_], w=[tA.trk()])
                    yield
                    kb.op(V, lambda e: e.tensor_tensor(out=tB[:], in0=gg[1][:], in1=cosT[:], op=ALU.mult), r=[gg[1].trk(), TT_], w=[tB.trk()])
                    yield
                    kb.op(V, lambda e, Him=Him: e.tensor_tensor(out=Him, in0=tB[:], in1=tA[:], op=ALU.add), r=[tA.trk(), tB.trk()], w=[HS[1]])
                    for ri in range(2):
                        h_ = hs[ri]
                        if d == 0:
                            fin = h_[:, 64:513:64]
                            icol = h_[:, 0:1]
                        else:
                            fin = h_[:, 0:512:64]
                            icol = h_[:, 512:513]
                        yield
                        kb.op(V, lambda e, ri=ri, d=d, fin=fin: e.tensor_copy(out=hfv[:, :, d, ri, gp], in_=fin), r=[HS[ri]], w=[HF])
                        yield
                        kb.op(V, lambda e, h_=h_: e.tensor_scalar(out=h_[:, 64:512:64], in0=h_[:, 64:512:64], scalar1=flg[:, 0:1], scalar2=None, op0=ALU.mult),
                              r=[CONST, HF], w=[HS[ri]])
                        yield
                        kb.op(V, lambda e, icol=icol, ri=ri, p=p: e.tensor_copy(out=icol, in_=p[:, H0R + ri, gp:gp + 1]), r=[PR], w=[HS[ri]])
                    hv = [hs[0][:, 0:512], hs[1][:, 0:512]] if d == 0 else [hs[0][:, 1:513], hs[1][:, 1:513]]
                    def fy(e, d=d, hv=hv, ub=ub):
                        e.matmul(py[:], lhsT=tzb[d][:], rhs=ub[:], start=(d == 0), stop=False)
                        e.matmul(py[:], lhsT=Qm[d][0][:], rhs=hv[0], start=False, stop=False)
                        return e.matmul(py[:], lhsT=Qm[d][1][:], rhs=hv[1], start=False, stop=(d == 1))
                    yield
                    kb.op("pe", fy, r=[tzb[d].trk(), ub.trk(), Qm[d][0].trk(), Qm[d][1].trk()] + HS, w=[py.trk()])
                    if d == 1:
                        ytm = xr[0]
                        yield
                        kb.op(V, lambda e: e.scalar_tensor_tensor(out=ytm[:], in0=ub[:], scalar=par[:, l, 96 + gp:97 + gp], in1=py[:], op0=ALU.mult, op1=ALU.add),
                              r=[ub.trk(), PAR, py.trk()], w=[ytm.trk()])
                        if gp == 0:
                            dump("s5y%d" % l, ytm[:], [128, 512], ytm.trk())
                        yield
                        kb.op(A_, lambda e: e.activation(out=tA[:], in_=ytm[:], func=AF.Square), r=[ytm.trk()], w=[tA.trk()])
                        yield
                        kb.op(V, lambda e: e.tensor_scalar(out=tA[:], in0=tA[:], scalar1=0.044715, scalar2=1.0, op0=ALU.mult, op1=ALU.add), r=[], w=[tA.trk()])
                        yield
                        kb.op(V, lambda e: e.tensor_tensor(out=tA[:], in0=tA[:], in1=ytm[:], op=ALU.mult), r=[ytm.trk()], w=[tA.trk()])
                        yield
                        kb.op(A_, lambda e: e.activation(out=tA[:], in_=tA[:], func=AF.Tanh, scale=0.7978845608), r=[], w=[tA.trk()])
                        yield
                        kb.op(V, lambda e: e.tensor_scalar(out=tA[:], in0=tA[:], scalar1=1.0, scalar2=0.5, op0=ALU.add, op1=ALU.mult), r=[], w=[tA.trk()])
                        yield
                        kb.op(V, lambda e, gp=gp: e.tensor_tensor(out=ycT[:, gp, :], in0=tA[:], in1=ytm[:], op=ALU.mult), r=[ytm.trk(), tA.trk()], w=[YC[gp]])
                    yield

                NK = 32
                g0 = prep_gen(0, 0)
                for _ in g0:
                    pass
                for k in range(NK):
                    gens = [recur_gen(k // 2, k % 2)]
                    if k + 1 < NK:
                        gens.append(prep_gen((k + 1) // 2, (k + 1) % 2))
                    while gens:
                        for g_ in list(gens):
                            try:
                                next(g_)
                            except StopIteration:
                                gens.remove(g_)
            except _Stop:
                pass
            kb.barrier()
        if S5STOP[0] not in (None, "fin", "glu"):
            return
        with ExitStack() as st5:
            hfo = sb(st5, "hfo", [128, 512])
            ps = PS[0]
            kb.op("pe", lambda e: [e.transpose(out=ps[:, j * 128:(j + 1) * 128], in_=hfin[:, j * 128:(j + 1) * 128], identity=ident_f[:]) for j in range(4)][-1],
                  r=[HF, CONST], w=[ps.trk()])
            kb.op(V, lambda e: e.tensor_copy(out=hfo[:], in_=ps[:]), r=[ps.trk()], w=[hfo.trk()])
            ov = dr["news5"][l].rearrange("q d r g p -> (q d r g p)").rearrange("(j x y) -> x j y", x=128, y=128)
            kb.dma("sp", ov, hfo[:].rearrange("p (j y) -> p j y", y=128), hfo.trk(), r=[hfo.trk()])
            kb.barrier()
        if S5STOP[0] == "fin":
            return
        with ExitStack() as st6:
            wrep = [[sb(st6, "wrep", [128, 16, 128], BF16) for _ in range(2)] for _ in range(2)]
            sgt = [sb(st6, "sgt", [128, 512]) for _ in range(2)]
            wgv = dr["s5_w_glu"][l].rearrange("(gp ii) n -> ii gp n", ii=32)
            it = 0
            for j in range(4):
                for ab in range(2):
                    w_ = wrep[j % 2][ab]
                    c0 = (ab * 4 + j) * 128
                    for t in range(4):
                        kb.dma("pool", w_[32 * t:32 * t + 32, :, :], wgv[:, :, c0:c0 + 128], w_.trk(), w=[w_.trk()])
                bk = [[PS[ab * 4 + t] for t in range(4)] for ab in range(2)]
                for ab in range(2):
                    w_ = wrep[j % 2][ab]

                    def fg(e, w_=w_, ab=ab):
                        ins = None
                        for gp in range(16):
                            for t in range(4):
                                ins = e.matmul(bk[ab][t][:], lhsT=w_[32 * t:32 * t + 32, gp, :], rhs=ycT[32 * t:32 * t + 32, gp, :], start=(gp == 0), stop=(gp == 15),
                                               tile_position=(32 * t, 0))
                        return ins
                    kb.op("pe", fg, r=[w_.trk()] + YC, w=[bk[ab][t].trk() for t in range(4)])
                for t in range(4):
                    pa, pb = bk[0][t], bk[1][t]
                    it += 1
                    s_ = sgt[it % 2]
                    kb.op(A_, lambda e, s_=s_, pb=pb, j=j: e.activation(out=s_[:], in_=pb[:], func=AF.Sigmoid, bias=par[:, l, 92 + j:93 + j]), r=[pb.trk(), PAR], w=[s_.trk()])
                    kb.op(V, lambda e, s_=s_, pa=pa, j=j, t=t: e.scalar_tensor_tensor(out=ocT[:, j, t::4], in0=pa[:], scalar=par[:, l, 88 + j:89 + j], in1=s_[:], op0=ALU.add, op1=ALU.mult),
                          r=[pa.trk(), s_.trk(), PAR], w=[OC[j][n] for n in range(NT)])
            kb.barrier()
    dump("oc%d" % l, ocT[:].rearrange("p c t -> p (c t)"), [128, 4 * T], OC[3][3], BF16)


def gla_phase(nc, kb, sb, dr, PS, l, hT, HT, oT, OT, par, PAR, cst, CONST, flg, ones_f, m_le, m_ge, m_gt, m_lt, WIN, rstd_from_ps, proj_fm, load_cols, dump):
    V, P_, A_ = "dve", "pool", "act"
    with ExitStack() as st:
        rTa = [sb(st, "rTa", [17, T], BF16) for _ in range(2)]
        wa2a = [sb(st, "wa2a", [17, 256], BF16) for _ in range(2)]
        wbr = sb(st, "wbr", [128, 8, 32], BF16)
        load_cols(wbr, WIN[l], C_BR, 32)
        RT = [Trk(), Trk()]
        for d in range(2):
            kb.op(P_, lambda e, d=d: e.memset(rTa[d][:], 1.0), w=[RT[d]])
            kb.dma("pool", wa2a[d][0:16, :], dr["gla_wa2"][l, d], wa2a[d].trk(), w=[wa2a[d].trk()])
            kb.dma("pool", wa2a[d][16:17, :], dr["gla_ba"][l, d].rearrange("(o n) -> o n", o=1), wa2a[d].trk(), w=[wa2a[d].trk()])
            for n in range(NT):
                ps = PS[n % 2]
                proj_fm(ps, wbr, d * 16, 16, n)
                kb.op(A_, lambda e, d=d, n=n, ps=ps: e.activation(out=rTa[d][0:16, n * 512:(n + 1) * 512], in_=ps[0:16, :], func=AF.Copy), r=[ps.trk()], w=[RT[d]])
        wq = sb(st, "gwq", [128, 8, 64], BF16)
        wk = sb(st, "gwk", [128, 8, 64], BF16)
        wv = sb(st, "gwv", [128, 8, 128], BF16)
        wg = sb(st, "gwg", [128, 8, 128], BF16)
        bqT = sb(st, "bqT", [64, T], BF16)
        bkT = sb(st, "bkT", [64, T], BF16)
        bkt = sb(st, "bkt", [128, NB, 64], BF16)
        bvt = sb(st, "bvt", [128, NB, 128], BF16)
        sgT = sb(st, "sgT", [128, T], BF16)
        obuf = sb(st, "obuf", [128, T])
        OBF = [obuf.trk(b) for b in range(NB)]
        Sd = [sb(st, "gS", [64, 128]) for _ in range(2)]
        Sbd = [sb(st, "gSb", [64, 128], BF16) for _ in range(2)]
        stg = [sb(st, "gstg", [64, 128]) for _ in range(2)]
        step = 0
        nst_ = [0]
        for h in range(4):
            load_cols(wq, WIN[l], C_BQ + h * 64, 64)
            load_cols(wk, WIN[l], C_BK + h * 64, 64)
            load_cols(wv, WIN[l], C_BV + h * 128, 128)
            load_cols(wg, WIN[l], C_BG + h * 128, 128)
            for n in range(NT):
                tsl = slice(n * 512, (n + 1) * 512)
                pa, pb = PS[0], PS[1]
                proj_fm(pa, wq, 0, 64, n)
                kb.op(A_, lambda e, tsl=tsl, pa=pa: e.activation(out=bqT[:, tsl], in_=pa[0:64, :], func=AF.Copy, scale=0.125), r=[pa.trk()], w=[bqT.trk()])
                proj_fm(pb, wk, 0, 64, n)
                kb.op(V, lambda e, tsl=tsl, pb=pb: e.tensor_copy(out=bkT[:, tsl], in_=pb[0:64, :]), r=[pb.trk()], w=[bkT.trk()])
                proj_fm(pa, wg, 0, 128, n)
                kb.op(A_, lambda e, tsl=tsl, pa=pa: e.activation(out=sgT[:, tsl], in_=pa[:], func=AF.Silu), r=[pa.trk()], w=[sgT.trk()])
            for g4 in range(4):
                pk, pv = PS[0], PS[1]

                def fk(e, g4=g4, pk=pk):
                    ins = None
                    for j in range(4):
                        blk = g4 * 4 + j
                        for kc in range(8):
                            ins = e.matmul(pk[:, j * 64:(j + 1) * 64], lhsT=hT[:, kc, blk * 128:(blk + 1) * 128], rhs=wk[:, kc, :], start=(kc == 0), stop=(kc == 7))
                    return ins
                kb.op("pe", fk, r=[wk.trk()] + [HT[c][g4] for c in range(8)], w=[pk.trk()])
                kb.op(V, lambda e, g4=g4, pk=pk: e.tensor_copy(out=bkt[:, g4 * 4:(g4 + 1) * 4, :], in_=pk[:, 0:256].rearrange("p (j f) -> p j f", f=64)), r=[pk.trk()], w=[bkt.trk()])

                def fv(e, g4=g4, pv=pv):
                    ins = None
                    for j in range(4):
                        blk = g4 * 4 + j
                        for kc in range(8):
                            ins = e.matmul(pv[:, j * 128:(j + 1) * 128], lhsT=hT[:, kc, blk * 128:(blk + 1) * 128], rhs=wv[:, kc, :], start=(kc == 0), stop=(kc == 7))
                    return ins
                kb.op("pe", fv, r=[wv.trk()] + [HT[c][g4] for c in range(8)], w=[pv.trk()])
                kb.op(A_, lambda e, g4=g4, pv=pv: e.activation(out=bvt[:, g4 * 4:(g4 + 1) * 4, :], in_=pv[:].rearrange("p (j f) -> p j f", f=128), func=AF.Copy), r=[pv.trk()], w=[bvt.trk()])
            for d in range(2):
                kb.dma("sp", Sd[d][:], dr["gla0"][l, d, h], Sd[d].trk(), w=[Sd[d].trk()])
                kb.op(A_, lambda e, d=d: e.activation(out=Sbd[d][:], in_=Sd[d][:], func=AF.Copy), r=[Sd[d].trk()], w=[Sbd[d].trk()])
            with ExitStack() as lt:
                def tn(nm, shape, dt=F32, n=4):
                    return [sb(lt, nm, shape, dt) for _ in range(n)]
                e1, spt, eD = tn("ge1", [128, 64], F32, 2), tn("gsp", [128, 64], BF16, 6), tn("ged", [128, 64], BF16, 6)
                eGT, enGT = tn("geg", [64, 128], F32, 6), tn("gen", [64, 128], F32, 6)
                qt, kt = tn("gqt", [64, 128], BF16), tn("gkt", [64, 128], BF16)
                kp, am = tn("gkp", [128, 64], BF16), tn("gam", [128, 128], BF16)

                def la_gen(d, i_):
                    trib = m_gt[0] if d == 0 else m_gt[1]
                    strict = m_gt[2] if d == 0 else m_gt[3]
                    b = i_ if d == 0 else NB - 1 - i_
                    i3 = d * 3 + i_ % 3
                    bsl = slice(b * 128, (b + 1) * 128)
                    bA = PS[2 * d + i_ % 2]
                    pla, pd, pgt = SubView(bA, 0), SubView(bA, 64), SubView(bA, 128)
                    kb.op("pe", lambda e: e.matmul(pla[:, 0:64], lhsT=rTa[d][:, bsl], rhs=wa2a[d][:, h * 64:(h + 1) * 64], start=True, stop=True),
                          r=[RT[d], wa2a[d].trk()], w=[pla.trk()])
                    yield
                    kb.op(A_, lambda e: e.activation(out=e1[d][:], in_=pla[:, 0:64], func=AF.Exp, scale=-1.0), r=[pla.trk()], w=[e1[d].trk()])
                    yield
                    kb.op(A_, lambda e: e.activation(out=spt[i3][:], in_=e1[d][:], func=AF.Ln, bias=cst[:, 1:2]), r=[e1[d].trk(), CONST], w=[spt[i3].trk()])
                    yield
                    kb.op("pe", lambda e: [e.matmul(pgt[0:64, 0:128], lhsT=spt[i3][:], rhs=trib[:], start=True, stop=True),
                                          e.matmul(pd[:, 0:64], lhsT=strict[:], rhs=spt[i3][:], start=True, stop=True)][-1], r=[spt[i3].trk(), CONST], w=[bA.trk()])
                    yield
                    kb.op(A_, lambda e: e.activation(out=eGT[i3][:], in_=pgt[0:64, 0:128], func=AF.Exp, scale=-1.0 / 16.0), r=[bA.trk()], w=[eGT[i3].trk()])
                    yield
                    kb.op(A_, lambda e: e.activation(out=enGT[i3][:], in_=pgt[0:64, 0:128], func=AF.Exp, scale=1.0 / 16.0), r=[bA.trk()], w=[enGT[i3].trk()])
                    yield
                    kb.op(A_, lambda e: e.activation(out=eD[i3][:], in_=pd[:, 0:64], func=AF.Exp, scale=-1.0 / 16.0), r=[bA.trk()], w=[eD[i3].trk()])
                    yield

                def prod_gen(d, i_):
                    tri = m_le if d == 0 else m_ge
                    b = i_ if d == 0 else NB - 1 - i_
                    i3 = d * 3 + i_ % 3
                    i2 = d * 2 + i_ % 2
                    bsl = slice(b * 128, (b + 1) * 128)
                    pat = SubView(PS[4 + d], 0)
                    kb.op(V, lambda e: e.tensor_tensor(out=qt[i2][:], in0=bqT[:, bsl], in1=eGT[i3][:], op=ALU.mult), r=[bqT.trk(), eGT[i3].trk()], w=[qt[i2].trk()])
                    yield
                    kb.op(P_, lambda e: e.tensor_tensor(out=kt[i2][:], in0=bkT[:, bsl], in1=enGT[i3][:], op=ALU.mult), r=[bkT.trk(), enGT[i3].trk()], w=[kt[i2].trk()])
                    yield
                    kb.op(P_, lambda e: e.tensor_tensor(out=kp[i2][:], in0=bkt[:, b, :], in1=eD[i3][:], op=ALU.mult), r=[bkt.trk(), eD[i3].trk()], w=[kp[i2].trk()])
                    yield
                    kb.op("pe", lambda e: e.matmul(pat[:, 0:128], lhsT=kt[i2][:], rhs=qt[i2][:], start=True, stop=True), r=[kt[i2].trk(), qt[i2].trk()], w=[pat.trk()])
                    yield
                    kb.op(V, lambda e: e.tensor_tensor(out=am[i2][:], in0=pat[:, 0:128], in1=tri[:], op=ALU.mult), r=[pat.trk(), CONST], w=[am[i2].trk()])
                    yield

                def fin_gen(d, i_):
                    edge = 127 if d == 0 else 0
                    S, Sb = Sd[d], Sbd[d]
                    b = i_ if d == 0 else NB - 1 - i_
                    first_touch = i_ < NB // 2
                    i2 = d * 2 + i_ % 2
                    i3 = d * 3 + i_ % 3
                    bsl = slice(b * 128, (b + 1) * 128)
                    bC = PS[6 + d]
                    pS, po = SubView(bC, 0), SubView(bC, 128)
                    kb.op("pe", lambda e: [e.matmul(po[:, 0:128], lhsT=bvt[:, b, :], rhs=am[i2][:], start=True, stop=False),
                                          e.matmul(po[:, 0:128], lhsT=Sb[:], rhs=qt[i2][:], start=False, stop=True),
                                          e.matmul(pS[0:64, 0:128], lhsT=kp[i2][:], rhs=bvt[:, b, :], start=True, stop=True)][-1],
                          r=[bvt.trk(), am[i2].trk(), Sb.trk(), qt[i2].trk(), kp[i2].trk()], w=[bC.trk()])
                    yield
                    kb.op(V, lambda e: e.scalar_tensor_tensor(out=S[:], in0=S[:], scalar=eGT[i3][:, edge:edge + 1], in1=pS[0:64, 0:128], op0=ALU.mult, op1=ALU.add),
                          r=[S.trk(), eGT[i3].trk(), bC.trk(), Sb.trk()], w=[S.trk()])
                    yield
                    seq_end = (b % 2 == 1) if d == 0 else (b % 2 == 0)
                    if seq_end:
                        sg_ = stg[d]
                        nst_[0] += 1
                        kb.op(P_, lambda e: e.tensor_copy(out=sg_[:], in_=S[:]), r=[S.trk()], w=[sg_.trk()])
                        yield
                        kb.dma("sp", dr["newgla"][l, b // 2, d, h], sg_[:], sg_.trk(), r=[sg_.trk()])
                        yield
                        kb.op(V, lambda e: e.tensor_scalar(out=S[:], in0=S[:], scalar1=flg[0:64, 0:1], scalar2=None, op0=ALU.mult), r=[S.trk(), CONST], w=[S.trk()])
                        yield
                    kb.op(A_, lambda e: e.activation(out=Sb[:], in_=S[:], func=AF.Copy), r=[S.trk()], w=[Sb.trk()])
                    yield
                    if first_touch:
                        kb.op(V, lambda e: e.tensor_copy(out=obuf[:, bsl], in_=po[:, 0:128]), r=[bC.trk()], w=[OBF[b]])
                    else:
                        kb.op(V, lambda e: e.tensor_tensor(out=obuf[:, bsl], in0=obuf[:, bsl], in1=po[:, 0:128], op=ALU.add), r=[bC.trk(), OBF[b]], w=[OBF[b]])
                    yield

                for i_ in range(NB + 2):
                    gens = []
                    if i_ >= 2:
                        gens += [fin_gen(0, i_ - 2), fin_gen(1, i_ - 2)]
                    if 1 <= i_ <= NB:
                        gens += [prod_gen(0, i_ - 1), prod_gen(1, i_ - 1)]
                    if i_ < NB:
                        gens += [la_gen(0, i_), la_gen(1, i_)]
                    while gens:
                        for g_ in list(gens):
                            try:
                                next(g_)
                            except StopIteration:
                                gens.remove(g_)
                kb.barrier()
            with ExitStack() as nt_:
                nsq = sb(nt_, "gnsq", [128, 512])
                nln = sb(nt_, "gnln", [128, 512])
                nrs = sb(nt_, "gnrs", [128, 512])
                ntm = sb(nt_, "gntm", [128, 512])
                def gnA(n):
                    tsl = slice(n * 512, (n + 1) * 512)
                    ob_tr = [OBF[n * 4 + j] for j in range(4)]
                    ps = PS[n % 2]
                    kb.op(A_, lambda e: e.activation(out=nsq[:], in_=obuf[:, tsl], func=AF.Square), r=ob_tr, w=[nsq.trk()])
                    yield
                    kb.op("pe", lambda e: e.matmul(ps[:], lhsT=ones_f[:], rhs=nsq[:], start=True, stop=True), r=[nsq.trk(), CONST], w=[ps.trk()])
                    yield

                def gnB(n):
                    tsl = slice(n * 512, (n + 1) * 512)
                    ob_tr = [OBF[n * 4 + j] for j in range(4)]
                    ps = PS[n % 2]
                    kb.op(A_, lambda e: e.activation(out=nln[:], in_=ps[:], func=AF.Ln, scale=1.0 / 128.0, bias=cst[:, 0:1]), r=[ps.trk(), CONST], w=[nln.trk()])
                    yield
                    kb.op(A_, lambda e: e.activation(out=nrs[:], in_=nln[:], func=AF.Exp, scale=-0.5), r=[nln.trk()], w=[nrs.trk()])
                    yield
                    kb.op(V, lambda e: e.scalar_tensor_tensor(out=ntm[:], in0=obuf[:, tsl], scalar=par[:, l, 84:85], in1=nrs[:], op0=ALU.mult, op1=ALU.mult),
                          r=ob_tr + [nrs.trk(), PAR], w=[ntm.trk()])
                    yield
                    kb.op(P_, lambda e: e.tensor_tensor(out=oT[:, h, tsl], in0=ntm[:], in1=sgT[:, tsl], op=ALU.mult), r=[ntm.trk(), sgT.trk()], w=[OT[h][n]])
                    yield

                for i in range(NT + 1):
                    gens = []
                    if i >= 1:
                        gens.append(gnB(i - 1))
                    if i < NT:
                        gens.append(gnA(i))
                    while gens:
                        for g_ in list(gens):
                            try:
                                next(g_)
                            except StopIteration:
                                gens.remove(g_)
                kb.barrier()
        kb.barrier()
    dump("ob%d" % l, oT[:].rearrange("p c t -> p (c t)"), [128, 4 * T], OT[3][3], BF16)


def attn_phase(nc, kb, sb, dr, PS, l, hT, HT, oT, OT, par, PAR, cst, CONST, maskb, ident_f, ones_f, ones_b, blk64, rrot, make_rope,
               WIN, rstd_from_ps, proj_fm, load_cols, dump):
    V, P_, A_ = "dve", "pool", "act"
    with ExitStack() as st:
        ropec, ropes, RT = make_rope(st)
        wq = sb(st, "awq", [128, 8, 128], BF16)
        wk = sb(st, "awk", [128, 8, 128], BF16)
        wv = sb(st, "awv", [128, 8, 128], BF16)
        qP = [sb(st, "aqP", [128, T], BF16) for _ in range(2)]
        kP = [sb(st, "akP", [128, 256 + T], BF16) for _ in range(2)]
        ZQ = Trk()
        kb.op(P_, lambda e: e.memset(qP[0][64:128, :], 0.0), w=[ZQ])
        kb.op(P_, lambda e: e.memset(qP[1][0:64, :], 0.0), w=[ZQ])
        kb.op(P_, lambda e: e.memset(kP[0][64:128, :], 0.0), w=[ZQ])
        kb.op(P_, lambda e: e.memset(kP[1][0:64, :], 0.0), w=[ZQ])
        kb.dma("pool", qP[0][64:72, :], dr["mq"], ZQ, w=[ZQ])
        kb.dma("pool", qP[1][0:8, :], dr["mq"], ZQ, w=[ZQ])
        kb.dma("pool", kP[0][64:72, :], dr["mk"], ZQ, w=[ZQ])
        kb.dma("pool", kP[1][0:8, :], dr["mk"], ZQ, w=[ZQ])
        vt = sb(st, "avt", [128, 18, 128], BF16)
        ckst = sb(st, "ackst", [128, 2, 128])
        sq = sb(st, "asq", [128, 512])
        sqb = sb(st, "asqb", [128, 512], BF16)
        lnv = sb(st, "aln", [128, 512])
        lnv2 = sb(st, "aln2", [128, 512])
        rstd = sb(st, "ars", [128, 512])
        qn = sb(st, "aqn", [128, 512])
        t1 = lnv
        pT = [sb(st, "apT", [128, 512], BF16) for _ in range(4)]

        acc = sb(st, "aacc", [128, 512])
        rs = sq
        tmpo = qn
        kst = [sb(st, "akst", [128, 512]) for _ in range(1)]
        vst = kst
        QT = [Trk() for n in range(NT)]
        KT = [Trk() for n in range(NT + 1)]
        VT = [vt.trk(g) for g in range(5)]
        nst = 0
        npt = 0
        for h in range(4):
            load_cols(wq, WIN[l], C_AQ + h * 128, 128)
            load_cols(wk, WIN[l], C_AK + h * 128, 128)
            load_cols(wv, WIN[l], C_AV + h * 128, 128)
            kb.dma("sp", ckst[:], dr["ctxk"][l, :, h * 128:(h + 1) * 128].rearrange("(b p) f -> p b f", p=128), ckst.trk(), w=[ckst.trk()])
            pc = PS[2]
            kb.op("pe", lambda e: [e.transpose(out=pc[:, b * 128:(b + 1) * 128], in_=ckst[:, b, :], identity=ident_f[:]) for b in range(2)][-1],
                  r=[ckst.trk(), CONST], w=[pc.trk()])
            kb.op(V, lambda e: e.tensor_copy(out=kP[0][0:64, 0:256], in_=pc[0:64, 0:256]), r=[pc.trk(), ZQ], w=[KT[0]])
            kb.op(V, lambda e: e.tensor_copy(out=kP[1][64:128, 0:256], in_=pc[64:128, 0:256]), r=[pc.trk(), ZQ], w=[KT[0]])
            kb.dma("pool", vt[:, 0:2, :], dr["ctxv"][l, :, h * 128:(h + 1) * 128].rearrange("(b p) f -> p b f", p=128), VT[0], w=[VT[0]])
            items = [(is_k, n) for is_k in (False, True) for n in range(NT)]
            psb = [PS[0], PS[4]]
            sqbb = [sqb, pT[0]]
            lnvb = [lnv, lnv2]
            rstb = [rstd, acc]

            def st1(i):
                is_k, n = items[i]
                w_ = wk if is_k else wq
                ps, pss = psb[i % 2], PS[1]
                sb_, ln_, rs_ = sqbb[i % 2], lnvb[i % 2], rstb[i % 2]
                proj_fm(ps, w_, 0, 128, n)
                yield
                kb.op(A_, lambda e: e.activation(out=sb_[:], in_=ps[:], func=AF.Square), r=[ps.trk()], w=[sb_.trk()])
                yield
                kb.op("pe", lambda e: e.matmul(pss[:], lhsT=blk64[:], rhs=sb_[:], start=True, stop=True), r=[sb_.trk(), CONST], w=[pss.trk()])
                yield
                kb.op(A_, lambda e: e.activation(out=ln_[:], in_=pss[:], func=AF.Ln, scale=1.0 / 64.0, bias=cst[:, 0:1]), r=[pss.trk(), CONST], w=[ln_.trk()])
                yield
                kb.op(A_, lambda e: e.activation(out=rs_[:], in_=ln_[:], func=AF.Exp, scale=-0.5), r=[ln_.trk()], w=[rs_.trk()])
                yield

            def st2(i):
                is_k, n = items[i]
                gcol = 81 if is_k else 80
                tsl = slice(n * 512, (n + 1) * 512)
                ps, prq = psb[i % 2], PS[3]
                rs_, t1_ = rstb[i % 2], lnvb[i % 2]
                kb.op(V, lambda e: e.scalar_tensor_tensor(out=qn[:], in0=ps[:], scalar=par[:, l, gcol:gcol + 1], in1=rs_[:], op0=ALU.mult, op1=ALU.mult),
                      r=[ps.trk(), rs_.trk(), PAR], w=[qn.trk()])
                yield
                if is_k:
                    ko = kst[0]
                    pk = PS[2]
                    kb.op("pe", lambda e: [e.transpose(out=pk[:, j * 128:(j + 1) * 128], in_=qn[:, j * 128:(j + 1) * 128], identity=ident_f[:]) for j in range(4)][-1],
                          r=[qn.trk(), CONST], w=[pk.trk()])
                    yield
                    kb.op(A_, lambda e: e.activation(out=ko[:], in_=pk[:], func=AF.Copy), r=[pk.trk()], w=[ko.trk()])
                    yield
                    kb.dma("sp", dr["newk"][l, n * 512:(n + 1) * 512, h * 128:(h + 1) * 128].rearrange("(j p) f -> p j f", p=128),
                           ko[:].rearrange("p (j f) -> p j f", f=128), ko.trk(), r=[ko.trk()])
                    yield
                kb.op("pe", lambda e: e.matmul(prq[:], lhsT=rrot[:], rhs=qn[:], start=True, stop=True), r=[qn.trk(), CONST], w=[prq.trk()])
                yield
                kb.op(P_, lambda e: e.tensor_tensor(out=t1_[:], in0=qn[:], in1=ropec[:, tsl], op=ALU.mult), r=[qn.trk(), RT], w=[t1_.trk()])
                yield
                kb.op(V, lambda e: e.tensor_tensor(out=sq[:], in0=prq[:], in1=ropes[:, tsl], op=ALU.mult), r=[prq.trk(), RT], w=[sq.trk()])
                yield
                if is_k:
                    kb.op(V, lambda e: e.tensor_tensor(out=kP[0][0:64, 256 + n * 512:256 + (n + 1) * 512], in0=t1_[0:64, :], in1=sq[0:64, :], op=ALU.add), r=[t1_.trk(), sq.trk(), ZQ], w=[KT[n + 1]])
                    yield
                    kb.op(V, lambda e: e.tensor_tensor(out=kP[1][64:128, 256 + n * 512:256 + (n + 1) * 512], in0=t1_[64:128, :], in1=sq[64:128, :], op=ALU.add), r=[t1_.trk(), sq.trk(), ZQ], w=[KT[n + 1]])
                else:
                    kb.op(V, lambda e: e.tensor_tensor(out=qP[0][0:64, tsl], in0=t1_[0:64, :], in1=sq[0:64, :], op=ALU.add), r=[t1_.trk(), sq.trk(), ZQ], w=[QT[n]])
                    yield
                    kb.op(V, lambda e: e.tensor_tensor(out=qP[1][64:128, tsl], in0=t1_[64:128, :], in1=sq[64:128, :], op=ALU.add), r=[t1_.trk(), sq.trk(), ZQ], w=[QT[n]])
                yield

            for i in range(len(items) + 1):
                gens = []
                if i >= 1:
                    gens.append(st2(i - 1))
                if i < len(items):
                    gens.append(st1(i))
                while gens:
                    for g_ in list(gens):
                        try:
                            next(g_)
                        except StopIteration:
                            gens.remove(g_)
            for g4 in range(4):
                pv = PS[4]

                def fv(e, g4=g4, pv=pv):
                    ins = None
                    for j in range(4):
                        blk = g4 * 4 + j
                        for kc in range(8):
                            ins = e.matmul(pv[:, j * 128:(j + 1) * 128], lhsT=hT[:, kc, blk * 128:(blk + 1) * 128], rhs=wv[:, kc, :], start=(kc == 0), stop=(kc == 7))
                    return ins
                kb.op("pe", fv, r=[wv.trk()] + [HT[c][g4] for c in range(8)], w=[pv.trk()])
                kb.op(A_, lambda e, g4=g4, pv=pv: e.activation(out=vt[:, 2 + g4 * 4:2 + (g4 + 1) * 4, :], in_=pv[:].rearrange("p (j f) -> p j f", f=128), func=AF.Copy),
                      r=[pv.trk()], w=[VT[g4 + 1]])
                vo = vst[0]
                kb.op(V, lambda e, vo=vo, pv=pv: e.tensor_copy(out=vo[:], in_=pv[:]), r=[pv.trk()], w=[vo.trk()])
                kb.dma("sp", dr["newv"][l, g4 * 512:(g4 + 1) * 512, h * 128:(h + 1) * 128].rearrange("(j p) f -> p j f", p=128),
                       vo[:].rearrange("p (j f) -> p j f", f=128), vo.trk(), r=[vo.trk()])
            units = [(qt, m, u) for qt in range(NT) for m in range(2) for u in range(9)]

            def emit_S(i):
                qt, m, u = units[i]
                qsl = slice(qt * 512, (qt + 1) * 512)
                banks = [PS[(2 * i) % 4], PS[(2 * i + 1) % 4]]
                ktrs = list({KT[0] if kc < 2 else KT[1 + (kc - 2) // 4] for kc in (2 * u, 2 * u + 1)})

                def f(e):
                    ins = None
                    for j in range(2):
                        kc = 2 * u + j
                        ins = e.matmul(banks[j][:], lhsT=kP[m][:, kc * 128:(kc + 1) * 128], rhs=qP[m][:, qsl], start=True, stop=True)
                    return ins
                kb.op("pe", f, r=ktrs + [QT[qt], ZQ], w=[banks[0].trk(), banks[1].trk()])
                for j in range(2):
                    p_ = pT[(2 * i + j) % 4]
                    kb.op(A_, lambda e, p_=p_, j=j: e.activation(out=p_[:], in_=banks[j][:], func=AF.Exp, scale=0.125), r=[banks[j].trk()], w=[p_.trk()])

            def emit_PV(i):
                qt, m, u = units[i]
                qsl = slice(qt * 512, (qt + 1) * 512)
                g = (qt * 2 + m) % 2
                po, psm = PS[4 + 2 * g], PS[5 + 2 * g]
                ps_ = [pT[(2 * i) % 4], pT[(2 * i + 1) % 4]]
                vtrs = list({VT[0] if kc < 2 else VT[1 + (kc - 2) // 4] for kc in (2 * u, 2 * u + 1)})

                def f(e):
                    ins = None
                    for j in range(2):
                        kc = 2 * u + j
                        e.matmul(po[:], lhsT=vt[:, kc, :], rhs=ps_[j][:], start=(kc == 0), stop=(kc == 17))
                        ins = e.matmul(psm[:], lhsT=ones_b[:], rhs=ps_[j][:], start=(kc == 0), stop=(kc == 17))
                    return ins
                kb.op("pe", f, r=[ps_[0].trk(), ps_[1].trk(), CONST] + vtrs, w=[po.trk(), psm.trk()])
                if u != 8:
                    return
                kb.op(V, lambda e: e.reciprocal(out=rs[:], in_=psm[:]), r=[psm.trk()], w=[rs.trk()])
                if m == 0:
                    kb.op(V, lambda e: e.tensor_tensor(out=acc[:], in0=po[:], in1=rs[:], op=ALU.mult), r=[po.trk(), rs.trk()], w=[acc.trk()])
                    return
                kb.op(V, lambda e: e.tensor_tensor(out=tmpo[:], in0=po[:], in1=rs[:], op=ALU.mult), r=[po.trk(), rs.trk()], w=[tmpo.trk()])
                kb.op(V, lambda e: e.scalar_tensor_tensor(out=acc[:], in0=tmpo[:], scalar=par[:, l, 83:84], in1=acc[:], op0=ALU.mult, op1=ALU.add),
                      r=[tmpo.trk(), PAR], w=[acc.trk()])
                pn = PS[4 + 2 * g]
                kb.op(A_, lambda e: e.activation(out=sq[:], in_=acc[:], func=AF.Square), r=[acc.trk()], w=[sq.trk()])
                kb.op("pe", lambda e: e.matmul(pn[:], lhsT=ones_f[:], rhs=sq[:], start=True, stop=True), r=[sq.trk(), CONST], w=[pn.trk()])
                rstd_from_ps((pn[:], pn.trk()), (lnv[:], lnv.trk(), 128), (rstd[:], rstd.trk()), 1.0 / 128.0)
                kb.op(V, lambda e: e.scalar_tensor_tensor(out=oT[:, h, qsl], in0=acc[:], scalar=par[:, l, 82:83], in1=rstd[:], op0=ALU.mult, op1=ALU.mult),
                      r=[acc.trk(), rstd.trk(), PAR], w=[OT[h][qt]])

            LA = 1
            for i in range(len(units) + LA):
                if i < len(units):
                    emit_S(i)
                if i - LA >= 0:
                    emit_PV(i - LA)
        kb.barrier()
    dump("oa%d" % l, oT[:].rearrange("p c t -> p (c t)"), [128, 4 * T], OT[3][3], BF16)


_CACHE = {}


def make_in_maps(inputs):
    f32 = np.float32
    xs = np.ascontiguousarray(inputs["x_sample"], dtype=f32)
    xp = np.ascontiguousarray(inputs["x_prompt"], dtype=f32)
    maps = []
    wts = {n: np.ascontiguousarray(inputs[n], dtype=f32) for n, _ in W_SPECS}
    for core in range(8):
        m = dict(wts)
        if core < 4:
            b = core
            m["xin"] = xs[b]
            m["cond"] = np.ascontiguousarray(inputs["c"][b], dtype=f32)
            m["ctxk"] = np.ascontiguousarray(inputs["cache_diff_k"][b], dtype=f32).reshape(2, 256, 512)
            m["ctxv"] = np.ascontiguousarray(inputs["cache_diff_v"][b], dtype=f32).reshape(2, 256, 512)
            m["gla0"] = np.ascontiguousarray(inputs["state_gla"][b], dtype=f32)
            m["s5h0"] = np.ascontiguousarray(inputs["state_s5"][b], dtype=f32)
            m["flags"] = np.ones((128, 2), f32)
            m["mq"] = np.zeros((8, 2048), f32)
            m["mk"] = np.zeros((8, 2304), f32)
        else:
            q0 = (core - 4) * 8
            m["xin"] = xp[q0:q0 + 8].reshape(2048, 1024)
            m["cond"] = np.ascontiguousarray(inputs["c_ctx"], dtype=f32)
            m["ctxk"] = np.zeros((2, 256, 512), f32)
            m["ctxv"] = np.zeros((2, 256, 512), f32)
            m["gla0"] = np.zeros((2, 2, 4, 64, 128), f32)
            m["s5h0"] = np.zeros((2, 2, 2, 32, 64), f32)
            m["flags"] = np.zeros((128, 2), f32)
            mq = np.zeros((8, 2048), f32)
            mk = np.full((8, 2304), NEG * 8.0, f32)
            for j in range(8):
                mq[j, j * 256:(j + 1) * 256] = 1.0
                mk[j, 256 + j * 256:256 + (j + 1) * 256] = 0.0
            m["mq"] = mq
            m["mk"] = mk
        maps.append(m)
    return maps


def kernel(**inputs):
    if "nc" not in _CACHE:
        waited = build_program(want_waited=True)
        _CACHE["nc"] = build_program(needed=waited)[0]
    nc = _CACHE["nc"]
    maps = make_in_maps(inputs)
    res = run_bass_kernel_spmd(nc, maps, core_ids=list(range(8))).results
    f32 = np.float32
    y_sample = np.stack([res[b]["y"] for b in range(4)]).astype(f32)
    y_prompt = np.concatenate([res[c]["y"].reshape(8, 256, 1024) for c in range(4, 8)]).astype(f32)
    nk = np.concatenate([res[c]["newk"].reshape(2, 8, 256, 4, 2, 64).transpose(1, 0, 2, 3, 4, 5) for c in range(4, 8)]).astype(f32)
    nv = np.concatenate([res[c]["newv"].reshape(2, 8, 256, 4, 128).transpose(1, 0, 2, 3, 4) for c in range(4, 8)]).astype(f32)
    ng = np.concatenate([res[c]["newgla"].transpose(1, 0, 2, 3, 4, 5) for c in range(4, 8)]).astype(f32)
    n5 = np.concatenate([res[c]["news5"].transpose(1, 0, 2, 3, 4, 5) for c in range(4, 8)]).astype(f32)
    return (y_prompt, y_sample, nk, nv, ng, n5)
```
